# Optimizing a Trainium2 kernel written in Bass

```python
import jax, jax.numpy as jnp
from jax import lax
import numpy as np

D_MODEL = 1024
BATCH = 8
SEQ = 2048
DEPTH = 1
DEC_BATCH = 128
DEC_SEQ = 1
PAST_LEN = 2048
PAGE_SIZE = 128

HEAD_DIM = 64
N_HEADS_A = D_MODEL // 128
N_KV_A = N_HEADS_A // 4
GROUP_A = N_HEADS_A // N_KV_A
CMP_BLK = 32
CMP_STRIDE = 16
SLC_BLK = 64
N_SEL = 16
WINDOW = 512
Q_BLOCK = 128
C_CONV = D_MODEL // 2
CONV_W = 31
N_MEM = 256
N_HEADS_M = 4
D_FF = ((8 * D_MODEL // 3 + 127) // 128) * 128
FFN_CONV_W = 3
ROPE_THETA = 500000.0
ROPE_DIM = HEAD_DIM // 4
EPS = 1e-6
BIG = 1e9
N_ROW_COMP = 4

Q_A_COLS = N_HEADS_A * HEAD_DIM
KV_A_COLS = 3 * 2 * N_KV_A * HEAD_DIM
GATE_A_COLS = 3 * N_HEADS_A
GLU_COLS = 2 * C_CONV
Q_M_COLS = N_HEADS_M * HEAD_DIM
MERGE_COLS = 3 * D_MODEL
IN_COLS = Q_A_COLS + KV_A_COLS + GATE_A_COLS + GLU_COLS + Q_M_COLS + MERGE_COLS

kernel_name = 'nsa_conformer_memory_hybrid_step'


def _rmsnorm(x, g):
    xf = x.astype(jnp.float32)
    y = xf * lax.rsqrt(jnp.mean(xf * xf, axis=-1, keepdims=True) + EPS)
    return (y * g.astype(jnp.float32)).astype(x.dtype)


def _layernorm(x, g, b):
    xf = x.astype(jnp.float32)
    mu = jnp.mean(xf, axis=-1, keepdims=True)
    var = jnp.mean(jnp.square(xf - mu), axis=-1, keepdims=True)
    y = (xf - mu) * lax.rsqrt(var + EPS)
    return (y * g.astype(jnp.float32) + b.astype(jnp.float32)).astype(x.dtype)


def _masked_softmax(s, mask):
    s = jnp.where(mask, s.astype(jnp.float32), -1e30)
    return jnp.where(mask, jax.nn.softmax(s, axis=-1), 0.0)


def _rope(x, pos):
    inv = ROPE_THETA ** (-jnp.arange(0, ROPE_DIM, 2, dtype=jnp.float32) / ROPE_DIM)
    ang = pos.astype(jnp.float32)[..., None] * inv
    cos, sin = jnp.cos(ang), jnp.sin(ang)
    xr = x[..., :ROPE_DIM].astype(jnp.float32)
    x1, x2 = xr[..., :ROPE_DIM // 2], xr[..., ROPE_DIM // 2:]
    rot = jnp.concatenate([x1 * cos - x2 * sin, x1 * sin + x2 * cos], axis=-1).astype(x.dtype)
    return jnp.concatenate([rot, x[..., ROPE_DIM:]], axis=-1)


def _causal_dwconv(x, buf, w, b):
    xp = jnp.concatenate([buf.astype(x.dtype), x], axis=1)
    out = lax.conv_general_dilated(xp, w[:, None, :].astype(x.dtype), window_strides=(1,), padding='VALID',
                                   dimension_numbers=('NWC', 'WIO', 'NWC'), feature_group_count=x.shape[-1])
    return out + b.astype(x.dtype), xp[:, -(w.shape[0] - 1):]


def _split_in(z):
    o1 = Q_A_COLS
    o2 = o1 + KV_A_COLS
    o3 = o2 + GATE_A_COLS
    o4 = o3 + GLU_COLS
    o5 = o4 + Q_M_COLS
    return z[..., :o1], z[..., o1:o2], z[..., o2:o3], z[..., o3:o4], z[..., o4:o5], z[..., o5:]


def _compress(rows, pe, w):
    n_cmp = (rows.shape[1] - CMP_BLK) // CMP_STRIDE + 1
    idx = jnp.arange(n_cmp)[:, None] * CMP_STRIDE + jnp.arange(CMP_BLK)[None, :]
    blk = rows[:, idx] + pe[:, None, :]
    return jnp.einsum('bnlgd,lde->bnge', blk, w)


def _slc_attend(q, qpos, idx, ok, k_blocks, v_blocks):
    B, KV, Tq, n = idx.shape
    gather = jax.vmap(jax.vmap(lambda kb, ix: kb[ix]))
    kg = gather(k_blocks, idx).reshape(B, KV, Tq, n * SLC_BLK, HEAD_DIM)
    vg = gather(v_blocks, idx).reshape(B, KV, Tq, n * SLC_BLK, HEAD_DIM)
    kpos = idx[..., None] * SLC_BLK + jnp.arange(SLC_BLK)
    mask = (ok[..., None] & (kpos <= qpos[:, None, None])).reshape(B, KV, Tq, n * SLC_BLK)
    s = jnp.einsum('btgjd,bgtmd->bgjtm', q, kg) * (HEAD_DIM ** -0.5)
    pr = _masked_softmax(s, mask[:, :, None])
    return jnp.einsum('bgjtm,bgtmd->btgjd', pr.astype(vg.dtype), vg)


def _nsa_cmp_slc(q, qpos, rows, p):
    B, L = rows.shape[0], rows.shape[1]
    Tq = q.shape[1]
    k_c = _compress(rows[:, :, 0], p['cmp_pe'][0], p['w_cmp'][0])
    v_c = _compress(rows[:, :, 1], p['cmp_pe'][1], p['w_cmp'][1])
    cmp_start = jnp.arange(k_c.shape[1]) * CMP_STRIDE
    cmp_end = cmp_start + (CMP_BLK - 1)
    k_c = _rope(_rmsnorm(k_c, p['k_norm'][0]), cmp_end[:, None])
    s = jnp.einsum('btgjd,bngd->bgjtn', q, k_c) * (HEAD_DIM ** -0.5)
    pc = _masked_softmax(s, cmp_end[None, :] <= qpos[:, None])
    o_cmp = jnp.einsum('bgjtn,bngd->btgjd', pc.astype(v_c.dtype), v_c)
    n_slc = -(-L // SLC_BLK)
    slc = jnp.arange(n_slc)
    slc_start = slc * SLC_BLK
    overlap = ((cmp_start[:, None] < slc_start[None, :] + SLC_BLK)
               & (cmp_start[:, None] + CMP_BLK > slc_start[None, :])).astype(jnp.float32)
    imp = jnp.einsum('bgjtn,ns->bgts', pc, overlap)
    q_blk = (qpos // SLC_BLK)[:, None]
    valid = slc[None, :] <= q_blk
    forced = (slc[None, :] == 0) | (slc[None, :] == q_blk) | (slc[None, :] == q_blk - 1)
    score = jnp.where(forced, BIG, jnp.where(valid, imp, -BIG))
    top_s, idx = lax.top_k(score, min(N_SEL, n_slc))
    ok = top_s > -0.5 * BIG
    pad = n_slc * SLC_BLK - L

    def to_blocks(r):
        r = jnp.pad(r, ((0, 0), (0, pad), (0, 0), (0, 0)))
        return r.reshape(B, n_slc, SLC_BLK, N_KV_A, HEAD_DIM).transpose(0, 3, 1, 2, 4)

    k_blocks = to_blocks(rows[:, :, 2])
    v_blocks = to_blocks(rows[:, :, 3])
    if Tq % Q_BLOCK == 0:
        nqb = Tq // Q_BLOCK
        n = idx.shape[-1]
        qb = q.reshape(B, nqb, Q_BLOCK, N_KV_A, GROUP_A, HEAD_DIM).swapaxes(0, 1)
        pb = qpos.reshape(nqb, Q_BLOCK)
        ib = idx.reshape(B, N_KV_A, nqb, Q_BLOCK, n).transpose(2, 0, 1, 3, 4)
        okb = ok.reshape(B, N_KV_A, nqb, Q_BLOCK, n).transpose(2, 0, 1, 3, 4)
        ob = lax.map(lambda a: _slc_attend(a[0], a[1], a[2], a[3], k_blocks, v_blocks), (qb, pb, ib, okb))
        o_slc = ob.swapaxes(0, 1).reshape(B, Tq, N_KV_A, GROUP_A, HEAD_DIM)
    else:
        o_slc = _slc_attend(q, qpos, idx, ok, k_blocks, v_blocks)
    return o_cmp, o_slc


def _window_attend(q, qpos, k, v, kpos):
    s = jnp.einsum('btgjd,bkgd->bgjtk', q, k) * (HEAD_DIM ** -0.5)
    diff = qpos[:, None] - kpos[None, :]
    mask = (diff >= 0) & (diff <= WINDOW) & (kpos[None, :] >= 0)
    pr = _masked_softmax(s, mask)
    return jnp.einsum('bgjtk,bkgd->btgjd', pr.astype(v.dtype), v)


def _window_banded(q, k, v):
    B, T = q.shape[0], q.shape[1]
    nb = T // Q_BLOCK
    span = WINDOW + Q_BLOCK
    idx = jnp.arange(nb)[:, None] * Q_BLOCK + jnp.arange(span)[None, :]
    padw = ((0, 0), (WINDOW, 0), (0, 0), (0, 0))
    kb = jnp.pad(k, padw)[:, idx]
    vb = jnp.pad(v, padw)[:, idx]
    qb = q.reshape(B, nb, Q_BLOCK, N_KV_A, GROUP_A, HEAD_DIM)
    qpos = jnp.arange(T).reshape(nb, Q_BLOCK)
    o = jax.vmap(_window_attend, in_axes=(1, 0, 1, 1, 0), out_axes=1)(qb, qpos, kb, vb, idx - WINDOW)
    return o.reshape(B, T, N_KV_A, GROUP_A, HEAD_DIM)


def _mem_kv(mem, p):
    B = mem.shape[0]
    m = (_rmsnorm(mem, p['norm_mem']) @ p['w_mem_kv']).reshape(B, N_MEM, 2, N_HEADS_M, HEAD_DIM)
    return jnp.stack([_rmsnorm(m[:, :, 0], p['mk_norm']), m[:, :, 1]], axis=2)


def _layer(p, x, pos0, rows_past, win_past, conv_buf, ffn_buf, mem_kv):
    B, T, _ = x.shape
    qpos = pos0 + jnp.arange(T)
    h = _rmsnorm(x, p['norm_attn'])
    q_a, kv_a, g_a, u_c, q_m, g_m = _split_in(h @ p['w_in'])
    q = _rope(_rmsnorm(q_a.reshape(B, T, N_HEADS_A, HEAD_DIM), p['q_norm']), qpos[:, None])
    q = q.reshape(B, T, N_KV_A, GROUP_A, HEAD_DIM)
    kv = kv_a.reshape(B, T, 3, 2, N_KV_A, HEAD_DIM)
    k_slc = _rope(_rmsnorm(kv[:, :, 1, 0], p['k_norm'][1]), qpos[:, None])
    k_win = _rope(_rmsnorm(kv[:, :, 2, 0], p['k_norm'][2]), qpos[:, None])
    new_rows = jnp.stack([kv[:, :, 0, 0], kv[:, :, 0, 1], k_slc, kv[:, :, 1, 1]], axis=2)
    new_win = jnp.stack([k_win, kv[:, :, 2, 1]], axis=2)
    rows = new_rows if rows_past is None else jnp.concatenate([rows_past, new_rows], axis=1)
    o_cmp, o_slc = _nsa_cmp_slc(q, qpos, rows, p)
    if win_past is None:
        o_win = _window_banded(q, new_win[:, :, 0], new_win[:, :, 1])
        win_all = new_win
    else:
        win_all = jnp.concatenate([win_past, new_win], axis=1)
        kpos = pos0 - win_past.shape[1] + jnp.arange(win_all.shape[1])
        o_win = _window_attend(q, qpos, win_all[:, :, 0], win_all[:, :, 1], kpos)
    win_state = win_all[:, -min(WINDOW, win_all.shape[1]):]
    ga = jax.nn.sigmoid(g_a.reshape(B, T, 3, N_KV_A, GROUP_A, 1))
    o_nsa = ga[:, :, 0] * o_cmp + ga[:, :, 1] * o_slc + ga[:, :, 2] * o_win
    y_a = o_nsa.reshape(B, T, N_HEADS_A * HEAD_DIM) @ p['w_o_nsa']
    a, b = jnp.split(u_c, 2, axis=-1)
    glu = a * jax.nn.sigmoid(b)
    cbuf = jnp.zeros((B, CONV_W - 1, C_CONV), glu.dtype) if conv_buf is None else conv_buf
    c, conv_state = _causal_dwconv(glu, cbuf, p['conv_w'], p['conv_b'])
    y_b = jax.nn.silu(_layernorm(c, p['conv_ln_g'], p['conv_ln_b'])) @ p['w_o_conv']
    qm = _rmsnorm(q_m.reshape(B, T, N_HEADS_M, HEAD_DIM), p['mq_norm'])
    sm = jnp.einsum('bthd,bnhd->bhtn', qm, mem_kv[:, :, 0]) * (HEAD_DIM ** -0.5)
    pm = jax.nn.softmax(sm.astype(jnp.float32), axis=-1).astype(mem_kv.dtype)
    y_m = jnp.einsum('bhtn,bnhd->bthd', pm, mem_kv[:, :, 1]).reshape(B, T, N_HEADS_M * HEAD_DIM) @ p['w_o_mem']
    gm = jax.nn.sigmoid(g_m.reshape(B, T, 3, D_MODEL))
    x = x + (gm[:, :, 0] * y_a + gm[:, :, 1] * y_b + gm[:, :, 2] * y_m) @ p['w_out']
    up = _rmsnorm(x, p['norm_ffn']) @ p['w_ffn_up']
    u, v = jnp.split(up, 2, axis=-1)
    fbuf = jnp.zeros((B, FFN_CONV_W - 1, D_FF), u.dtype) if ffn_buf is None else ffn_buf
    uc, ffn_state = _causal_dwconv(u, fbuf, p['ffn_conv_w'], p['ffn_conv_b'])
    y = x + (jax.nn.gelu(uc) * v) @ p['w_ffn_down']
    return y, new_rows, win_state, conv_state, ffn_state


def setup_inputs(seed: int = 0) -> dict:
    key = jax.random.key(seed)

    def nrm(i, shape, scale):
        return jax.random.normal(jax.random.fold_in(key, i), shape, jnp.float32) * scale

    def gain(i, shape):
        return 1.0 + nrm(i, shape, 0.01)

    n_pages = PAST_LEN // PAGE_SIZE
    n_used = DEC_BATCH * n_pages
    n_phys = n_used + (n_used + 3) // 4
    w_buf = min(WINDOW, PAST_LEN)
    page_table = jax.random.permutation(jax.random.fold_in(key, 999), n_phys)[:n_used]
    page_table = page_table.reshape(DEC_BATCH, n_pages).astype(jnp.int32)
    return {
        'x_prompt': nrm(0, (BATCH, SEQ, D_MODEL), 1.0),
        'x_sample': nrm(1, (DEC_BATCH, DEC_SEQ, D_MODEL), 1.0),
        'cache_nsa': nrm(2, (DEPTH, n_phys, PAGE_SIZE, N_ROW_COMP, N_KV_A, HEAD_DIM), 1.0),
        'cache_win': nrm(3, (DEPTH, DEC_BATCH, w_buf, 2, N_KV_A, HEAD_DIM), 1.0),
        'cache_conv': nrm(4, (DEPTH, DEC_BATCH, CONV_W - 1, C_CONV), 1.0),
        'cache_ffn': nrm(5, (DEPTH, DEC_BATCH, FFN_CONV_W - 1, D_FF), 1.0),
        'cache_mem': nrm(6, (DEPTH, DEC_BATCH, N_MEM, 2, N_HEADS_M, HEAD_DIM), 1.0),
        'page_table': page_table,
        'mem_prompt': nrm(7, (BATCH, N_MEM, D_MODEL), 1.0),
        'norm_attn': gain(8, (DEPTH, D_MODEL)),
        'w_in': nrm(9, (DEPTH, D_MODEL, IN_COLS), D_MODEL ** -0.5),
        'q_norm': gain(10, (DEPTH, HEAD_DIM)),
        'k_norm': gain(11, (DEPTH, 3, HEAD_DIM)),
        'cmp_pe': nrm(12, (DEPTH, 2, CMP_BLK, HEAD_DIM), 0.02),
        'w_cmp': nrm(13, (DEPTH, 2, CMP_BLK, HEAD_DIM, HEAD_DIM), (CMP_BLK * HEAD_DIM) ** -0.5),
        'w_o_nsa': nrm(14, (DEPTH, N_HEADS_A * HEAD_DIM, D_MODEL), (N_HEADS_A * HEAD_DIM) ** -0.5),
        'conv_w': nrm(15, (DEPTH, CONV_W, C_CONV), CONV_W ** -0.5),
        'conv_b': nrm(16, (DEPTH, C_CONV), 0.01),
        'conv_ln_g': gain(17, (DEPTH, C_CONV)),
        'conv_ln_b': nrm(18, (DEPTH, C_CONV), 0.01),
        'w_o_conv': nrm(19, (DEPTH, C_CONV, D_MODEL), C_CONV ** -0.5),
        'norm_mem': gain(20, (DEPTH, D_MODEL)),
        'w_mem_kv': nrm(21, (DEPTH, D_MODEL, 2 * N_HEADS_M * HEAD_DIM), D_MODEL ** -0.5),
        'mq_norm': gain(22, (DEPTH, HEAD_DIM)),
        'mk_norm': gain(23, (DEPTH, HEAD_DIM)),
        'w_o_mem': nrm(24, (DEPTH, N_HEADS_M * HEAD_DIM, D_MODEL), (N_HEADS_M * HEAD_DIM) ** -0.5),
        'w_out': nrm(25, (DEPTH, D_MODEL, D_MODEL), D_MODEL ** -0.5),
        'norm_ffn': gain(26, (DEPTH, D_MODEL)),
        'w_ffn_up': nrm(27, (DEPTH, D_MODEL, 2 * D_FF), D_MODEL ** -0.5),
        'ffn_conv_w': nrm(28, (DEPTH, FFN_CONV_W, D_FF), FFN_CONV_W ** -0.5),
        'ffn_conv_b': nrm(29, (DEPTH, D_FF), 0.01),
        'w_ffn_down': nrm(30, (DEPTH, D_FF, D_MODEL), D_FF ** -0.5),
    }


def reference(x_prompt, x_sample, cache_nsa, cache_win, cache_conv, cache_ffn, cache_mem, page_table, mem_prompt,
              norm_attn, w_in, q_norm, k_norm, cmp_pe, w_cmp, w_o_nsa, conv_w, conv_b, conv_ln_g, conv_ln_b, w_o_conv,
              norm_mem, w_mem_kv, mq_norm, mk_norm, w_o_mem, w_out, norm_ffn, w_ffn_up, ffn_conv_w, ffn_conv_b,
              w_ffn_down):
    params = dict(norm_attn=norm_attn, w_in=w_in, q_norm=q_norm, k_norm=k_norm, cmp_pe=cmp_pe, w_cmp=w_cmp,
                  w_o_nsa=w_o_nsa, conv_w=conv_w, conv_b=conv_b, conv_ln_g=conv_ln_g, conv_ln_b=conv_ln_b,
                  w_o_conv=w_o_conv, norm_mem=norm_mem, w_mem_kv=w_mem_kv, mq_norm=mq_norm, mk_norm=mk_norm,
                  w_o_mem=w_o_mem, w_out=w_out, norm_ffn=norm_ffn, w_ffn_up=w_ffn_up, ffn_conv_w=ffn_conv_w,
                  ffn_conv_b=ffn_conv_b, w_ffn_down=w_ffn_down)
    n_pages = page_table.shape[1]
    dec_b = x_sample.shape[0]
    y_p, y_s = x_prompt, x_sample
    rows_p, rows_s, win_p, win_s, conv_p, conv_s, ffn_p, ffn_s, mem_p = [], [], [], [], [], [], [], [], []
    for layer in range(DEPTH):
        p = {name: w[layer] for name, w in params.items()}
        mkv = _mem_kv(mem_prompt, p)
        y_p, r, wv, c, f = _layer(p, y_p, 0, None, None, None, None, mkv)
        rows_p.append(r); win_p.append(wv); conv_p.append(c); ffn_p.append(f); mem_p.append(mkv)
        past = cache_nsa[layer][page_table].reshape(dec_b, n_pages * PAGE_SIZE, N_ROW_COMP, N_KV_A, HEAD_DIM)
        y_s, r, wv, c, f = _layer(p, y_s, PAST_LEN, past, cache_win[layer], cache_conv[layer], cache_ffn[layer],
                                  cache_mem[layer])
        rows_s.append(r); win_s.append(wv); conv_s.append(c); ffn_s.append(f)
    return (y_p, y_s, jnp.stack(rows_p), jnp.stack(rows_s), jnp.stack(win_p), jnp.stack(win_s),
            jnp.stack(conv_p), jnp.stack(conv_s), jnp.stack(ffn_p), jnp.stack(ffn_s), jnp.stack(mem_p))
```

```python
import numpy as np
from contextlib import ExitStack
import concourse.bass as bass
import concourse.mybir as mybir
from concourse.bass_utils import run_bass_kernel_spmd

F32 = mybir.dt.float32; BF16 = mybir.dt.bfloat16; I32 = mybir.dt.int32
AF = mybir.ActivationFunctionType; ALU = mybir.AluOpType; AX = mybir.AxisListType

D = 1024; T = 2048; NS = 16; TT = T + NS; NTL = 17
HD = 64; NH = 8; NKV = 2
DFF = 2816; NFC = 22
CC = 512
EPS = 1e-6
NEG = -30000.0
IN_COLS = 5656
O_KV = 512; O_GA = 1280; O_UC = 1304; O_QM = 2328; O_GM = 2584
PHASES = 9
DBG = False
NPHYS = 2560
TILES = None
STOP = -1


class Buf:
    __slots__ = ('name', 'w', 'r', 'rd')

    def __init__(self, name):
        self.name = name; self.w = None; self.r = {}; self.rd = []


class _PEProxy:
    def __init__(self, eng):
        self.eng = eng; self.last = True

    def matmul(self, *a, **kw):
        self.last = bool(kw.get('stop', True))
        return self.eng.matmul(*a, **kw)

    def transpose(self, *a, **kw):
        self.last = True
        return self.eng.transpose(*a, **kw)


class KB:
    def __init__(self, nc, es, n_dsem=20):
        self.nc = nc; self.es = es
        self.E = dict(pe=nc.tensor, dve=nc.vector, act=nc.scalar, pool=nc.gpsimd, sp=nc.sync)
        self.sem = {e: es.enter_context(nc.semaphore('s_' + e)) for e in ['pe', 'dve', 'act', 'pool']}
        self.cnt = {e: 0 for e in self.sem}
        self.waited = {}
        self.dsem = {}; self.dval = {}; self.dnext = {}
        for q in ['sp', 'pool']:
            self.dsem[q] = [es.enter_context(nc.semaphore('d_%s%d' % (q, i))) for i in range(n_dsem)]
            self.dval[q] = [0] * n_dsem; self.dnext[q] = 0
        self.out_events = []

    def _wait(self, eng, key, sem, val):
        if self.waited.get((eng, key), 0) >= val:
            return
        self.E[eng].wait_ge(sem, val); self.waited[(eng, key)] = val

    def _dep(self, eng, ev, raw):
        kind, key, val = ev
        if kind == 'e':
            if key == eng and eng == 'pe':
                return
            self._wait(eng, key, self.sem[key], val)
        else:
            q, i = key
            self._wait(eng, key, self.dsem[q][i], val)

    def deps(self, eng, reads, writes):
        for b in reads:
            if b.w is not None:
                self._dep(eng, b.w, True)
        for b in writes:
            if b.w is not None:
                self._dep(eng, b.w, False)
            for e2, c in b.r.items():
                self._dep(eng, ('e', e2, c), False)
            for ev in b.rd:
                self._dep(eng, ev, False)

    def _record(self, ev, reads, writes):
        for b in reads:
            if ev[0] == 'e':
                b.r[ev[1]] = ev[2]
            else:
                b.rd.append(ev)
        for b in writes:
            b.w = ev; b.r = {}; b.rd = []

    def op(self, eng, fn, reads=(), writes=()):
        self.deps(eng, reads, writes)
        if eng == 'pe':
            px = _PEProxy(self.E[eng])
            ins = fn(px)
            if not px.last and not getattr(self, 'pe_force_inc', False):
                self._record(('e', eng, self.cnt[eng] + 1), reads, writes)
                return
        else:
            ins = fn(self.E[eng])
        self.cnt[eng] += 1
        ins.then_inc(self.sem[eng], 1)
        self._record(('e', eng, self.cnt[eng]), reads, writes)

    def dma(self, q, out, in_, reads=(), writes=(), is_output=False, fn=None, **kw):
        self.deps(q, reads, writes)
        i = self.dnext[q]; self.dnext[q] = (i + 1) % len(self.dsem[q])
        self._wait(q, (q, i), self.dsem[q][i], self.dval[q][i])
        self.dval[q][i] += 16
        if fn is None:
            ins = self.E[q].dma_start(out=out, in_=in_, **kw)
        else:
            ins = fn(self.E[q])
        ins.then_inc(self.dsem[q][i], 16)
        ev = ('d', (q, i), self.dval[q][i])
        self._record(ev, reads, writes)
        if is_output:
            self.out_events.append(ev)

    def barrier(self):
        for eng in ['pe', 'dve', 'act', 'pool', 'sp']:
            for e2 in self.sem:
                if e2 != eng and self.cnt[e2] > 0:
                    self._wait(eng, e2, self.sem[e2], self.cnt[e2])
            for q in self.dsem:
                for i, v in enumerate(self.dval[q]):
                    if v > 0:
                        self._wait(eng, (q, i), self.dsem[q][i], v)

    def finish(self, eng='sp'):
        last = {}
        for kind, key, val in self.out_events:
            last[key] = max(last.get(key, 0), val)
        for key, val in last.items():
            q, i = key
            self.E[eng].wait_ge(self.dsem[q][i], val)


def _consts():
    c = {}
    c['ident'] = np.eye(128, dtype=np.float32)
    inv = (np.float32(500000.0) ** (-np.arange(0, 16, 2, dtype=np.float32) / np.float32(16))).astype(np.float32)

    def cs(pos):
        ang = pos.astype(np.float32)[:, None] * inv[None, :]
        return np.concatenate([np.cos(ang), np.sin(ang)], axis=1).astype(np.float32)
    pos = np.concatenate([np.arange(T), np.full(NS, T)]).astype(np.float32)
    csp = np.zeros((NTL * 128, 16), np.float32); csp[:TT] = cs(pos)
    c['cs_tok'] = np.ascontiguousarray(csp.reshape(NTL, 128, 16).transpose(1, 0, 2))
    cc = np.zeros((128, 16), np.float32); cc[:127] = cs(np.arange(127) * 16 + 31.0)
    c['cs_cmp'] = cc
    p = np.arange(128)
    c['diag'] = np.where(p[:, None] <= p[None, :], 0.0, NEG).astype(np.float32)
    c['winlow'] = np.where(p[:, None] >= p[None, :], 0.0, NEG).astype(np.float32)
    n = np.arange(128)[None, :, None]; i = np.arange(16)[:, None, None]; q = np.arange(128)[None, None, :]
    cb = np.where(16 * n + 31 <= 128 * i + q, 0.0, NEG).astype(np.float32)
    c['cmpbias'] = np.ascontiguousarray(cb.transpose(1, 0, 2))
    t = np.arange(T); qb = t // 64; b = np.arange(32)
    forced = (b[None, :] == 0) | (b[None, :] == qb[:, None]) | (b[None, :] == qb[:, None] - 1)
    valid = b[None, :] <= qb[:, None]
    sa = np.where(forced, 1e9, np.where(valid, 0.0, -1e9)).astype(np.float32)
    c['seladd'] = np.ascontiguousarray(sa.reshape(16, 128, 32).transpose(1, 0, 2))
    sas = np.zeros((8, 32), np.float32); sas[:, 0] = 1e9; sas[:, 31] = 1e9
    c['seladd_s'] = sas
    c['E'] = (t[None, :] // 64 == b[:, None]).astype(np.float32)
    cs_ = np.arange(127) * 16
    ov = ((cs_[:, None] < b[None, :] * 64 + 64) & (cs_[:, None] + 32 > b[None, :] * 64)).astype(np.float32)
    ovp = np.zeros((128, 33), np.float32); ovp[:127, :32] = ov; ovp[:127, 32] = 1.0
    c['ovl'] = ovp
    G = (np.arange(8)[:, None] // 4 == np.arange(8)[None, :] // 4).astype(np.float32)
    c['G8'] = G
    dm = np.full((16, 16, 8), NEG, np.float32)
    for s in range(16):
        dm[s, s, :] = 0.0
    c['dmask'] = dm.reshape(16, 128)
    c['hm8'] = np.array([[1.0, 0.0]] * 4 + [[0.0, 1.0]] * 4, dtype=np.float32)
    return c


def build(phases=PHASES):
    nc = bass.Bass("TRN2", target_bir_lowering=False)

    def din(name, shape, dt=F32):
        return nc.dram_tensor(name, list(shape), dt, kind="ExternalInput").ap()

    def dout(name, shape):
        return nc.dram_tensor(name, list(shape), F32, kind="ExternalOutput").ap()

    xs = din("xs", [TT, D]); mem = din("mem", [256, D])
    cache_nsa = din("cache_nsa", [NPHYS * 128 if phases >= 9 else 128, 512])
    cache_win = din("cache_win", [NS, 512, 256])
    cache_conv = din("cache_conv", [NS, 30, CC])
    cache_ffn = din("cache_ffn", [NS, 2, DFF])
    cache_mem = din("cache_mem", [NS, 256, 512])
    ptab = din("ptab", [1, 256], I32)
    norm_attn = din("norm_attn", [1, D]); w_in = din("w_in", [D, IN_COLS])
    q_norm = din("q_norm", [1, HD]); k_norm = din("k_norm", [3, HD])
    cmp_pe = din("cmp_pe", [2, 32, HD]); w_cmp = din("w_cmp", [2, 32, HD, HD])
    w_o_nsa = din("w_o_nsa", [512, D])
    conv_w = din("conv_w", [31, CC]); conv_b = din("conv_b", [1, CC])
    conv_ln_g = din("conv_ln_g", [1, CC]); conv_ln_b = din("conv_ln_b", [1, CC])
    w_o_conv = din("w_o_conv", [CC, D])
    norm_mem = din("norm_mem", [1, D]); w_mem_kv = din("w_mem_kv", [D, 512])
    mq_norm = din("mq_norm", [1, HD]); mk_norm = din("mk_norm", [1, HD])
    w_o_mem = din("w_o_mem", [256, D]); w_out = din("w_out", [D, D])
    norm_ffn = din("norm_ffn", [1, D]); w_ffn_up = din("w_ffn_up", [D, 2 * DFF])
    ffn_conv_w = din("ffn_conv_w", [3, DFF]); ffn_conv_b = din("ffn_conv_b", [1, DFF])
    w_ffn_down = din("w_ffn_down", [DFF, D])
    C = {k: din("c_" + k, v.shape) for k, v in _consts().items()}

    o_y = dout("o_y", [TT, D]); o_rows = dout("o_rows", [TT, 512])
    o_winp = dout("o_winp", [512, 256]); o_wins = dout("o_wins", [NS, 512, 256])
    o_convp = dout("o_convp", [30, CC]); o_convs = dout("o_convs", [NS, 30, CC])
    o_ffnp = dout("o_ffnp", [2, DFF]); o_ffns = dout("o_ffns", [NS, 2, DFF])
    o_memkv = dout("o_memkv", [256, 512])
    x2s = nc.dram_tensor("x2s", [TT, D], F32, kind="Internal").ap()
    Bx2s = [Buf("x2s%d" % i) for i in range(NTL)]
    scr = nc.dram_tensor("scr", [65536], F32, kind="Internal").ap(); Bscr = Buf("scr")
    dbg = {}
    if DBG:
        for nm, k in [('csT', 4), ('omT', 2), ('onsaT', 4), ('mT', 8)]:
            dbg[nm] = dout("d_" + nm, [128, k, TT])

    w_in_v = w_in.rearrange("(kc p) c -> p kc c", p=128)

    def bc(ap1row):
        return ap1row.partition_broadcast(128).rearrange("p a d -> p (a d)")

    es = ExitStack()
    with es:
        kb = KB(nc, es)
        op = kb.op; dma = kb.dma

        def mk_sb(stack):
            def f(name, shape, dt=F32):
                return stack.enter_context(nc.sbuf_tensor(name, list(shape), dt))
            return f
        sb = mk_sb(es)

        def tsz(i):
            return 128 if i < 16 else NS
        TB = [(j * 512, 512) for j in range(4)] + [(T, NS)]

        PF = [es.enter_context(nc.psum_tensor("pf%d" % i, [128, 512], F32)) for i in range(6)]
        BPF = [Buf("pf%d" % i) for i in range(6)]
        PT = [es.enter_context(nc.psum_tensor("pt%d" % i, [128, 1024], BF16)) for i in range(2)]
        BPT = [Buf("pt%d" % i) for i in range(2)]
        rr = {'pf': 0, 'pt': 0}

        def next_pf():
            i = rr['pf']; rr['pf'] = (i + 1) % 6
            return PF[i], BPF[i]

        def next_pt():
            i = rr['pt']; rr['pt'] = (i + 1) % 2
            return PT[i], BPT[i]

        def load_const(sbf, name, shape, src, dt=F32):
            t = sbf(name, shape, dt); b = Buf(name)
            dma('pool' if dt != F32 else 'sp', t[:], src, writes=[b])
            return t, b

        ident, Bident = load_const(sb, "ident", [128, 128], C['ident'][:, :], BF16)
        identf, Bidentf = load_const(sb, "identf", [128, 128], C['ident'][:, :], F32)
        mhalf = sb("mhalf", [128, 512]); Bmhalf = Buf("mhalf")
        op('pool', lambda e: e.memset(mhalf[:], -0.5), writes=[Bmhalf])
        onesf = sb("onesf", [128, 128]); Bonesf = Buf("onesf")
        op('pool', lambda e: e.memset(onesf[:], 1.0), writes=[Bonesf])

        hT = sb("hT", [128, 8, TT], BF16); BhT = [Buf("hT%d" % i) for i in range(NTL)]
        omT = sb("omT", [128, 2, TT], BF16); BomT = Buf("omT")
        onsaT = sb("onsaT", [128, 4, TT], BF16); BonsaT = Buf("onsaT")
        for t_, b_ in ((omT, BomT), (onsaT, BonsaT)):
            op('pool', lambda e, t_=t_: e.memset(t_[:, :, T:TT], 0.0), writes=[b_])

        st = sb("st", [128, 16]); Bst = Buf("st")
        nsq = sb("nsq", [128, 8, HD]); Bnsq = Buf("nsq")
        nst = sb("nst", [128, 16]); Bnst = Buf("nst")
        rt = sb("rt", [128, 4, 8, 8]); Brt = Buf("rt")
        xt = [sb("xt%d" % i, [128, D]) for i in range(2)]; Bxt = [Buf("xt%d" % i) for i in range(2)]
        xn = [sb("xn0", [128, D], BF16)] * 2; Bxn = [Buf("xn0")] * 2

        def rms_to_T(src_rows, n, gain_t, Bgain, dstT, col0, Bdst, slot, from_sbuf=None):
            if from_sbuf is None:
                x_t, Bx = xt[slot], Bxt[slot]
                dma('sp', x_t[:n, :], src_rows, writes=[Bx])
            else:
                x_t, Bx = from_sbuf
            xn_t, Bn = xn[slot], Bxn[slot]
            op('act', lambda e: e.activation(out=xn_t[:n, :], in_=x_t[:n, :], func=AF.Square, accum_out=st[:n, 0:1]),
               reads=[Bx], writes=[Bn, Bst])
            op('dve', lambda e: e.tensor_scalar(out=st[:n, 0:1], in0=st[:n, 0:1], scalar1=1.0 / D, scalar2=EPS, op0=ALU.mult, op1=ALU.add),
               reads=[Bst], writes=[Bst])
            op('pool', lambda e: e.tensor_tensor(out=st[:n, 1:2], in0=st[:n, 0:1], in1=mhalf[:n, 0:1], op=ALU.pow),
               reads=[Bst, Bmhalf], writes=[Bst])
            op('dve', lambda e: e.scalar_tensor_tensor(out=xn_t[:n, :], in0=x_t[:n, :], scalar=st[:n, 1:2], in1=gain_t[:n, :],
                                                       op0=ALU.mult, op1=ALU.mult), reads=[Bx, Bst, Bgain], writes=[Bn])
            pt, Bp = next_pt()
            for k in range(8):
                op('pe', lambda e, k=k: e.transpose(out=pt[:, k * 128:k * 128 + n], in_=xn_t[:n, k * 128:(k + 1) * 128], identity=ident[:n, :n]),
                   reads=[Bn, Bident], writes=[Bp])
            op('act', lambda e: e.copy(out=dstT[:, :, col0:col0 + n], in_=pt[:, :].rearrange("p (k t) -> p k t", k=8)[:, :, :n]),
               reads=[Bp], writes=[Bdst])

        def headnorm(src, Bsrc, n, H, gain, Bgain, cs_ap=None, Bcs=None):
            op('dve', lambda e: e.tensor_tensor(out=nsq[:n, :H, :], in0=src, in1=src, op=ALU.mult), reads=[Bsrc], writes=[Bnsq])
            op('dve', lambda e: e.tensor_reduce(out=nst[:n, 0:H], in_=nsq[:n, :H, :], axis=AX.X, op=ALU.add), reads=[Bnsq], writes=[Bnst])
            op('dve', lambda e: e.tensor_scalar(out=nst[:n, 0:H], in0=nst[:n, 0:H], scalar1=1.0 / HD, scalar2=EPS, op0=ALU.mult, op1=ALU.add),
               reads=[Bnst], writes=[Bnst])
            op('pool', lambda e: e.tensor_tensor(out=nst[:n, 8:8 + H], in0=nst[:n, 0:H], in1=mhalf[:n, 0:H], op=ALU.pow),
               reads=[Bnst, Bmhalf], writes=[Bnst])
            op('dve', lambda e: e.tensor_tensor(out=src, in0=src, in1=nst[:n, 8:8 + H].unsqueeze(2).broadcast_to([n, H, HD]), op=ALU.mult),
               reads=[Bsrc, Bnst], writes=[Bsrc])
            op('dve', lambda e: e.tensor_tensor(out=src, in0=src, in1=gain[:n, :H, :], op=ALU.mult), reads=[Bsrc, Bgain], writes=[Bsrc])
            if cs_ap is not None:
                a = src[:, :, 0:8]; b = src[:, :, 8:16]
                cos = cs_ap[:, 0:8].unsqueeze(1).broadcast_to([n, H, 8]); sin = cs_ap[:, 8:16].unsqueeze(1).broadcast_to([n, H, 8])
                for j, (u, v) in enumerate([(a, cos), (b, sin), (a, sin), (b, cos)]):
                    op('dve', lambda e, j=j, u=u, v=v: e.tensor_tensor(out=rt[:n, j, :H, :], in0=u, in1=v, op=ALU.mult),
                       reads=[Bsrc, Bcs], writes=[Brt])
                op('dve', lambda e: e.tensor_tensor(out=a, in0=rt[:n, 0, :H, :], in1=rt[:n, 1, :H, :], op=ALU.subtract), reads=[Brt], writes=[Bsrc])
                op('dve', lambda e: e.tensor_tensor(out=b, in0=rt[:n, 2, :H, :], in1=rt[:n, 3, :H, :], op=ALU.add), reads=[Brt], writes=[Bsrc])

        def dbg_dump(name, t, B):
            if DBG:
                for k in range(t.shape[1]):
                    dma('pool', dbg[name][:, k, :], t[:, k, :], reads=[B], is_output=True)

        esT = ExitStack()
        with esT:
            sbT = mk_sb(esT)
            cs_tok, Bcs = load_const(sbT, "cs_tok", [128, NTL, 16], C['cs_tok'][:, :, :])
            cs_cmp, Bcsc = load_const(sbT, "cs_cmp", [128, 16], C['cs_cmp'][:, :])
            gq = sbT("gq", [128, 8, HD]); Bgq = Buf("gq")
            for h in range(8):
                dma('sp', gq[:, h, :], bc(q_norm), writes=[Bgq])
            gk = sbT("gk", [128, 4, HD]); Bgk = Buf("gk")
            for j in range(4):
                dma('sp', gk[:, j, :], bc(k_norm[1 + j // 2:2 + j // 2, :]), writes=[Bgk])
            gkc = sbT("gkc", [128, 2, HD]); Bgkc = Buf("gkc")
            for j in range(2):
                dma('sp', gkc[:, j, :], bc(k_norm[0:1, :]), writes=[Bgkc])
            gmq = sbT("gmq", [128, 4, HD]); Bgmq = Buf("gmq")
            gmk = sbT("gmk", [128, 4, HD]); Bgmk = Buf("gmk")
            for j in range(4):
                dma('sp', gmq[:, j, :], bc(mq_norm), writes=[Bgmq])
                dma('sp', gmk[:, j, :], bc(mk_norm), writes=[Bgmk])

            QT = sbT("QT", [96, 8, TT], BF16); BQT = [Buf("QT%d" % i) for i in range(NTL)]
            KsT = sbT("KsT", [96, 2, T], BF16); BKsT = Buf("KsT")
            KwT = sbT("KwT", [64, 2, T], BF16); BKwT = Buf("KwT")
            KcTr = sbT("KcTr", [128, T], BF16); BKcTr = Buf("KcTr")
            VcTr = sbT("VcTr", [128, T], BF16); BVcTr = Buf("VcTr")
            Vs = sbT("Vs", [128, 16, 2, 65], BF16); BVs = Buf("Vs")
            Vw = sbT("Vw", [128, 16, 2, 65], BF16); BVw = Buf("Vw")
            gates = sbT("gates", [128, NTL, 24]); Bgates = Buf("gates")
            KnT = sbT("KnT", [128, 2, NS], BF16); BKnT = Buf("KnT")
            Vn = sbT("Vn", [NS, 2, 128], BF16); BVn = Buf("Vn")
            q16 = sbT("q16", [NS, 512], BF16); Bq16 = Buf("q16")
            qm16 = sbT("qm16", [NS, 256], BF16); Bqm16 = Buf("qm16")
            KmT = sbT("KmT", [64, 4, 256], BF16); BKmT = Buf("KmT")
            Vm = sbT("Vm", [128, 2, 4, 65], BF16); BVm = Buf("Vm")
            op('pool', lambda e: e.memset(Vs[:, :, :, 64:65], 1.0), writes=[BVs])
            op('pool', lambda e: e.memset(Vw[:, :, :, 64:65], 1.0), writes=[BVw])
            op('pool', lambda e: e.memset(Vm[:, :, :, 64:65], 1.0), writes=[BVm])
            for g in range(2):
                dma('pool', KsT[64:96, g, :], C['E'][:, :], writes=[BKsT])

            esB = ExitStack()
            with esB:
                sbB = mk_sb(esB)
                gmem, Bgmem = load_const(sbB, "gmem", [128, D], bc(norm_mem))
                wMK = sbB("wMK", [128, 8, 512], BF16); BwMK = Buf("wMK")
                dma('pool', wMK[:], w_mem_kv.rearrange("(kc p) c -> p kc c", p=128), writes=[BwMK])
                mT_ = sbB("mT_", [128, 8, 256], BF16); BmT_ = Buf("mT_")
                mk_t = sbB("mk_t", [128, 512]); Bmk = Buf("mk_t")
                mkb = sbB("mkb", [128, 256], BF16); Bmkb = Buf("mkb")
                for i in range(2):
                    rms_to_T(mem[i * 128:(i + 1) * 128, :], 128, gmem, Bgmem, mT_, i * 128, BmT_, i)
                    pf, Bp = next_pf()
                    for k in range(8):
                        op('pe', lambda e, k=k, pf=pf: e.matmul(pf[:, :], lhsT=mT_[:, k, i * 128:(i + 1) * 128], rhs=wMK[:, k, :], start=(k == 0), stop=(k == 7)),
                           reads=[BmT_, BwMK], writes=[Bp])
                    op('act', lambda e, pf=pf: e.copy(out=mk_t[:, :], in_=pf[:, :]), reads=[Bp], writes=[Bmk])
                    headnorm(mk_t[:, 0:256].rearrange("p (h d) -> p h d", h=4), Bmk, 128, 4, gmk, Bgmk)
                    dma('sp', o_memkv[i * 128:(i + 1) * 128, :], mk_t[:, :], reads=[Bmk], is_output=True)
                    op('act', lambda e: e.copy(out=mkb[:, :], in_=mk_t[:, 0:256]), reads=[Bmk], writes=[Bmkb])
                    op('dve', lambda e: e.tensor_copy(out=Vm[:, i, :, 0:64], in_=mk_t[:, 256:512].rearrange("p (h d) -> p h d", h=4)), reads=[Bmk], writes=[BVm])
                    pt, Bp = next_pt()
                    for h in range(4):
                        op('pe', lambda e, h=h, pt=pt: e.transpose(out=pt[0:64, h * 128:(h + 1) * 128], in_=mkb[:, h * 64:(h + 1) * 64], identity=ident[:, :]),
                           reads=[Bmkb, Bident], writes=[Bp])
                    op('dve', lambda e, pt=pt: e.tensor_copy(out=KmT[0:64, :, i * 128:(i + 1) * 128], in_=pt[0:64, 0:512].rearrange("p (k t) -> p k t", k=4)),
                       reads=[Bp], writes=[BKmT])
                kb.barrier()

            esA = ExitStack()
            with esA:
                sbA = mk_sb(esA)
                gattn, Bgattn = load_const(sbA, "gattn", [128, D], bc(norm_attn))
                wA = sbA("wA", [128, 8, 1304], BF16); BwA = Buf("wA")
                for k in range(8):
                    dma('pool', wA[:, k, :], w_in_v[:, k, 0:1304], writes=[BwA])
                wQM = sbA("wQM", [128, 8, 256], BF16); BwQM = Buf("wQM")
                dma('pool', wQM[:], w_in_v[:, :, O_QM:O_QM + 256], writes=[BwQM])
                qf = sbA("qf", [128, 8, HD]); Bqf = Buf("qf")
                qb = sbA("qb", [128, 512], BF16); Bqb = Buf("qb")
                rows_t = [sbA("rows_t0", [128, 512])] * 2; Brows = [Buf("rows0")] * 2
                win_t = [sbA("win_t0", [128, 256])] * 2; Bwin = [Buf("win0")] * 2
                kk = sbA("kk", [128, 4, HD]); Bkk = Buf("kk")
                kkb = sbA("kkb", [128, 4, HD], BF16); Bkkb = Buf("kkb")
                kvcb = sbA("kvcb", [128, 256], BF16); Bkvcb = Buf("kvcb")
                qmf = sbA("qmf", [128, 4, HD]); Bqmf = Buf("qmf")
                qmb = sbA("qmb", [128, 256], BF16); Bqmb = Buf("qmb")
                qmT = sbA("qmT", [64, 4, 128], BF16); BqmT = Buf("qmT")
                PmT = [sbA("PmT%d" % j, [128, 512], BF16) for j in range(2)]; BPmT = [Buf("PmT%d" % j) for j in range(2)]
                omt = sbA("omt", [128, 4, HD], BF16); Bomt = Buf("omt")
                mrec = sbA("mrec", [128, 4]); Bmrec = Buf("mrec")

                for i in (TILES if TILES is not None else list(range(NTL))):
                    n = tsz(i); t0 = i * 128; sl = i % 2
                    rms_to_T(xs[t0:t0 + n, :], n, gattn, Bgattn, hT, t0, BhT[i], sl)
                    zp = []
                    for cbi, (wt, Bw, c0, cw) in enumerate([(wA, BwA, 0, 512), (wA, BwA, 512, 512), (wA, BwA, 1024, 280), (wQM, BwQM, 0, 256)]):
                        pf, Bp = next_pf()
                        for k in range(8):
                            op('pe', lambda e, k=k, pf=pf, wt=wt, c0=c0, cw=cw: e.matmul(pf[:n, 0:cw], lhsT=hT[:, k, t0:t0 + n], rhs=wt[:, k, c0:c0 + cw],
                                                                                         start=(k == 0), stop=(k == 7)),
                               reads=[BhT[i], Bw], writes=[Bp])
                        zp.append((pf, Bp))
                    (p0, B0), (p1, B1), (p2, B2), (p3, B3) = zp
                    op('act', lambda e: e.copy(out=qf[:n].rearrange("p h d -> p (h d)"), in_=p0[:n, 0:512]), reads=[B0], writes=[Bqf])
                    headnorm(qf[:n, :, :], Bqf, n, 8, gq, Bgq, cs_tok[:n, i, :], Bcs)
                    tgt_q, Btq = (qb, Bqb) if i < 16 else (q16, Bq16)
                    op('act', lambda e: e.copy(out=tgt_q[:n, :], in_=qf[:n].rearrange("p h d -> p (h d)")), reads=[Bqf], writes=[Btq])
                    pt, Bp = next_pt()
                    for h in range(8):
                        op('pe', lambda e, h=h, pt=pt: e.transpose(out=pt[0:64, h * 128:h * 128 + n], in_=tgt_q[:n, h * 64:(h + 1) * 64], identity=ident[:n, :n]),
                           reads=[Btq, Bident], writes=[Bp])
                    op('dve', lambda e, pt=pt: e.tensor_copy(out=QT[0:64, :, t0:t0 + n], in_=pt[0:64, :].rearrange("p (k t) -> p k t", k=8)[:, :, :n]),
                       reads=[Bp], writes=[BQT[i]])
                    rw, Brw = rows_t[sl], Brows[sl]; wn, Bwn = win_t[sl], Bwin[sl]
                    op('act', lambda e: e.copy(out=rw[:n, :], in_=p1[:n, 0:512]), reads=[B1], writes=[Brw])
                    op('act', lambda e: e.copy(out=wn[:n, :], in_=p2[:n, 0:256]), reads=[B2], writes=[Bwn])
                    op('act', lambda e: e.activation(out=gates[:n, i, :], in_=p2[:n, 256:280], func=AF.Sigmoid), reads=[B2], writes=[Bgates])
                    op('dve', lambda e: e.tensor_copy(out=kk[:n, 0:2, :], in_=rw[:n, 256:384].rearrange("p (g d) -> p g d", g=2)), reads=[Brw], writes=[Bkk])
                    op('dve', lambda e: e.tensor_copy(out=kk[:n, 2:4, :], in_=wn[:n, 0:128].rearrange("p (g d) -> p g d", g=2)), reads=[Bwn], writes=[Bkk])
                    headnorm(kk[:n, :, :], Bkk, n, 4, gk, Bgk, cs_tok[:n, i, :], Bcs)
                    op('dve', lambda e: e.tensor_copy(out=rw[:n, 256:384].rearrange("p (g d) -> p g d", g=2), in_=kk[:n, 0:2, :]), reads=[Bkk], writes=[Brw])
                    op('dve', lambda e: e.tensor_copy(out=wn[:n, 0:128].rearrange("p (g d) -> p g d", g=2), in_=kk[:n, 2:4, :]), reads=[Bkk], writes=[Bwn])
                    op('act', lambda e: e.copy(out=kkb[:n], in_=kk[:n]), reads=[Bkk], writes=[Bkkb])
                    dma('sp', o_rows[t0:t0 + n, :], rw[:n, :], reads=[Brw], is_output=True)
                    if 12 <= i < 16:
                        dma('sp', o_winp[(i - 12) * 128:(i - 11) * 128, :], wn[:n, :], reads=[Bwn], is_output=True)
                    if i == 16:
                        dma('sp', o_wins[:, 511, :], wn[:n, :], reads=[Bwn], is_output=True)
                    if i < 16:
                        pt, Bp = next_pt()
                        for j in range(4):
                            op('pe', lambda e, j=j, pt=pt: e.transpose(out=pt[0:64, j * 128:(j + 1) * 128], in_=kkb[:n, j, :], identity=ident[:n, :n]),
                               reads=[Bkkb, Bident], writes=[Bp])
                        op('act', lambda e: e.copy(out=kvcb[:n, :], in_=p1[:n, 0:256]), reads=[B1], writes=[Bkvcb])
                        for j in range(2):
                            op('pe', lambda e, j=j, pt=pt: e.transpose(out=pt[:, (4 + j) * 128:(5 + j) * 128], in_=kvcb[:n, j * 128:(j + 1) * 128], identity=ident[:n, :n]),
                               reads=[Bkvcb, Bident], writes=[Bp])
                        op('dve', lambda e, pt=pt: e.tensor_copy(out=KsT[0:64, :, t0:t0 + n], in_=pt[0:64, 0:256].rearrange("p (g t) -> p g t", g=2)),
                           reads=[Bp], writes=[BKsT])
                        op('dve', lambda e, pt=pt: e.tensor_copy(out=KwT[0:64, :, t0:t0 + n], in_=pt[0:64, 256:512].rearrange("p (g t) -> p g t", g=2)),
                           reads=[Bp], writes=[BKwT])
                        op('dve', lambda e, pt=pt: e.tensor_copy(out=KcTr[:, :].rearrange("p (r m) -> p r m", r=16)[:, :, 8 * i:8 * i + 8].rearrange("p r m -> p m r"), in_=pt[:, 512:640].rearrange("p (m r) -> p m r", r=16)), reads=[Bp], writes=[BKcTr])
                        op('dve', lambda e, pt=pt: e.tensor_copy(out=VcTr[:, :].rearrange("p (r m) -> p r m", r=16)[:, :, 8 * i:8 * i + 8].rearrange("p r m -> p m r"), in_=pt[:, 640:768].rearrange("p (m r) -> p m r", r=16)), reads=[Bp], writes=[BVcTr])
                        op('dve', lambda e: e.tensor_copy(out=Vs[:n, i, :, 0:64], in_=p1[:n, 384:512].rearrange("p (g d) -> p g d", g=2)), reads=[B1], writes=[BVs])
                        op('dve', lambda e: e.tensor_copy(out=Vw[:n, i, :, 0:64], in_=p2[:n, 128:256].rearrange("p (g d) -> p g d", g=2)), reads=[B2], writes=[BVw])
                    else:
                        pt, Bp = next_pt()
                        for j in range(2):
                            op('pe', lambda e, j=j, pt=pt: e.transpose(out=pt[:, j * 128:j * 128 + n], in_=kkb[:n, 2 * j:2 * j + 2, :].rearrange("p g d -> p (g d)"),
                                                                        identity=ident[:n, :n]), reads=[Bkkb, Bident], writes=[Bp])
                        op('dve', lambda e, pt=pt: e.tensor_copy(out=KnT[:, :, :], in_=pt[:, 0:256].rearrange("p (j t) -> p j t", j=2)[:, :, :n]),
                           reads=[Bp], writes=[BKnT])
                        op('dve', lambda e: e.tensor_copy(out=Vn[:n, 0, :], in_=p1[:n, 384:512]), reads=[B1], writes=[BVn])
                        op('dve', lambda e: e.tensor_copy(out=Vn[:n, 1, :], in_=p2[:n, 128:256]), reads=[B2], writes=[BVn])
                    op('act', lambda e: e.copy(out=qmf[:n].rearrange("p h d -> p (h d)"), in_=p3[:n, 0:256]), reads=[B3], writes=[Bqmf])
                    headnorm(qmf[:n, :, :], Bqmf, n, 4, gmq, Bgmq)
                    tgt_m, Btm = (qmb, Bqmb) if i < 16 else (qm16, Bqm16)
                    op('act', lambda e: e.copy(out=tgt_m[:n, :], in_=qmf[:n].rearrange("p h d -> p (h d)")), reads=[Bqmf], writes=[Btm])
                    if i < 16 and phases >= 4:
                        pt, Bp = next_pt()
                        for h in range(4):
                            op('pe', lambda e, h=h, pt=pt: e.transpose(out=pt[0:64, h * 128:h * 128 + n], in_=tgt_m[:n, h * 64:(h + 1) * 64], identity=ident[:n, :n]),
                               reads=[Btm, Bident], writes=[Bp])
                        op('dve', lambda e, pt=pt: e.tensor_copy(out=qmT[0:64, :, :n], in_=pt[0:64, 0:512].rearrange("p (k t) -> p k t", k=4)[:, :, :n]),
                           reads=[Bp], writes=[BqmT])
                        for nt in range(2):
                            pf, Bp = next_pf()
                            for h in range(4):
                                op('pe', lambda e, h=h, pf=pf, nt=nt: e.matmul(pf[:, h * 128:h * 128 + n], lhsT=KmT[0:64, h, nt * 128:(nt + 1) * 128], rhs=qmT[0:64, h, :n],
                                                                               start=True, stop=True), reads=[BKmT, BqmT], writes=[Bp])
                            op('act', lambda e, pf=pf, nt=nt: e.activation(out=PmT[nt][:, :], in_=pf[:, :], func=AF.Exp, scale=0.125), reads=[Bp], writes=[BPmT[nt]])
                        pf, Bp = next_pf()
                        for h in range(4):
                            for nt in range(2):
                                op('pe', lambda e, h=h, pf=pf, nt=nt: e.matmul(pf[:n, h * 65:(h + 1) * 65], lhsT=PmT[nt][:, h * 128:h * 128 + n], rhs=Vm[:, nt, h, :],
                                                                               start=(nt == 0), stop=(nt == 1)), reads=[BPmT[nt], BVm], writes=[Bp])
                        pv = pf[:n, 0:260].rearrange("p (h c) -> p h c", c=65)
                        op('dve', lambda e, pv=pv: e.reciprocal(out=mrec[:n, :].unsqueeze(2), in_=pv[:, :, 64:65]), reads=[Bp], writes=[Bmrec])
                        op('dve', lambda e, pv=pv: e.tensor_tensor(out=omt[:n, :, :], in0=pv[:, :, 0:64], in1=mrec[:n, :].unsqueeze(2).broadcast_to([n, 4, HD]), op=ALU.mult),
                           reads=[Bp, Bmrec], writes=[Bomt])
                        pt, Bp = next_pt()
                        for j in range(2):
                            op('pe', lambda e, j=j, pt=pt: e.transpose(out=pt[:, j * 128:j * 128 + n], in_=omt[:n, 2 * j:2 * j + 2, :].rearrange("p h d -> p (h d)"),
                                                                        identity=ident[:n, :n]), reads=[Bomt, Bident], writes=[Bp])
                        op('act', lambda e, pt=pt: e.copy(out=omT[:, :, t0:t0 + n], in_=pt[:, 0:256].rearrange("p (j t) -> p j t", j=2)[:, :, :n]),
                           reads=[Bp], writes=[BomT])
                kb.barrier()

            esC = ExitStack()
            with esC:
                sbC = mk_sb(esC)
                dma('sp', o_wins[:, 0:511, :], cache_win[:, 1:512, :], is_output=True)
                dma('sp', o_convs[:, 0:29, :], cache_conv[:, 1:30, :], is_output=True)
                dma('sp', o_ffns[:, 0, :], cache_ffn[:, 1, :], is_output=True)
                wUk = [sbC("wUk%d" % j, [128, 1024], BF16) for j in range(2)]; BwUk = [Buf("wUk%d" % j) for j in range(2)]
                sig_t = sbC("sig_t", [128, 512]); Bsig = Buf("sig_t")
                glu_t = [sbC("glu_t%d" % i, [128, 512]) for i in range(2)]; Bglu = [Buf("glu_t%d" % i) for i in range(2)]
                pabs = {}
                for i in (15, 16):
                    for cbi in range(2):
                        pabs[(i, cbi)] = next_pf()
                kb.pe_force_inc = True
                for k in range(8):
                    dma('pool', wUk[k % 2][:, :], w_in_v[:, k, O_UC:O_UC + 1024], writes=[BwUk[k % 2]])
                    for i in (15, 16):
                        n = tsz(i); t0 = i * 128
                        for cbi in range(2):
                            pf, Bp = pabs[(i, cbi)]
                            op('pe', lambda e, k=k, pf=pf, cbi=cbi, n=n, t0=t0: e.matmul(pf[:n, :], lhsT=hT[:, k, t0:t0 + n], rhs=wUk[k % 2][:, cbi * 512:(cbi + 1) * 512],
                                                                                       start=(k == 0), stop=(k == 7)), reads=[BhT[i], BwUk[k % 2]], writes=[Bp])
                kb.pe_force_inc = False
                for ii, i in enumerate((15, 16)):
                    n = tsz(i); t0 = i * 128
                    (pa, Ba), (pb, Bb) = pabs[(i, 0)], pabs[(i, 1)]
                    op('act', lambda e, pb=pb: e.activation(out=sig_t[:n, :], in_=pb[:n, :], func=AF.Sigmoid), reads=[Bb], writes=[Bsig])
                    op('dve', lambda e, pa=pa, ii=ii: e.tensor_tensor(out=glu_t[ii][:n, :], in0=pa[:n, :], in1=sig_t[:n, :], op=ALU.mult),
                       reads=[Ba, Bsig], writes=[Bglu[ii]])
                    if i == 15:
                        dma('sp', o_convp[:, :], glu_t[ii][98:128, :], reads=[Bglu[ii]], is_output=True)
                    else:
                        dma('sp', o_convs[:, 29, :], glu_t[ii][:n, :], reads=[Bglu[ii]], is_output=True)
                kb.barrier()

            if phases >= 8:
                esE = ExitStack()
                with esE:
                    sbE = mk_sb(esE)
                    cm = sbE("cm", [128, NS, 2, 512], BF16); Bcm = Buf("cm")
                    for s_i in range(NS):
                        dma('pool', cm[:, s_i, :, :], cache_mem[s_i].rearrange("(nt p) c -> p nt c", p=128), writes=[Bcm])
                    QmP = sbE("QmP", [128, 2, NS], BF16); BQmP = Buf("QmP")
                    pt, Bp = next_pt()
                    for hp in range(2):
                        op('pe', lambda e, hp=hp, pt=pt: e.transpose(out=pt[:, hp * 128:hp * 128 + NS], in_=qm16[:NS, hp * 128:(hp + 1) * 128], identity=ident[:NS, :NS]),
                           reads=[Bqm16, Bident], writes=[Bp])
                    op('dve', lambda e, pt=pt: e.tensor_copy(out=QmP[:, :, :], in_=pt[:, 0:256].rearrange("p (j t) -> p j t", j=2)[:, :, :NS]), reads=[Bp], writes=[BQmP])
                    Qmbd = sbE("Qmbd", [128, NS, 2, 2], BF16); BQmbd = Buf("Qmbd")
                    op('pool', lambda e: e.memset(Qmbd[:].rearrange("p s a b -> p (s a b)"), 0.0), writes=[BQmbd])
                    op('dve', lambda e: e.tensor_copy(out=Qmbd[0:64, :, :, 0], in_=QmP[0:64, :, :].rearrange("p hp s -> p s hp")), reads=[BQmP, BQmbd], writes=[BQmbd])
                    op('dve', lambda e: e.tensor_copy(out=Qmbd[64:128, :, :, 1], in_=QmP[64:128, :, :].rearrange("p hp s -> p s hp")), reads=[BQmP, BQmbd], writes=[BQmbd])
                    KmsT = [sbE("KmsT%d" % j, [128, 2, 256], BF16) for j in range(2)]; BKmsT = [Buf("KmsT%d" % j) for j in range(2)]
                    pSm, BSm = PF[0], BPF[0]
                    for s_i in range(NS):
                        pt, Bp = next_pt()
                        for hp in range(2):
                            for nt in range(2):
                                op('pe', lambda e, hp=hp, nt=nt, pt=pt, s_i=s_i: e.transpose(out=pt[:, (hp * 2 + nt) * 128:(hp * 2 + nt + 1) * 128], in_=cm[:, s_i, nt, hp * 128:(hp + 1) * 128],
                                                                                          identity=ident[:, :]), reads=[Bcm, Bident], writes=[Bp])
                        kt, Bkt = KmsT[s_i % 2], BKmsT[s_i % 2]
                        op('dve', lambda e, pt=pt, kt=kt: e.tensor_copy(out=kt[:, :, :], in_=pt[:, 0:512].rearrange("p (hp n) -> p hp n", hp=2)), reads=[Bp], writes=[Bkt])
                        for nt in range(2):
                            for hp in range(2):
                                c0_ = s_i * 8 + nt * 4 + hp * 2
                                op('pe', lambda e, hp=hp, nt=nt, kt=kt, c0_=c0_, s_i=s_i: e.matmul(pSm[:, c0_:c0_ + 2], lhsT=kt[:, hp, nt * 128:(nt + 1) * 128], rhs=Qmbd[:, s_i, hp, :],
                                                                                                start=True, stop=True), reads=[Bkt, BQmbd], writes=[BSm])
                    PmsT = sbE("PmsT", [128, NS, 2, 4], BF16); BPmsT = Buf("PmsT")
                    op('act', lambda e: e.activation(out=PmsT[:].rearrange("p s a b -> p (s a b)"), in_=pSm[:, 0:128], func=AF.Exp, scale=0.125), reads=[BSm], writes=[BPmsT])
                    Rm = sbE("Rm", [128, NS, 4]); BRm = Buf("Rm")
                    op('dve', lambda e: e.tensor_tensor(out=Rm[:, :, :], in0=PmsT[:, :, 0, :], in1=PmsT[:, :, 1, :], op=ALU.add), reads=[BPmsT], writes=[BRm])
                    pden, Bpden = PF[1], BPF[1]
                    for s_i in range(NS):
                        op('pe', lambda e, s_i=s_i: e.matmul(pden[0:4, s_i:s_i + 1], lhsT=Rm[:, s_i, :], rhs=onesf[:, 0:1], start=True, stop=True), reads=[BRm, Bonesf], writes=[Bpden])
                    rden = sbE("rden", [4, NS]); Brden = Buf("rden")
                    op('dve', lambda e: e.reciprocal(out=rden[:, :], in_=pden[0:4, 0:NS]), reads=[Bpden], writes=[Brden])
                    oms = sbE("oms", [4, NS, HD]); Boms = Buf("oms")
                    otmp = sbE("otmp", [4, 2, 4, HD]); Botmp = Buf("otmp")
                    for s2 in range(NS // 2):
                        pO, BpO = next_pf()
                        for ss in range(2):
                            s_i = 2 * s2 + ss
                            for nt in range(2):
                                op('pe', lambda e, ss=ss, nt=nt, s_i=s_i, pO=pO: e.matmul(pO[0:4, ss * 256:(ss + 1) * 256], lhsT=PmsT[:, s_i, nt, :], rhs=cm[:, s_i, nt, 256:512],
                                                                                       start=(nt == 0), stop=(nt == 1)), reads=[BPmsT, Bcm], writes=[BpO])
                        op('dve', lambda e, pO=pO: e.tensor_tensor(out=otmp[:], in0=pO[0:4, :].rearrange("p (s h d) -> p s h d", s=2, h=4),
                                                                   in1=identf[0:4, 0:4].unsqueeze(1).unsqueeze(3).broadcast_to([4, 2, 4, HD]), op=ALU.mult),
                           reads=[BpO, Bidentf], writes=[Botmp])
                        op('dve', lambda e, s2=s2: e.tensor_reduce(out=oms[:, 2 * s2:2 * s2 + 2, :], in_=otmp[:].rearrange("p s h d -> p s d h"), axis=AX.X, op=ALU.add),
                           reads=[Botmp], writes=[Boms])
                    op('dve', lambda e: e.tensor_tensor(out=oms[:], in0=oms[:], in1=rden[:, :].unsqueeze(2).broadcast_to([4, NS, HD]), op=ALU.mult), reads=[Boms, Brden], writes=[Boms])
                    dma('sp', scr[0:4 * NS * HD].rearrange("(h s d) -> h s d", h=4, s=NS), oms[:, :, :], reads=[Boms], writes=[Bscr])
                    om16 = sbE("om16", [NS, 4, HD]); Bom16 = Buf("om16")
                    dma('sp', om16[:, :, :], scr[0:4 * NS * HD].rearrange("(h s d) -> s h d", h=4, s=NS), reads=[Bscr], writes=[Bom16])
                    om16b = sbE("om16b", [NS, 256], BF16); Bom16b = Buf("om16b")
                    op('dve', lambda e: e.tensor_copy(out=om16b[:, :], in_=om16[:].rearrange("s h d -> s (h d)")), reads=[Bom16], writes=[Bom16b])
                    pt, Bp = next_pt()
                    for j in range(2):
                        op('pe', lambda e, j=j, pt=pt: e.transpose(out=pt[:, j * 128:j * 128 + NS], in_=om16b[:NS, j * 128:(j + 1) * 128], identity=ident[:NS, :NS]),
                           reads=[Bom16b, Bident], writes=[Bp])
                    op('act', lambda e, pt=pt: e.copy(out=omT[:, :, T:TT], in_=pt[:, 0:256].rearrange("p (j t) -> p j t", j=2)[:, :, :NS]), reads=[Bp], writes=[BomT])
                    kb.barrier()

            if phases >= 6:
                esF = ExitStack()
                with esF:
                    sbF = mk_sb(esF)
                    KcT = sbF("KcT", [64, 2, 128], BF16); BKcT = Buf("KcT")
                    Vc = sbF("Vc", [128, 2, 97], BF16); BVc = Buf("Vc")
                    esF1 = ExitStack(); esF1.__enter__(); sbF_keep = sbF; sbF = mk_sb(esF1)
                    W2 = sbF("W2", [128, 2, 32, 128], BF16); BW2 = Buf("W2")
                    op('pool', lambda e: e.memset(W2[:].rearrange("p a l e -> p (a l e)"), 0.0), writes=[BW2])
                    wc_v = w_cmp.rearrange("kv l d e -> d kv l e")
                    for kv_ in range(2):
                        dma('pool', W2[0:64, kv_, :, 0:64], wc_v[:, kv_, :, :], reads=[BW2], writes=[BW2])
                        dma('pool', W2[64:128, kv_, :, 64:128], wc_v[:, kv_, :, :], reads=[BW2], writes=[BW2])
                    pe2 = sbF("pe2", [64, 128]); Bpe2 = Buf("pe2")
                    pe_v = cmp_pe.rearrange("kv l d -> (kv l) d")
                    dma('sp', pe2[:, 0:64], pe_v, writes=[Bpe2]); dma('sp', pe2[:, 64:128], pe_v, writes=[Bpe2])
                    peT2 = sbF("peT2", [128, 64], BF16); BpeT2 = Buf("peT2")
                    pf, Bp = next_pf()
                    op('pe', lambda e, pf=pf: e.transpose(out=pf[:, 0:64], in_=pe2[:, :], identity=identf[0:64, 0:64]), reads=[Bpe2, Bidentf], writes=[Bp])
                    op('dve', lambda e, pf=pf: e.tensor_copy(out=peT2[:, :], in_=pf[:, 0:64]), reads=[Bp], writes=[BpeT2])
                    cpe = sbF("cpe", [1, 2, 128]); Bcpe = Buf("cpe")
                    cpeb = sbF("cpeb", [128, 2, 128]); Bcpeb = Buf("cpeb")
                    for kv_ in range(2):
                        pf, Bp = next_pf()
                        for l in range(32):
                            op('pe', lambda e, l=l, pf=pf, kv_=kv_: e.matmul(pf[0:1, 0:128], lhsT=peT2[:, kv_ * 32 + l:kv_ * 32 + l + 1], rhs=W2[:, kv_, l, :],
                                                                             start=(l == 0), stop=(l == 31)), reads=[BpeT2, BW2], writes=[Bp])
                        op('dve', lambda e, pf=pf, kv_=kv_: e.tensor_copy(out=cpe[0:1, kv_, :], in_=pf[0:1, 0:128]), reads=[Bp], writes=[Bcpe])
                        pf2, Bp2 = next_pf()
                        op('pe', lambda e, pf2=pf2, kv_=kv_: e.matmul(pf2[:, 0:128], lhsT=onesf[0:1, :], rhs=cpe[0:1, kv_, :], start=True, stop=True),
                           reads=[Bcpe, Bonesf], writes=[Bp2])
                        op('dve', lambda e, pf2=pf2, kv_=kv_: e.tensor_copy(out=cpeb[:, kv_, :], in_=pf2[:, 0:128]), reads=[Bp2], writes=[Bcpeb])
                    kcf = sbF("kcf", [128, 2, HD]); Bkcf = Buf("kcf")
                    kcb = sbF("kcb", [128, 2, HD], BF16); Bkcb = Buf("kcb")
                    ovl, Bovl = load_const(sbF, "ovl", [128, 33], C['ovl'][:, :])
                    for g in range(2):
                        op('dve', lambda e, g=g: e.tensor_copy(out=Vc[:, g, 0:33], in_=ovl[:, :]), reads=[Bovl], writes=[BVc])
                    for kv_, (src, Bsrc) in enumerate(((KcTr, BKcTr), (VcTr, BVcTr))):
                        pf, Bp = next_pf()
                        for l in range(32):
                            op('pe', lambda e, l=l, pf=pf, kv_=kv_, src=src: e.matmul(pf[0:127, 0:128], lhsT=src[:, (l % 16) * 128 + l // 16:(l % 16) * 128 + l // 16 + 127], rhs=W2[:, kv_, l, :],
                                                                                      start=(l == 0), stop=(l == 31)), reads=[Bsrc, BW2], writes=[Bp])
                        if kv_ == 0:
                            op('dve', lambda e, pf=pf: e.tensor_tensor(out=kcf[0:127].rearrange("p g d -> p (g d)"), in0=pf[0:127, 0:128], in1=cpeb[0:127, 0, :], op=ALU.add),
                               reads=[Bp, Bcpeb], writes=[Bkcf])
                            headnorm(kcf[0:127, :, :], Bkcf, 127, 2, gkc, Bgkc, cs_cmp[0:127, :], Bcsc)
                            op('act', lambda e: e.copy(out=kcb[0:127], in_=kcf[0:127]), reads=[Bkcf], writes=[Bkcb])
                            pt, Bpp = next_pt()
                            for g in range(2):
                                op('pe', lambda e, g=g, pt=pt: e.transpose(out=pt[0:64, g * 128:g * 128 + 127], in_=kcb[0:127, g, :], identity=ident[0:127, 0:127]),
                                   reads=[Bkcb, Bident], writes=[Bpp])
                            op('dve', lambda e, pt=pt: e.tensor_copy(out=KcT[:, :, 0:127], in_=pt[0:64, 0:256].rearrange("p (g t) -> p g t", g=2)[:, :, 0:127]),
                               reads=[Bpp], writes=[BKcT])
                        else:
                            op('dve', lambda e, pf=pf: e.tensor_tensor(out=Vc[0:127, :, 33:97], in0=pf[0:127, 0:128].rearrange("p (g d) -> p g d", g=2),
                                                                       in1=cpeb[0:127, 1, :].rearrange("p (g d) -> p g d", g=2), op=ALU.add),
                               reads=[Bp, Bcpeb], writes=[BVc])
                    if phases >= 9:
                        ptb = sbF("ptb", [128, 256], I32); Bptb = Buf("ptb")
                        dma('sp', ptb[:, :], ptab.partition_broadcast(128).rearrange("p a d -> p (a d)"), writes=[Bptb])
                        io_i = sbF("io_i", [128, 1], I32); Bio = Buf("io")
                        op('pool', lambda e: e.iota(out=io_i[:, :], pattern=[[0, 1]], base=0, channel_multiplier=1), writes=[Bio])
                        io_f = sbF("io_f", [128, 1]); Biof = Buf("io_f")
                        op('dve', lambda e: e.tensor_copy(out=io_f[:, :], in_=io_i[:, :]), reads=[Bio], writes=[Biof])
                        idx = sbF("idx", [128, 256], I32); Bidx = Buf("idx")
                        op('dve', lambda e: e.tensor_scalar(out=idx[:, :], in0=ptb[:, :], scalar1=128.0, scalar2=io_f[:, 0:1], op0=ALU.mult, op1=ALU.add),
                           reads=[Bptb, Biof], writes=[Bidx])
                        wGA = sbF("wGA", [128, 8, 24], BF16); BwGA = Buf("wGA")
                        dma('pool', wGA[:], w_in_v[:, :, O_GA:O_GA + 24], writes=[BwGA])
                        gT = sbF("gT", [8, 3, NS]); BgT = Buf("gT")
                        pf, Bp = next_pf()
                        for b_ in range(3):
                            for k in range(8):
                                op('pe', lambda e, b_=b_, k=k, pf=pf: e.matmul(pf[0:8, b_ * NS:(b_ + 1) * NS], lhsT=wGA[:, k, b_ * 8:(b_ + 1) * 8], rhs=hT[:, k, T:TT],
                                                                               start=(k == 0), stop=(k == 7)), reads=[BwGA, BhT[16]], writes=[Bp])
                        op('act', lambda e, pf=pf: e.activation(out=gT[:].rearrange("p b s -> p (b s)"), in_=pf[0:8, 0:3 * NS], func=AF.Sigmoid), reads=[Bp], writes=[BgT])
                        q_st = sbF("q_st", [NS, 4, 128], BF16); Bq_st = Buf("q_st")
                        for g in range(2):
                            op('dve', lambda e, g=g: e.tensor_copy(out=q_st[:, :, g * 64:(g + 1) * 64], in_=q16[:NS, g * 256:(g + 1) * 256].rearrange("p (j d) -> p j d", j=4)),
                               reads=[Bq16], writes=[Bq_st])
                        pt, Bp = next_pt()
                        for j in range(4):
                            op('pe', lambda e, j=j, pt=pt: e.transpose(out=pt[:, j * 128:j * 128 + NS], in_=q_st[:NS, j, :], identity=ident[:NS, :NS]), reads=[Bq_st, Bident], writes=[Bp])
                        Qbd = sbF("Qbd", [128, NS, 8], BF16); BQbd = Buf("Qbd")
                        op('pool', lambda e: e.memset(Qbd[:].rearrange("p s h -> p (s h)"), 0.0), writes=[BQbd])
                        ptv = pt[:, 0:512].rearrange("p (j t) -> p j t", j=4)[:, :, :NS].rearrange("p j s -> p s j")
                        op('dve', lambda e: e.tensor_copy(out=Qbd[0:64, :, 0:4], in_=ptv[0:64]), reads=[Bp, BQbd], writes=[BQbd])
                        op('dve', lambda e: e.tensor_copy(out=Qbd[64:128, :, 4:8], in_=ptv[64:128]), reads=[Bp, BQbd], writes=[BQbd])
                        ovlb = sbF("ovlb", [128, 33], BF16); Bovlb = Buf("ovlb")
                        op('dve', lambda e: e.tensor_copy(out=ovlb[:, :], in_=ovl[:, :]), reads=[Bovl], writes=[Bovlb])
                        dmk, Bdmk = load_const(sbF, "dmk", [NS, NS, 8], C['dmask'][:, :].rearrange("a (s h) -> a s h", h=8), BF16)
                        G8, BG8 = load_const(sbF, "G8", [8, 8], C['G8'][:, :])
                        hm8, Bhm8 = load_const(sbF, "hm8", [8, 2], C['hm8'][:, :])
                        sadd, Bsadd = load_const(sbF, "sadd", [8, 32], C['seladd_s'][:, :])
                        gbuf = sbF("gbuf", [128, 16, 512], BF16); Bgbuf = Buf("gbuf")
                        cwb = sbF("cwb", [128, 4, 256], BF16); Bcwb = Buf("cwb")
                        KT3 = sbF("KT3", [128, 2, T], BF16); BKT3 = [Buf("KT3a"), Buf("KT3b")]
                        KwTs = sbF("KwTs", [128, 512], BF16); BKwTs = Buf("KwTs")
                        kcs = sbF("kcs", [128, 2, HD]); Bkcs = Buf("kcs")
                        kcsb = sbF("kcsb", [128, 128], BF16); Bkcsb = Buf("kcsb")
                        KcTs = sbF("KcTs", [128, 128], BF16); BKcTs = Buf("KcTs")
                        Vcs = sbF("Vcs", [128, 128], BF16); BVcs = Buf("Vcs")
                        Pc8 = sbF("Pc8", [128, 8], BF16); BPc8 = Buf("Pc8")
                        Ps8 = sbF("Ps8", [128, 128], BF16); BPs8 = Buf("Ps8")
                        Pw8 = sbF("Pw8", [128, 32], BF16); BPw8 = Buf("Pw8")
                        Pnf = sbF("Pnf", [NS, 2, 8]); BPnf = Buf("Pnf")
                        Pnb = sbF("Pnb", [NS, 2, 8], BF16); BPnb = Buf("Pnb")
                        Rr = sbF("Rr", [128, 2, 8]); BRr = Buf("Rr")
                        sm = sbF("sm", [8, 128]); Bsm = Buf("sm")
                        sc_s = sbF("sc_s", [8, 32]); Bsc_s = Buf("sc_s")
                        sc_2 = sbF("sc_2", [8, 32]); Bsc_2 = Buf("sc_2")
                        m8s = sbF("m8s", [8, 16]); Bm8s = Buf("m8s")
                        selT = sbF("selT", [96, 8], BF16); BselT = Buf("selT")
                        sc_3 = sbF("sc_3", [8, 96]); Bsc_3 = Buf("sc_3")
                        op('pool', lambda e: e.memset(sc_3[:, :], 0.0), writes=[Bsc_3])
                        ofull = sbF("ofull", [8, 128]); Bofull = Buf("ofull")
                        osamp = sbF("osamp", [8, 2, HD]); Bosamp = Buf("osamp")
                        pSc, BSc = PF[0], BPF[0]; pMi, BMi = PF[1], BPF[1]; pOo, BOo = PF[2], BPF[2]
                        crr = [0]

                        def next_c():
                            crr[0] = (crr[0] + 1) % 3
                            return PF[3 + crr[0]], BPF[3 + crr[0]]
                        cache_rows = cache_nsa
                        for s_i in range(NS):
                            for j in range(16):
                                col = s_i * 16 + j
                                dma('pool', None, None, reads=[Bidx], writes=[Bgbuf], fn=lambda e, j=j, col=col: e.indirect_dma_start(
                                    out=gbuf[:, j, :], out_offset=None, in_=cache_rows[:, :], in_offset=bass.IndirectOffsetOnAxis(ap=idx[:, col:col + 1], axis=0)))
                            dma('pool', cwb[:, :, :], cache_win[s_i].rearrange("(j p) c -> p j c", p=128), writes=[Bcwb])
                            def tr_comp(comp, slot, deint):
                                for half in range(2):
                                    pt, Bp = next_pt()
                                    for jj in range(8):
                                        j = half * 8 + jj
                                        op('pe', lambda e, jj=jj, j=j, pt=pt: e.transpose(out=pt[:, jj * 128:(jj + 1) * 128], in_=gbuf[:, j, comp * 128:(comp + 1) * 128],
                                                                                    identity=ident[:, :]), reads=[Bgbuf, Bident], writes=[Bp])
                                    if deint:
                                        dst = KT3[:, slot, :].rearrange("p (r m) -> p r m", r=16)[:, :, half * 64:(half + 1) * 64].rearrange("p r m -> p m r")
                                        src_ = pt[:, :].rearrange("p (m r) -> p m r", r=16)
                                    else:
                                        dst = KT3[:, slot, half * 1024:(half + 1) * 1024]; src_ = pt[:, :]
                                    if half == 0:
                                        op('act', lambda e: e.copy(out=dst, in_=src_), reads=[Bp], writes=[BKT3[slot]])
                                    else:
                                        op('dve', lambda e: e.tensor_copy(out=dst, in_=src_), reads=[Bp], writes=[BKT3[slot]])
                            tr_comp(0, 0, True); tr_comp(1, 1, True)
                            pt, Bp = next_pt()
                            for j in range(4):
                                op('pe', lambda e, j=j, pt=pt: e.transpose(out=pt[:, j * 128:(j + 1) * 128], in_=cwb[:, j, 0:128], identity=ident[:, :]), reads=[Bcwb, Bident], writes=[Bp])
                            op('dve', lambda e, pt=pt: e.tensor_copy(out=KwTs[:, :], in_=pt[:, 0:512]), reads=[Bp], writes=[BKwTs])
                            for kv_ in range(2):
                                pf, Bp = next_c()
                                for l in range(32):
                                    op('pe', lambda e, l=l, pf=pf, kv_=kv_: e.matmul(pf[0:127, 0:128], lhsT=KT3[:, kv_, (l % 16) * 128 + l // 16:(l % 16) * 128 + l // 16 + 127], rhs=W2[:, kv_, l, :],
                                                                                     start=(l == 0), stop=(l == 31)), reads=[BKT3[kv_], BW2], writes=[Bp])
                                if kv_ == 0:
                                    op('dve', lambda e, pf=pf: e.tensor_tensor(out=kcs[0:127].rearrange("p g d -> p (g d)"), in0=pf[0:127, 0:128], in1=cpeb[0:127, 0, :], op=ALU.add),
                                       reads=[Bp, Bcpeb], writes=[Bkcs])
                                    headnorm(kcs[0:127, :, :], Bkcs, 127, 2, gkc, Bgkc, cs_cmp[0:127, :], Bcsc)
                                    op('act', lambda e: e.copy(out=kcsb[0:127, :], in_=kcs[0:127].rearrange("p g d -> p (g d)")), reads=[Bkcs], writes=[Bkcsb])
                                    pt, Bpp = next_pt()
                                    op('pe', lambda e, pt=pt: e.transpose(out=pt[:, 0:127], in_=kcsb[0:127, :], identity=ident[0:127, 0:127]), reads=[Bkcsb, Bident], writes=[Bpp])
                                    op('dve', lambda e, pt=pt: e.tensor_copy(out=KcTs[:, 0:127], in_=pt[:, 0:127]), reads=[Bpp], writes=[BKcTs])
                                else:
                                    op('dve', lambda e, pf=pf: e.tensor_tensor(out=Vcs[0:127, :], in0=pf[0:127, 0:128], in1=cpeb[0:127, 1, :], op=ALU.add), reads=[Bp, Bcpeb], writes=[BVcs])
                            tr_comp(2, 0, False)
                            qs = Qbd[:, s_i, :]
                            op('pe', lambda e: e.matmul(pSc[0:127, 0:8], lhsT=KcTs[:, 0:127], rhs=qs, start=True, stop=True), reads=[BKcTs, BQbd], writes=[BSc])
                            op('act', lambda e: e.activation(out=Pc8[0:127, :], in_=pSc[0:127, 0:8], func=AF.Exp, scale=0.125), reads=[BSc], writes=[BPc8])
                            op('pe', lambda e: e.matmul(pMi[0:8, 0:33], lhsT=Pc8[0:127, :], rhs=ovlb[0:127, :], start=True, stop=True), reads=[BPc8, Bovlb], writes=[BMi])
                            op('dve', lambda e: e.reciprocal(out=sm[:, 0:1], in_=pMi[0:8, 32:33]), reads=[BMi], writes=[Bsm])
                            op('dve', lambda e: e.tensor_scalar(out=sc_s[:, :], in0=pMi[0:8, 0:32], scalar1=sm[:, 0:1], scalar2=None, op0=ALU.mult), reads=[BMi, Bsm], writes=[Bsc_s])
                            op('pe', lambda e: e.matmul(pMi[0:8, 64:96], lhsT=G8[:, :], rhs=sc_s[:, :], start=True, stop=True), reads=[BG8, Bsc_s], writes=[BMi])
                            op('dve', lambda e: e.tensor_tensor(out=sc_s[:, :], in0=pMi[0:8, 64:96], in1=sadd[:, :], op=ALU.add), reads=[BMi, Bsadd], writes=[Bsc_s])
                            op('dve', lambda e: e.max(out=m8s[:, 0:8], in_=sc_s[:, :]), reads=[Bsc_s], writes=[Bm8s])
                            op('dve', lambda e: e.match_replace(out=sc_2[:, :], in_to_replace=m8s[:, 0:8], in_values=sc_s[:, :], imm_value=-3.0e38), reads=[Bsc_s, Bm8s], writes=[Bsc_2])
                            op('dve', lambda e: e.max(out=m8s[:, 8:16], in_=sc_2[:, :]), reads=[Bsc_2], writes=[Bm8s])
                            op('dve', lambda e: e.tensor_scalar(out=sc_2[:, :], in0=sc_s[:, :], scalar1=m8s[:, 14:15], scalar2=None, op0=ALU.is_ge), reads=[Bsc_s, Bm8s], writes=[Bsc_2])
                            op('dve', lambda e: e.tensor_scalar(out=sc_3[:, 64:96], in0=sc_2[:, :], scalar1=-NEG, scalar2=NEG, op0=ALU.mult, op1=ALU.add), reads=[Bsc_2], writes=[Bsc_3])
                            op('pe', lambda e: e.transpose(out=pMi[0:96, 128:136], in_=sc_3[0:8, :], identity=identf[0:8, 0:8]), reads=[Bsc_3, Bidentf], writes=[BMi])
                            op('dve', lambda e: e.tensor_copy(out=selT[64:96, :], in_=pMi[64:96, 128:136]), reads=[BMi], writes=[BselT])
                            for j in range(16):
                                op('pe', lambda e, j=j: e.matmul(pSc[:, 8 + j * 8:16 + j * 8], lhsT=KT3[:, 0, j * 128:(j + 1) * 128], rhs=qs, start=True, stop=False), reads=[BKT3[0], BQbd], writes=[BSc])
                                op('pe', lambda e, j=j: e.matmul(pSc[:, 8 + j * 8:16 + j * 8], lhsT=KsT[64:96, 0, j * 128:(j + 1) * 128], rhs=selT[64:96, :], start=False, stop=True), reads=[BKsT, BselT], writes=[BSc])
                            for j in range(4):
                                op('pe', lambda e, j=j: e.matmul(pSc[:, 136 + j * 8:144 + j * 8], lhsT=KwTs[:, j * 128:(j + 1) * 128], rhs=qs, start=True, stop=True), reads=[BKwTs, BQbd], writes=[BSc])
                            for b_ in range(2):
                                op('pe', lambda e, b_=b_: e.matmul(pSc[0:NS, 168 + b_ * 8:176 + b_ * 8], lhsT=KnT[:, b_, :], rhs=qs, start=True, stop=False), reads=[BKnT, BQbd], writes=[BSc])
                                op('pe', lambda e, b_=b_: e.matmul(pSc[0:NS, 168 + b_ * 8:176 + b_ * 8], lhsT=ident[0:NS, 0:NS], rhs=dmk[0:NS, s_i, :], start=False, stop=True), reads=[Bident, Bdmk], writes=[BSc])
                            op('act', lambda e: e.activation(out=Ps8[:, :], in_=pSc[:, 8:136], func=AF.Exp, scale=0.125), reads=[BSc], writes=[BPs8])
                            op('act', lambda e: e.activation(out=Pw8[:, :], in_=pSc[:, 136:168], func=AF.Exp, scale=0.125), reads=[BSc], writes=[BPw8])
                            op('act', lambda e: e.activation(out=Pnf[:].rearrange("p b h -> p (b h)"), in_=pSc[0:NS, 168:184], func=AF.Exp, scale=0.125), reads=[BSc], writes=[BPnf])
                            op('dve', lambda e: e.tensor_copy(out=Pnb[:], in_=Pnf[:]), reads=[BPnf], writes=[BPnb])
                            op('pe', lambda e: e.matmul(pOo[0:8, 0:128], lhsT=Pc8[0:127, :], rhs=Vcs[0:127, :], start=True, stop=True), reads=[BPc8, BVcs], writes=[BOo])
                            for j in range(16):
                                op('pe', lambda e, j=j: e.matmul(pOo[0:8, 128:256], lhsT=Ps8[:, j * 8:(j + 1) * 8], rhs=gbuf[:, j, 384:512], start=(j == 0), stop=False), reads=[BPs8, Bgbuf], writes=[BOo])
                            op('pe', lambda e: e.matmul(pOo[0:8, 128:256], lhsT=Pnb[0:NS, 0, :], rhs=Vn[0:NS, 0, :], start=False, stop=True), reads=[BPnb, BVn], writes=[BOo])
                            for j in range(4):
                                op('pe', lambda e, j=j: e.matmul(pOo[0:8, 256:384], lhsT=Pw8[:, j * 8:(j + 1) * 8], rhs=cwb[:, j, 128:256], start=(j == 0), stop=False), reads=[BPw8, Bcwb], writes=[BOo])
                            op('pe', lambda e: e.matmul(pOo[0:8, 256:384], lhsT=Pnb[0:NS, 1, :], rhs=Vn[0:NS, 1, :], start=False, stop=True), reads=[BPnb, BVn], writes=[BOo])
                            op('dve', lambda e: e.tensor_reduce(out=Rr[:, 0, :], in_=Ps8[:, :].rearrange("p (j h) -> p h j", h=8), axis=AX.X, op=ALU.add), reads=[BPs8], writes=[BRr])
                            op('dve', lambda e: e.tensor_reduce(out=Rr[:, 1, :], in_=Pw8[:, :].rearrange("p (j h) -> p h j", h=8), axis=AX.X, op=ALU.add), reads=[BPw8], writes=[BRr])
                            for b_ in range(2):
                                op('pe', lambda e, b_=b_: e.matmul(pMi[0:8, 160 + b_:161 + b_], lhsT=Rr[:, b_, :], rhs=onesf[:, 0:1], start=True, stop=False), reads=[BRr, Bonesf], writes=[BMi])
                                op('pe', lambda e, b_=b_: e.matmul(pMi[0:8, 160 + b_:161 + b_], lhsT=Pnf[0:NS, b_, :], rhs=onesf[0:NS, 0:1], start=False, stop=True), reads=[BPnf, Bonesf], writes=[BMi])
                            op('dve', lambda e: e.reciprocal(out=sm[:, 1:3], in_=pMi[0:8, 160:162]), reads=[BMi], writes=[Bsm])
                            op('dve', lambda e: e.tensor_tensor(out=sm[:, 0:3], in0=sm[:, 0:3], in1=gT[:, :, s_i], op=ALU.mult), reads=[Bsm, BgT], writes=[Bsm])
                            op('dve', lambda e: e.tensor_scalar(out=ofull[:, :], in0=pOo[0:8, 0:128], scalar1=sm[:, 0:1], scalar2=None, op0=ALU.mult), reads=[BOo, Bsm], writes=[Bofull])
                            op('dve', lambda e: e.scalar_tensor_tensor(out=ofull[:, :], in0=pOo[0:8, 128:256], scalar=sm[:, 1:2], in1=ofull[:, :], op0=ALU.mult, op1=ALU.add), reads=[BOo, Bsm, Bofull], writes=[Bofull])
                            op('dve', lambda e: e.scalar_tensor_tensor(out=ofull[:, :], in0=pOo[0:8, 256:384], scalar=sm[:, 2:3], in1=ofull[:, :], op0=ALU.mult, op1=ALU.add), reads=[BOo, Bsm, Bofull], writes=[Bofull])
                            op('dve', lambda e: e.tensor_scalar(out=osamp[:, 0, :], in0=ofull[:, 0:64], scalar1=hm8[:, 0:1], scalar2=None, op0=ALU.mult), reads=[Bofull, Bhm8], writes=[Bosamp])
                            op('dve', lambda e: e.scalar_tensor_tensor(out=osamp[:, 1, :], in0=ofull[:, 64:128], scalar=hm8[:, 1:2], in1=osamp[:, 0, :], op0=ALU.mult, op1=ALU.add),
                               reads=[Bofull, Bhm8, Bosamp], writes=[Bosamp])
                            dma('sp', scr[8192:8192 + 8 * NS * HD].rearrange("(h s d) -> h s d", h=8, s=NS)[:, s_i, :], osamp[:, 1, :], reads=[Bosamp], writes=[Bscr])
                        on16b = q_st[:].rearrange("s j d -> s (j d)"); Bon16b = Bq_st
                        dma('pool', q_st[:].rearrange("s j (a d) -> s (j a) d", a=2), scr[8192:8192 + 8 * NS * HD].rearrange("(h s d) -> s h d", h=8, s=NS), reads=[Bscr], writes=[Bq_st])
                        pt, Bp = next_pt()
                        for c in range(4):
                            op('pe', lambda e, c=c, pt=pt: e.transpose(out=pt[:, c * 128:c * 128 + NS], in_=on16b[:NS, c * 128:(c + 1) * 128], identity=ident[:NS, :NS]), reads=[Bon16b, Bident], writes=[Bp])
                        op('act', lambda e, pt=pt: e.copy(out=onsaT[:, :, T:TT], in_=pt[:, 0:512].rearrange("p (c t) -> p c t", c=4)[:, :, :NS]), reads=[Bp], writes=[BonsaT])
                    kb.barrier(); esF1.close(); sbF = sbF_keep
                    diag4 = sbF("diag4", [128, 4, 128], BF16); Bdiag4 = Buf("diag4")
                    winl4 = sbF("winl4", [128, 4, 128], BF16); Bwinl4 = Buf("winl4")
                    cmpb4 = sbF("cmpb4", [128, 4, 128], BF16); Bcmpb4 = Buf("cmpb4")
                    tmpc = sbF("tmpc", [128, 2, 128]); Btmpc = Buf("tmpc")
                    dma('sp', tmpc[:, 0, :], C['diag'][:, :], writes=[Btmpc])
                    dma('sp', tmpc[:, 1, :], C['winlow'][:, :], writes=[Btmpc])
                    op('dve', lambda e: e.tensor_copy(out=diag4[:], in_=tmpc[:, 0, :].unsqueeze(1).broadcast_to([128, 4, 128])), reads=[Btmpc], writes=[Bdiag4])
                    op('dve', lambda e: e.tensor_copy(out=winl4[:], in_=tmpc[:, 1, :].unsqueeze(1).broadcast_to([128, 4, 128])), reads=[Btmpc], writes=[Bwinl4])
                    cmpbias, Bcmpbias = load_const(sbF, "cmpbias", [128, 16, 128], C['cmpbias'][:, :, :], BF16)
                    seladd, Bseladd = load_const(sbF, "seladd", [128, 16, 32], C['seladd'][:, :, :])
                    PcT = sbF("PcT", [128, 512], BF16); BPcT = Buf("PcT")
                    PwT = [sbF("PwT%d" % j, [128, 512], BF16) for j in range(5)]; BPwT = [Buf("PwT%d" % j) for j in range(5)]
                    PsT = [sbF("PsT%d" % j, [128, 512], BF16) for j in range(16)]; BPsT = [Buf("PsT%d" % j) for j in range(16)]
                    stg = sbF("stg", [128, 128], BF16); Bstg = Buf("stg")
                    op('pool', lambda e: e.memset(stg[:], 0.0), writes=[Bstg])
                    sel = sbF("sel", [128, 160]); Bsel = Buf("sel")
                    sc2 = sbF("sc2", [128, 32]); Bsc2 = Buf("sc2")
                    m8 = sbF("m8", [128, 16]); Bm8 = Buf("m8")
                    rcp = sbF("rcp", [128, 3, 4]); Brcp = Buf("rcp")
                    of = sbF("of", [128, 4, HD]); Bof = Buf("of")
                    ot = sbF("ot", [128, 4, HD]); Bot = Buf("ot")
                    onsa_tok = sbF("onsa_tok", [128, 8, HD], BF16); Bonsa_tok = Buf("onsa_tok")
                    pS = [(PF[0], BPF[0]), (PF[1], BPF[1])]
                    (pOC, BOC), (pOW, BOW), (pOS, BOS), (pC, BC) = (PF[2], BPF[2]), (PF[3], BPF[3]), (PF[4], BPF[4]), (PF[5], BPF[5])
                    srr = [0]

                    def next_s():
                        srr[0] ^= 1
                        return pS[srr[0]]

                    for i in range(16):
                        q0 = i * 128
                        op('dve', lambda e, i=i: e.tensor_copy(out=cmpb4[:], in_=cmpbias[:, i, :].unsqueeze(1).broadcast_to([128, 4, 128])),
                           reads=[Bcmpbias], writes=[Bcmpb4])
                        for g in range(2):
                            qr64 = QT[0:64, 4 * g:4 * g + 4, q0:q0 + 128]
                            qr96 = QT[0:96, 4 * g:4 * g + 4, q0:q0 + 128]
                            op('pe', lambda e: e.matmul(pC[0:127, :], lhsT=KcT[0:64, g, 0:127], rhs=qr64, start=True, stop=False), reads=[BKcT, BQT[i]], writes=[BC])
                            op('pe', lambda e: e.matmul(pC[0:127, :], lhsT=ident[0:127, 0:127], rhs=cmpb4[0:127, :, :], start=False, stop=True),
                               reads=[Bident, Bcmpb4], writes=[BC])
                            op('act', lambda e: e.activation(out=PcT[0:127, :], in_=pC[0:127, :], func=AF.Exp, scale=0.125), reads=[BC], writes=[BPcT])
                            wj = list(range(max(0, i - 4), i + 1))
                            for jj, j in enumerate(wj):
                                ps_, Bs_ = next_s()
                                extra = diag4 if j == i else (winl4 if j == i - 4 else None)
                                Bex = Bdiag4 if j == i else Bwinl4
                                op('pe', lambda e, ps_=ps_, j=j, extra=extra: e.matmul(ps_[:, :], lhsT=KwT[0:64, g, j * 128:(j + 1) * 128], rhs=qr64, start=True, stop=(extra is None)),
                                   reads=[BKwT, BQT[i]], writes=[Bs_])
                                if extra is not None:
                                    op('pe', lambda e, ps_=ps_, extra=extra: e.matmul(ps_[:, :], lhsT=ident[:, :], rhs=extra[:, :, :], start=False, stop=True),
                                       reads=[Bident, Bex], writes=[Bs_])
                                op('act', lambda e, ps_=ps_, jj=jj: e.activation(out=PwT[jj][:, :], in_=ps_[:, :], func=AF.Exp, scale=0.125), reads=[Bs_], writes=[BPwT[jj]])
                            for h in range(4):
                                op('pe', lambda e, h=h: e.matmul(pOC[:, h * 97:(h + 1) * 97], lhsT=PcT[0:127, h * 128:(h + 1) * 128], rhs=Vc[0:127, g, :], start=True, stop=True),
                                   reads=[BPcT, BVc], writes=[BOC])
                            for h in range(4):
                                for jj, j in enumerate(wj):
                                    op('pe', lambda e, h=h, jj=jj, j=j: e.matmul(pOW[:, h * 65:(h + 1) * 65], lhsT=PwT[jj][:, h * 128:(h + 1) * 128], rhs=Vw[:, j, g, :],
                                                                                 start=(jj == 0), stop=(jj == len(wj) - 1)), reads=[BPwT[jj], BVw], writes=[BOW])
                            oc = pOC[:, 0:388].rearrange("p (h c) -> p h c", c=97)
                            op('dve', lambda e: e.tensor_scalar(out=rcp[:, 0, :].unsqueeze(2), in0=oc[:, :, 32:33], scalar1=1e-30, scalar2=None, op0=ALU.max),
                               reads=[BOC], writes=[Brcp])
                            op('dve', lambda e: e.reciprocal(out=rcp[:, 0, :], in_=rcp[:, 0, :]), reads=[Brcp], writes=[Brcp])
                            op('dve', lambda e: e.tensor_tensor(out=sel[:, 0:128].rearrange("p (h b) -> p h b", h=4), in0=oc[:, :, 0:32],
                                                                in1=rcp[:, 0, :].unsqueeze(2).broadcast_to([128, 4, 32]), op=ALU.mult), reads=[BOC, Brcp], writes=[Bsel])
                            op('dve', lambda e: e.tensor_reduce(out=sel[:, 128:160], in_=sel[:, 0:128].rearrange("p (h b) -> p b h", h=4), axis=AX.X, op=ALU.add),
                               reads=[Bsel], writes=[Bsel])
                            op('dve', lambda e, i=i: e.tensor_tensor(out=sel[:, 128:160], in0=sel[:, 128:160], in1=seladd[:, i, :], op=ALU.add), reads=[Bsel, Bseladd], writes=[Bsel])
                            op('dve', lambda e: e.max(out=m8[:, 0:8], in_=sel[:, 128:160]), reads=[Bsel], writes=[Bm8])
                            op('dve', lambda e: e.match_replace(out=sc2[:, :], in_to_replace=m8[:, 0:8], in_values=sel[:, 128:160], imm_value=-3.0e38),
                               reads=[Bsel, Bm8], writes=[Bsc2])
                            op('dve', lambda e: e.max(out=m8[:, 8:16], in_=sc2[:, :]), reads=[Bsc2], writes=[Bm8])
                            op('dve', lambda e: e.tensor_scalar(out=sc2[:, :], in0=sel[:, 128:160], scalar1=m8[:, 15:16], scalar2=None, op0=ALU.is_ge), reads=[Bsel, Bm8], writes=[Bsc2])
                            op('dve', lambda e: e.tensor_scalar(out=stg[:, 64:96], in0=sc2[:, :], scalar1=-NEG, scalar2=NEG, op0=ALU.mult, op1=ALU.add), reads=[Bsc2], writes=[Bstg])
                            pt, Bpp = next_pt()
                            op('pe', lambda e, pt=pt: e.transpose(out=pt[:, 0:128], in_=stg[:, :], identity=ident[:, :]), reads=[Bstg, Bident], writes=[Bpp])
                            op('dve', lambda e, pt=pt: e.tensor_copy(out=QT[64:96, 4 * g:4 * g + 4, q0:q0 + 128], in_=pt[64:96, 0:128].unsqueeze(1).broadcast_to([32, 4, 128])),
                               reads=[Bpp], writes=[BQT[i]])
                            for j in range(i + 1):
                                ps_, Bs_ = next_s()
                                op('pe', lambda e, ps_=ps_, j=j: e.matmul(ps_[:, :], lhsT=KsT[0:96, g, j * 128:(j + 1) * 128], rhs=qr96, start=True, stop=(j != i)),
                                   reads=[BKsT, BQT[i]], writes=[Bs_])
                                if j == i:
                                    op('pe', lambda e, ps_=ps_: e.matmul(ps_[:, :], lhsT=ident[:, :], rhs=diag4[:, :, :], start=False, stop=True), reads=[Bident, Bdiag4], writes=[Bs_])
                                op('act', lambda e, ps_=ps_, j=j: e.activation(out=PsT[j][:, :], in_=ps_[:, :], func=AF.Exp, scale=0.125), reads=[Bs_], writes=[BPsT[j]])
                            for h in range(4):
                                for j in range(i + 1):
                                    op('pe', lambda e, h=h, j=j: e.matmul(pOS[:, h * 65:(h + 1) * 65], lhsT=PsT[j][:, h * 128:(h + 1) * 128], rhs=Vs[:, j, g, :],
                                                                          start=(j == 0), stop=(j == i)), reads=[BPsT[j], BVs], writes=[BOS])
                            osv = pOS[:, 0:260].rearrange("p (h c) -> p h c", c=65); owv = pOW[:, 0:260].rearrange("p (h c) -> p h c", c=65)
                            op('dve', lambda e: e.tensor_scalar(out=rcp[:, 1, :].unsqueeze(2), in0=osv[:, :, 64:65], scalar1=1e-30, scalar2=None, op0=ALU.max), reads=[BOS], writes=[Brcp])
                            op('dve', lambda e: e.tensor_scalar(out=rcp[:, 2, :].unsqueeze(2), in0=owv[:, :, 64:65], scalar1=1e-30, scalar2=None, op0=ALU.max), reads=[BOW], writes=[Brcp])
                            op('dve', lambda e: e.reciprocal(out=rcp[:, 1:3, :], in_=rcp[:, 1:3, :]), reads=[Brcp], writes=[Brcp])
                            op('dve', lambda e, i=i, g=g: e.tensor_tensor(out=rcp[:, :, :], in0=rcp[:, :, :], in1=gates[:, i, :].rearrange("p (b h) -> p b h", b=3)[:, :, 4 * g:4 * g + 4],
                                                                          op=ALU.mult), reads=[Brcp, Bgates], writes=[Brcp])
                            op('dve', lambda e: e.tensor_tensor(out=of[:], in0=oc[:, :, 33:97], in1=rcp[:, 0, :].unsqueeze(2).broadcast_to([128, 4, HD]), op=ALU.mult),
                               reads=[BOC, Brcp], writes=[Bof])
                            op('dve', lambda e: e.tensor_tensor(out=ot[:], in0=osv[:, :, 0:64], in1=rcp[:, 1, :].unsqueeze(2).broadcast_to([128, 4, HD]), op=ALU.mult),
                               reads=[BOS, Brcp], writes=[Bot])
                            op('dve', lambda e: e.tensor_tensor(out=of[:], in0=of[:], in1=ot[:], op=ALU.add), reads=[Bof, Bot], writes=[Bof])
                            op('dve', lambda e: e.tensor_tensor(out=ot[:], in0=owv[:, :, 0:64], in1=rcp[:, 2, :].unsqueeze(2).broadcast_to([128, 4, HD]), op=ALU.mult),
                               reads=[BOW, Brcp], writes=[Bot])
                            op('dve', lambda e, g=g: e.tensor_tensor(out=onsa_tok[:, 4 * g:4 * g + 4, :], in0=of[:], in1=ot[:], op=ALU.add), reads=[Bof, Bot], writes=[Bonsa_tok])
                        pt, Bpp = next_pt()
                        for c in range(4):
                            op('pe', lambda e, c=c, pt=pt: e.transpose(out=pt[:, c * 128:(c + 1) * 128], in_=onsa_tok[:, 2 * c:2 * c + 2, :].rearrange("p h d -> p (h d)"),
                                                                        identity=ident[:, :]), reads=[Bonsa_tok, Bident], writes=[Bpp])
                        op('act', lambda e, pt=pt, q0=q0: e.copy(out=onsaT[:, :, q0:q0 + 128], in_=pt[:, 0:512].rearrange("p (c t) -> p c t", c=4)), reads=[Bpp], writes=[BonsaT])
                    kb.barrier()
            kb.barrier()
        if phases >= 4:
            dbg_dump('omT', omT, BomT)
        if phases >= 6:
            dbg_dump('onsaT', onsaT, BonsaT)
        csT = sb("csT", [128, 4, TT], BF16); BcsT = Buf("csT")
        op('pool', lambda e: e.memset(csT[:, :, T:TT], 0.0), writes=[BcsT])
        if phases >= 5:
            esD = ExitStack()
            with esD:
                sbD = mk_sb(esD)
                wU = sbD("wU", [128, 8, 1024], BF16); BwU = Buf("wU")
                for k in range(8):
                    dma('pool', wU[:, k, :], w_in_v[:, k, O_UC:O_UC + 1024], writes=[BwU])
                cw34 = sbD("cw34", [34, CC]); Bcw34 = Buf("cw34")
                dma('sp', cw34[0:31, :], conv_w[:, :], writes=[Bcw34])
                dma('sp', cw34[31:32, :], conv_b[:, :], writes=[Bcw34])
                dma('sp', cw34[32:33, :], conv_ln_g[:, :], writes=[Bcw34])
                dma('sp', cw34[33:34, :], conv_ln_b[:, :], writes=[Bcw34])
                cwT = sbD("cwT", [128, 4, 34]); BcwT = Buf("cwT")
                pf, Bp = next_pf()
                for c in range(4):
                    op('pe', lambda e, c=c, pf=pf: e.transpose(out=pf[:, c * 34:(c + 1) * 34], in_=cw34[0:34, c * 128:(c + 1) * 128], identity=identf[0:34, 0:34]),
                       reads=[Bcw34, Bidentf], writes=[Bp])
                op('dve', lambda e, pf=pf: e.tensor_copy(out=cwT[:, :, :], in_=pf[:, 0:136].rearrange("p (c w) -> p c w", c=4)), reads=[Bp], writes=[BcwT])
                glu = sbD("glu", [128, 30 + T + NS]); Bglu_ = Buf("glu")
                cT = sbD("cT", [128, 4, TT]); BcT = Buf("cT")
                sg = [sbD("sg%d" % j, [128, 512]) for j in range(2)]; Bsg = [Buf("sg%d" % j) for j in range(2)]
                op('pool', lambda e: e.memset(glu[:, 0:30], 0.0), writes=[Bglu_])
                if phases >= 8:
                    ccs = sbD("ccs", [30, NS, CC]); Bccs = Buf("ccs")
                    dma('sp', ccs[:, :, :], cache_conv.rearrange("s w c -> w s c"), writes=[Bccs])
                    xp = sbD("xp", [128, NS, 31]); Bxp = Buf("xp")
                for c in range(4):
                    for tb, (c0, cn) in enumerate(TB if phases >= 8 else TB[:4]):
                        (pa, Ba), (pb, Bb) = next_pf(), next_pf()
                        for (pp, Bpp, off) in ((pa, Ba, 0), (pb, Bb, 512)):
                            for k in range(8):
                                op('pe', lambda e, k=k, pp=pp, off=off: e.matmul(pp[:, 0:cn], lhsT=wU[:, k, off + c * 128:off + (c + 1) * 128], rhs=hT[:, k, c0:c0 + cn],
                                                                                start=(k == 0), stop=(k == 7)), reads=BhT[c0 // 128:(c0 + cn + 127) // 128] + [BwU], writes=[Bpp])
                        s_ = sg[tb % 2]; Bs_ = Bsg[tb % 2]
                        op('act', lambda e, pb=pb, s_=s_: e.activation(out=s_[:, 0:cn], in_=pb[:, 0:cn], func=AF.Sigmoid), reads=[Bb], writes=[Bs_])
                        op('dve', lambda e, pa=pa, s_=s_: e.tensor_tensor(out=glu[:, 30 + c0:30 + c0 + cn], in0=pa[:, 0:cn], in1=s_[:, 0:cn], op=ALU.mult),
                           reads=[Ba, Bs_], writes=[Bglu_])
                    op('dve', lambda e, c=c: e.tensor_scalar(out=cT[:, c, 0:T], in0=glu[:, 30:30 + T], scalar1=cwT[:, c, 30:31], scalar2=cwT[:, c, 31:32],
                                                             op0=ALU.mult, op1=ALU.add), reads=[Bglu_, BcwT], writes=[BcT])
                    for w in range(30):
                        op('dve', lambda e, c=c, w=w: e.scalar_tensor_tensor(out=cT[:, c, 0:T], in0=glu[:, w:w + T], scalar=cwT[:, c, w:w + 1], in1=cT[:, c, 0:T],
                                                                             op0=ALU.mult, op1=ALU.add), reads=[Bglu_, BcwT, BcT], writes=[BcT])
                    if phases >= 8:
                        pfc, Bpfc = next_pf()
                        for s_i in range(NS):
                            op('pe', lambda e, s_i=s_i, c=c, pfc=pfc: e.transpose(out=pfc[:, s_i * 30:(s_i + 1) * 30], in_=ccs[0:30, s_i, c * 128:(c + 1) * 128], identity=identf[0:30, 0:30]),
                               reads=[Bccs, Bidentf], writes=[Bpfc])
                        op('dve', lambda e, pfc=pfc: e.tensor_copy(out=xp[:, :, 0:30], in_=pfc[:, 0:480].rearrange("p (s w) -> p s w", w=30)), reads=[Bpfc], writes=[Bxp])
                        op('dve', lambda e: e.tensor_copy(out=xp[:, :, 30:31], in_=glu[:, 30 + T:30 + TT].unsqueeze(2)), reads=[Bglu_], writes=[Bxp])
                        op('dve', lambda e, c=c: e.tensor_tensor(out=xp[:, :, :], in0=xp[:, :, :], in1=cwT[:, c, 0:31].unsqueeze(1).broadcast_to([128, NS, 31]), op=ALU.mult),
                           reads=[Bxp, BcwT], writes=[Bxp])
                        op('dve', lambda e, c=c: e.tensor_reduce(out=cT[:, c, T:TT], in_=xp[:, :, :], axis=AX.X, op=ALU.add), reads=[Bxp], writes=[BcT])
                        op('dve', lambda e, c=c: e.tensor_scalar(out=cT[:, c, T:TT], in0=cT[:, c, T:TT], scalar1=cwT[:, c, 31:32], scalar2=None, op0=ALU.add), reads=[BcT, BcwT], writes=[BcT])
                mean = sbD("mean", [128, 512]); Bmean = Buf("mean")
                var = sbD("var", [128, 512]); Bvar = Buf("var")
                for tb, (c0, cn) in enumerate(TB if phases >= 8 else TB[:4]):
                    (p1_, B1_), (p2_, B2_) = next_pf(), next_pf()
                    for c in range(4):
                        op('pe', lambda e, c=c, p1_=p1_: e.matmul(p1_[:, 0:cn], lhsT=onesf[:, :], rhs=cT[:, c, c0:c0 + cn], start=(c == 0), stop=(c == 3)),
                           reads=[BcT, Bonesf], writes=[B1_])
                    kb.pe_force_inc = True
                    for c in range(4):
                        s_ = sg[c % 2]; Bs_ = Bsg[c % 2]
                        op('act', lambda e, c=c, s_=s_: e.activation(out=s_[:, 0:cn], in_=cT[:, c, c0:c0 + cn], func=AF.Square), reads=[BcT], writes=[Bs_])
                        op('pe', lambda e, c=c, p2_=p2_, s_=s_: e.matmul(p2_[:, 0:cn], lhsT=onesf[:, :], rhs=s_[:, 0:cn], start=(c == 0), stop=(c == 3)),
                           reads=[Bs_, Bonesf], writes=[B2_])
                    kb.pe_force_inc = False
                    op('act', lambda e, p1_=p1_: e.activation(out=mean[:, 0:cn], in_=p1_[:, 0:cn], func=AF.Copy, scale=1.0 / CC), reads=[B1_], writes=[Bmean])
                    op('dve', lambda e: e.tensor_tensor(out=var[:, 0:cn], in0=mean[:, 0:cn], in1=mean[:, 0:cn], op=ALU.mult), reads=[Bmean], writes=[Bvar])
                    op('dve', lambda e, p2_=p2_: e.scalar_tensor_tensor(out=var[:, 0:cn], in0=p2_[:, 0:cn], scalar=1.0 / CC, in1=var[:, 0:cn], op0=ALU.mult, op1=ALU.subtract),
                       reads=[B2_, Bvar], writes=[Bvar])
                    op('dve', lambda e: e.tensor_scalar(out=var[:, 0:cn], in0=var[:, 0:cn], scalar1=EPS, scalar2=None, op0=ALU.add), reads=[Bvar], writes=[Bvar])
                    op('act', lambda e: e.activation(out=var[:, 0:cn], in_=var[:, 0:cn], func=AF.Sqrt), reads=[Bvar], writes=[Bvar])
                    op('dve', lambda e: e.reciprocal(out=var[:, 0:cn], in_=var[:, 0:cn]), reads=[Bvar], writes=[Bvar])
                    for c in range(4):
                        s_ = sg[c % 2]; Bs_ = Bsg[c % 2]
                        op('dve', lambda e, c=c, s_=s_: e.tensor_tensor(out=s_[:, 0:cn], in0=cT[:, c, c0:c0 + cn], in1=mean[:, 0:cn], op=ALU.subtract),
                           reads=[BcT, Bmean], writes=[Bs_])
                        op('dve', lambda e, s_=s_: e.tensor_tensor(out=s_[:, 0:cn], in0=s_[:, 0:cn], in1=var[:, 0:cn], op=ALU.mult), reads=[Bs_, Bvar], writes=[Bs_])
                        op('dve', lambda e, c=c, s_=s_: e.tensor_scalar(out=s_[:, 0:cn], in0=s_[:, 0:cn], scalar1=cwT[:, c, 32:33], scalar2=cwT[:, c, 33:34],
                                                                        op0=ALU.mult, op1=ALU.add), reads=[Bs_, BcwT], writes=[Bs_])
                        op('act', lambda e, c=c, s_=s_: e.activation(out=csT[:, c, c0:c0 + cn], in_=s_[:, 0:cn], func=AF.Silu), reads=[Bs_], writes=[BcsT])
                kb.barrier()


        if phases >= 5:
            dbg_dump('csT', csT, BcsT)

        if phases >= 7:
            esG = ExitStack()
            with esG:
                sbG = mk_sb(esG)
                gffn, Bgffn = load_const(sbG, "gffn", [128, D], bc(norm_ffn))
                mT = sbG("mT", [128, 8, TT], BF16); BmT = Buf("mT")
                wON = sbG("wON", [128, 4, D], BF16); BwON = Buf("wON")
                wOC = sbG("wOC", [128, 4, D], BF16); BwOC = Buf("wOC")
                wOM = sbG("wOM", [128, 2, D], BF16); BwOM = Buf("wOM")
                dma('pool', wON[:], w_o_nsa.rearrange("(kc p) c -> p kc c", p=128), writes=[BwON])
                dma('pool', wOC[:], w_o_conv.rearrange("(kc p) c -> p kc c", p=128), writes=[BwOC])
                dma('pool', wOM[:], w_o_mem.rearrange("(kc p) c -> p kc c", p=128), writes=[BwOM])
                wGM = [sbG("wGM%d" % j, [128, 8, 3, 128], BF16) for j in range(2)]; BwGM = [Buf("wGM%d" % j) for j in range(2)]
                sgm = sbG("sgm", [128, 3, 512]); Bsgm = Buf("sgm")
                mt1 = sbG("mt1", [128, 512]); Bmt1 = Buf("mt1")
                mt2 = sbG("mt2", [128, 512]); Bmt2 = Buf("mt2")
                for f in range(8):
                    wg, Bwg = wGM[f % 2], BwGM[f % 2]
                    for j in range(3):
                        dma('pool', wg[:, :, j, :], w_in_v[:, :, O_GM + j * D + f * 128:O_GM + j * D + (f + 1) * 128], writes=[Bwg])
                    for tb, (c0, cn) in enumerate(TB):
                        hb = BhT[c0 // 128:(c0 + cn + 127) // 128]
                        pya, Bya = PF[0], BPF[0]; pyb, Byb = PF[1], BPF[1]; pym, Bym = PF[2], BPF[2]
                        for (pp, Bpp, wt, Bwt, src, Bsrc, nk) in ((pya, Bya, wON, BwON, onsaT, BonsaT, 4), (pyb, Byb, wOC, BwOC, csT, BcsT, 4), (pym, Bym, wOM, BwOM, omT, BomT, 2)):
                            for k in range(nk):
                                op('pe', lambda e, k=k, pp=pp, wt=wt, src=src, nk=nk: e.matmul(pp[:, 0:cn], lhsT=wt[:, k, f * 128:(f + 1) * 128], rhs=src[:, k, c0:c0 + cn],
                                                                                            start=(k == 0), stop=(k == nk - 1)), reads=[Bwt, Bsrc], writes=[Bpp])
                        for j in range(3):
                            pg, Bpg = PF[3 + j], BPF[3 + j]
                            for k in range(8):
                                op('pe', lambda e, k=k, pg=pg, j=j: e.matmul(pg[:, 0:cn], lhsT=wg[:, k, j, :], rhs=hT[:, k, c0:c0 + cn], start=(k == 0), stop=(k == 7)),
                                   reads=hb + [Bwg], writes=[Bpg])
                            op('act', lambda e, pg=pg, j=j: e.activation(out=sgm[:, j, 0:cn], in_=pg[:, 0:cn], func=AF.Sigmoid), reads=[Bpg], writes=[Bsgm])
                        op('dve', lambda e: e.tensor_tensor(out=mt1[:, 0:cn], in0=pya[:, 0:cn], in1=sgm[:, 0, 0:cn], op=ALU.mult), reads=[Bya, Bsgm], writes=[Bmt1])
                        op('dve', lambda e: e.tensor_tensor(out=mt2[:, 0:cn], in0=pyb[:, 0:cn], in1=sgm[:, 1, 0:cn], op=ALU.mult), reads=[Byb, Bsgm], writes=[Bmt2])
                        op('dve', lambda e: e.tensor_tensor(out=mt1[:, 0:cn], in0=mt1[:, 0:cn], in1=mt2[:, 0:cn], op=ALU.add), reads=[Bmt1, Bmt2], writes=[Bmt1])
                        op('dve', lambda e: e.tensor_tensor(out=mt2[:, 0:cn], in0=pym[:, 0:cn], in1=sgm[:, 2, 0:cn], op=ALU.mult), reads=[Bym, Bsgm], writes=[Bmt2])
                        op('dve', lambda e: e.tensor_tensor(out=mT[:, f, c0:c0 + cn], in0=mt1[:, 0:cn], in1=mt2[:, 0:cn], op=ALU.add), reads=[Bmt1, Bmt2], writes=[BmT])
                dbg_dump('mT', mT, BmT)
                wO = sbG("wO", [128, 8, D], BF16); BwO = Buf("wO")
                for k in range(8):
                    dma('pool', wO[:, k, :], w_out[k * 128:(k + 1) * 128, :], writes=[BwO])
                x2t = [sbG("x2t%d" % j, [128, D]) for j in range(2)]; Bx2t = [Buf("x2t%d" % j) for j in range(2)]
                for i in range(NTL):
                    n = tsz(i); t0 = i * 128; sl = i % 2
                    x_t, Bx = xt[sl], Bxt[sl]
                    dma('sp', x_t[:n, :], xs[t0:t0 + n, :], writes=[Bx])
                    for cb in range(2):
                        pf, Bp = next_pf()
                        for k in range(8):
                            op('pe', lambda e, k=k, pf=pf, cb=cb: e.matmul(pf[:n, :], lhsT=mT[:, k, t0:t0 + n], rhs=wO[:, k, cb * 512:(cb + 1) * 512], start=(k == 0), stop=(k == 7)),
                               reads=[BmT, BwO], writes=[Bp])
                        op('dve', lambda e, pf=pf, cb=cb: e.tensor_tensor(out=x2t[sl][:n, cb * 512:(cb + 1) * 512], in0=pf[:n, :], in1=x_t[:n, cb * 512:(cb + 1) * 512], op=ALU.add),
                           reads=[Bp, Bx], writes=[Bx2t[sl]])
                    dma('sp', x2s[t0:t0 + n, :], x2t[sl][:n, :], reads=[Bx2t[sl]], writes=[Bx2s[i]])
                    rms_to_T(None, n, gffn, Bgffn, hT, t0, BhT[i], sl, from_sbuf=(x2t[sl], Bx2t[sl]))
                kb.barrier()

            esU = ExitStack()
            actT = sb("actT", [128, NFC, TT], BF16); BactT = Buf("actT")
            with esU:
                sbU = mk_sb(esU)
                fwT = sbU("fwT", [128, NFC, 4]); BfwT = Buf("fwT")
                cfT = sbU("cfT", [128, NFC, 32]); BcfT = Buf("cfT")
                fc36 = [sbU("fc36_%d" % j, [36, 128]) for j in range(2)]; Bfc36 = [Buf("fc36_%d" % j) for j in range(2)]
                cf_v = cache_ffn.rearrange("s j f -> (s j) f")
                for c in range(NFC):
                    t36, B36 = fc36[c % 2], Bfc36[c % 2]
                    dma('sp', t36[0:3, :], ffn_conv_w[:, c * 128:(c + 1) * 128], writes=[B36])
                    dma('sp', t36[3:4, :], ffn_conv_b[:, c * 128:(c + 1) * 128], writes=[B36])
                    dma('sp', t36[4:36, :], cf_v[:, c * 128:(c + 1) * 128], writes=[B36])
                    pf, Bp = next_pf()
                    op('pe', lambda e, pf=pf: e.transpose(out=pf[:, 0:36], in_=t36[0:36, :], identity=identf[0:36, 0:36]), reads=[B36, Bidentf], writes=[Bp])
                    op('dve', lambda e, c=c, pf=pf: e.tensor_copy(out=fwT[:, c, :], in_=pf[:, 0:4]), reads=[Bp], writes=[BfwT])
                    op('dve', lambda e, c=c, pf=pf: e.tensor_copy(out=cfT[:, c, :], in_=pf[:, 4:36]), reads=[Bp], writes=[BcfT])
                ust = sbU("ust", [128, NFC, 18]); Bust = Buf("ust")
                wup = [sbU("wup%d" % j, [128, 8, 2, 128], BF16) for j in range(2)]; Bwup = [Buf("wup%d" % j) for j in range(2)]
                ub = [sbU("ub0", [128, 2 + T + 48])] * 2; Bub = [Buf("ub0")] * 2
                uc = [sbU("uc%d" % j, [128, 512]) for j in range(2)]; Buc = [Buf("uc%d" % j) for j in range(2)]
                op('pool', lambda e: e.memset(ub[0][:, 0:2], 0.0), writes=[Bub[0]])
                w_up_v = w_ffn_up.rearrange("(kc p) c -> p kc c", p=128)
                for c in range(NFC):
                    wu, Bwu = wup[c % 2], Bwup[c % 2]; u_, Bu_ = ub[c % 2], Bub[c % 2]
                    dma('pool', wu[:, :, 0, :], w_up_v[:, :, c * 128:(c + 1) * 128], writes=[Bwu])
                    dma('pool', wu[:, :, 1, :], w_up_v[:, :, DFF + c * 128:DFF + (c + 1) * 128], writes=[Bwu])
                    usv = u_[:, 2 + T:2 + T + 48].rearrange("p (s j) -> p s j", j=3)
                    op('dve', lambda e, c=c, usv=usv: e.tensor_copy(out=usv[:, :, 0:2], in_=cfT[:, c, :].rearrange("p (s j) -> p s j", j=2)), reads=[BcfT], writes=[Bu_])
                    for tb, (c0, cn) in enumerate(TB):
                        hb = BhT[c0 // 128:(c0 + cn + 127) // 128]
                        (pu, Bpu), (pv, Bpv) = next_pf(), next_pf()
                        for (pp, Bpp, jj) in ((pu, Bpu, 0), (pv, Bpv, 1)):
                            for k in range(8):
                                op('pe', lambda e, k=k, pp=pp, jj=jj: e.matmul(pp[:, 0:cn], lhsT=wu[:, k, jj, :], rhs=hT[:, k, c0:c0 + cn], start=(k == 0), stop=(k == 7)),
                                   reads=hb + [Bwu], writes=[Bpp])
                        if tb < 4:
                            cur = u_[:, 2 + c0:2 + c0 + cn]; m1 = u_[:, 1 + c0:1 + c0 + cn]; m2 = u_[:, c0:c0 + cn]
                        else:
                            cur = usv[:, :, 2]; m1 = usv[:, :, 1]; m2 = usv[:, :, 0]
                        t_, Bt_ = uc[tb % 2], Buc[tb % 2]
                        op('act', lambda e, pu=pu, cur=cur: e.copy(out=cur, in_=pu[:, 0:cn]), reads=[Bpu], writes=[Bu_])
                        op('dve', lambda e, c=c, cur=cur, t_=t_: e.tensor_scalar(out=t_[:, 0:cn], in0=cur, scalar1=fwT[:, c, 2:3], scalar2=fwT[:, c, 3:4], op0=ALU.mult, op1=ALU.add),
                           reads=[Bu_, BfwT], writes=[Bt_])
                        op('dve', lambda e, c=c, m1=m1, t_=t_: e.scalar_tensor_tensor(out=t_[:, 0:cn], in0=m1, scalar=fwT[:, c, 1:2], in1=t_[:, 0:cn], op0=ALU.mult, op1=ALU.add),
                           reads=[Bu_, BfwT, Bt_], writes=[Bt_])
                        op('dve', lambda e, c=c, m2=m2, t_=t_: e.scalar_tensor_tensor(out=t_[:, 0:cn], in0=m2, scalar=fwT[:, c, 0:1], in1=t_[:, 0:cn], op0=ALU.mult, op1=ALU.add),
                           reads=[Bu_, BfwT, Bt_], writes=[Bt_])
                        op('act', lambda e, t_=t_: e.activation(out=t_[:, 0:cn], in_=t_[:, 0:cn], func=AF.Gelu_apprx_tanh), reads=[Bt_], writes=[Bt_])
                        op('dve', lambda e, c=c, pv=pv, t_=t_: e.tensor_tensor(out=actT[:, c, c0:c0 + cn], in0=pv[:, 0:cn], in1=t_[:, 0:cn], op=ALU.mult), reads=[Bpv, Bt_], writes=[BactT])
                    op('dve', lambda e, c=c: e.tensor_copy(out=ust[:, c, 0:2], in_=u_[:, T:T + 2]), reads=[Bu_], writes=[Bust])
                    op('dve', lambda e, c=c, usv=usv: e.tensor_copy(out=ust[:, c, 2:18], in_=usv[:, :, 2]), reads=[Bu_], writes=[Bust])
                fst = [sbU("fst%d" % j, [18, 128]) for j in range(2)]; Bfst = [Buf("fst%d" % j) for j in range(2)]
                for c in range(NFC):
                    pf, Bp = next_pf()
                    op('pe', lambda e, c=c, pf=pf: e.transpose(out=pf[0:18, 0:128], in_=ust[:, c, :], identity=identf[:, :]), reads=[Bust, Bidentf], writes=[Bp])
                    op('act', lambda e, c=c, pf=pf: e.copy(out=fst[c % 2][0:18, :], in_=pf[0:18, 0:128]), reads=[Bp], writes=[Bfst[c % 2]])
                    dma('sp', o_ffnp[:, c * 128:(c + 1) * 128], fst[c % 2][0:2, :], reads=[Bfst[c % 2]], is_output=True)
                    dma('sp', o_ffns[:, 1, c * 128:(c + 1) * 128], fst[c % 2][2:18, :], reads=[Bfst[c % 2]], is_output=True)
                kb.barrier()

            esW = ExitStack()
            with esW:
                sbW = mk_sb(esW)
                wd_extra = sbW("wd_extra", [128, 2, D], BF16)
                cs_flat = csT[:].rearrange("p a b -> p (a b)")[:, 0:8 * D].rearrange("p (c n) -> p c n", c=8)
                on_flat = onsaT[:].rearrange("p a b -> p (a b)")[:, 0:8 * D].rearrange("p (c n) -> p c n", c=8)
                om_flat = omT[:].rearrange("p a b -> p (a b)")[:, 0:4 * D].rearrange("p (c n) -> p c n", c=4)
                Bwd = Buf("wdn")

                def wd(c):
                    if c < 8:
                        return cs_flat[:, c, :]
                    if c < 16:
                        return on_flat[:, c - 8, :]
                    if c < 20:
                        return om_flat[:, c - 16, :]
                    return wd_extra[:, c - 20, :]
                for c in range(NFC):
                    dma('pool', wd(c), w_ffn_down[c * 128:(c + 1) * 128, :], writes=[Bwd])
                yt = [sbW("yt%d" % j, [128, D]) for j in range(2)]; Byt = [Buf("yt%d" % j) for j in range(2)]
                for i in range(NTL):
                    n = tsz(i); t0 = i * 128; sl = i % 2
                    x_t, Bx = xt[sl], Bxt[sl]
                    dma('sp', x_t[:n, :], x2s[t0:t0 + n, :], reads=[Bx2s[i]], writes=[Bx])
                    for cb in range(2):
                        pf, Bp = next_pf()
                        for c in range(NFC):
                            op('pe', lambda e, c=c, pf=pf, cb=cb: e.matmul(pf[:n, :], lhsT=actT[:, c, t0:t0 + n], rhs=wd(c)[:, cb * 512:(cb + 1) * 512], start=(c == 0), stop=(c == NFC - 1)),
                               reads=[BactT, Bwd], writes=[Bp])
                        op('dve', lambda e, pf=pf, cb=cb: e.tensor_tensor(out=yt[sl][:n, cb * 512:(cb + 1) * 512], in0=pf[:n, :], in1=x_t[:n, cb * 512:(cb + 1) * 512], op=ALU.add),
                           reads=[Bp, Bx], writes=[Byt[sl]])
                    dma('sp', o_y[t0:t0 + n, :], yt[sl][:n, :], reads=[Byt[sl]], is_output=True)

        kb.finish()
    return nc


_NC_CACHE = {}


def kernel(**inputs):
    phases = PHASES
    f32 = np.float32
    consts = _consts()
    if phases not in _NC_CACHE:
        _NC_CACHE[phases] = build(phases)
    nc = _NC_CACHE[phases]
    cn = np.ascontiguousarray(inputs['cache_nsa'][0].reshape(2560 * 128, 512))
    shared = {}
    for k in ['norm_attn', 'w_in', 'q_norm', 'w_o_nsa', 'conv_w', 'conv_b', 'conv_ln_g', 'conv_ln_b', 'w_o_conv', 'norm_mem',
              'w_mem_kv', 'mq_norm', 'mk_norm', 'w_o_mem', 'w_out', 'norm_ffn', 'w_ffn_up', 'ffn_conv_w', 'ffn_conv_b', 'w_ffn_down']:
        a = np.asarray(inputs[k])[0]
        if a.ndim == 1:
            a = a[None, :]
        shared[k] = np.ascontiguousarray(a, dtype=f32)
    shared['k_norm'] = np.ascontiguousarray(inputs['k_norm'][0], dtype=f32)
    shared['cmp_pe'] = np.ascontiguousarray(inputs['cmp_pe'][0], dtype=f32)
    shared['w_cmp'] = np.ascontiguousarray(inputs['w_cmp'][0], dtype=f32)
    shared['cache_nsa'] = cn if phases >= 9 else cn[:128]
    for k, v in consts.items():
        shared['c_' + k] = v
    in_maps = []
    for c in range(8):
        m = dict(shared)
        s0 = c * NS
        m['xs'] = np.ascontiguousarray(np.concatenate([inputs['x_prompt'][c], inputs['x_sample'][s0:s0 + NS, 0]], axis=0), dtype=f32)
        m['mem'] = np.ascontiguousarray(inputs['mem_prompt'][c], dtype=f32)
        m['cache_win'] = np.ascontiguousarray(inputs['cache_win'][0, s0:s0 + NS].reshape(NS, 512, 256))
        m['cache_conv'] = np.ascontiguousarray(inputs['cache_conv'][0, s0:s0 + NS])
        m['cache_ffn'] = np.ascontiguousarray(inputs['cache_ffn'][0, s0:s0 + NS])
        m['cache_mem'] = np.ascontiguousarray(inputs['cache_mem'][0, s0:s0 + NS].reshape(NS, 256, 512))
        m['ptab'] = np.ascontiguousarray(inputs['page_table'][s0:s0 + NS].reshape(1, 256).astype(np.int32))
        in_maps.append(m)
    res = run_bass_kernel_spmd(nc, in_maps, core_ids=list(range(8)))
    R = res.results

    def cat(name):
        return np.stack([np.asarray(r[name]) for r in R], axis=0)
    y = cat('o_y'); rows = cat('o_rows')
    y_p = y[:, :T]; y_s = y[:, T:].reshape(128, 1, D)
    rows_p = rows[:, :T].reshape(1, 8, T, 4, 2, 64); rows_s = rows[:, T:].reshape(1, 128, 1, 4, 2, 64)
    win_p = cat('o_winp').reshape(1, 8, 512, 2, 2, 64)
    win_s = cat('o_wins').reshape(1, 128, 512, 2, 2, 64)
    conv_p = cat('o_convp').reshape(1, 8, 30, CC); conv_s = cat('o_convs').reshape(1, 128, 30, CC)
    ffn_p = cat('o_ffnp').reshape(1, 8, 2, DFF); ffn_s = cat('o_ffns').reshape(1, 128, 2, DFF)
    memkv = cat('o_memkv').reshape(1, 8, 256, 2, 4, 64)
    return tuple(np.ascontiguousarray(a, dtype=f32) for a in
                 (y_p, y_s, rows_p, rows_s, win_p, win_s, conv_p, conv_s, ffn_p, ffn_s, memkv))
```

```python
import numpy as np
from contextlib import ExitStack
import concourse.bass as bass
import concourse.mybir as mybir
from concourse.bass_utils import run_bass_kernel_spmd

F32 = mybir.dt.float32; BF16 = mybir.dt.bfloat16; I32 = mybir.dt.int32
AF = mybir.ActivationFunctionType; ALU = mybir.AluOpType; AX = mybir.AxisListType

D = 1024; T = 2048; NS = 16; TT = T + NS; NTL = 17
HD = 64; NH = 8; NKV = 2
DFF = 2816; NFC = 22
CC = 512
EPS = 1e-6
NEG = -30000.0
IN_COLS = 5656
O_KV = 512; O_GA = 1280; O_UC = 1304; O_QM = 2328; O_GM = 2584
PHASES = 9
DBG = False
NPHYS = 2560
TILES = None
STOP = -1


class Buf:
    __slots__ = ('name', 'w', 'r', 'rd')

    def __init__(self, name):
        self.name = name; self.w = None; self.r = {}; self.rd = []


class _PEProxy:
    def __init__(self, eng):
        self.eng = eng; self.last = True

    def matmul(self, *a, **kw):
        self.last = bool(kw.get('stop', True))
        return self.eng.matmul(*a, **kw)

    def transpose(self, *a, **kw):
        self.last = True
        return self.eng.transpose(*a, **kw)


class KB:
    def __init__(self, nc, es, n_dsem=20):
        self.nc = nc; self.es = es
        self.E = dict(pe=nc.tensor, dve=nc.vector, act=nc.scalar, pool=nc.gpsimd, sp=nc.sync)
        self.sem = {e: es.enter_context(nc.semaphore('s_' + e)) for e in ['pe', 'dve', 'act', 'pool']}
        self.cnt = {e: 0 for e in self.sem}
        self.waited = {}
        self.dsem = {}; self.dval = {}; self.dnext = {}
        for q in ['sp', 'pool']:
            self.dsem[q] = [es.enter_context(nc.semaphore('d_%s%d' % (q, i))) for i in range(n_dsem)]
            self.dval[q] = [0] * n_dsem; self.dnext[q] = 0
        self.out_events = []

    def _wait(self, eng, key, sem, val):
        if self.waited.get((eng, key), 0) >= val:
            return
        self.E[eng].wait_ge(sem, val); self.waited[(eng, key)] = val

    def _dep(self, eng, ev, raw):
        kind, key, val = ev
        if kind == 'e':
            if key == eng and eng == 'pe':
                return
            self._wait(eng, key, self.sem[key], val)
        else:
            q, i = key
            self._wait(eng, key, self.dsem[q][i], val)

    def deps(self, eng, reads, writes):
        for b in reads:
            if b.w is not None:
                self._dep(eng, b.w, True)
        for b in writes:
            if b.w is not None:
                self._dep(eng, b.w, False)
            for e2, c in b.r.items():
                self._dep(eng, ('e', e2, c), False)
            for ev in b.rd:
                self._dep(eng, ev, False)

    def _record(self, ev, reads, writes):
        for b in reads:
            if ev[0] == 'e':
                b.r[ev[1]] = ev[2]
            else:
                b.rd.append(ev)
        for b in writes:
            b.w = ev; b.r = {}; b.rd = []

    def op(self, eng, fn, reads=(), writes=()):
        self.deps(eng, reads, writes)
        if eng == 'pe':
            px = _PEProxy(self.E[eng])
            ins = fn(px)
            if not px.last and not getattr(self, 'pe_force_inc', False):
                self._record(('e', eng, self.cnt[eng] + 1), reads, writes)
                return
        else:
            ins = fn(self.E[eng])
        self.cnt[eng] += 1
        ins.then_inc(self.sem[eng], 1)
        self._record(('e', eng, self.cnt[eng]), reads, writes)

    def dma(self, q, out, in_, reads=(), writes=(), is_output=False, fn=None, **kw):
        self.deps(q, reads, writes)
        i = self.dnext[q]; self.dnext[q] = (i + 1) % len(self.dsem[q])
        self._wait(q, (q, i), self.dsem[q][i], self.dval[q][i])
        self.dval[q][i] += 16
        if fn is None:
            ins = self.E[q].dma_start(out=out, in_=in_, **kw)
        else:
            ins = fn(self.E[q])
        ins.then_inc(self.dsem[q][i], 16)
        ev = ('d', (q, i), self.dval[q][i])
        self._record(ev, reads, writes)
        if is_output:
            self.out_events.append(ev)

    def barrier(self):
        for eng in ['pe', 'dve', 'act', 'pool', 'sp']:
            for e2 in self.sem:
                if e2 != eng and self.cnt[e2] > 0:
                    self._wait(eng, e2, self.sem[e2], self.cnt[e2])
            for q in self.dsem:
                for i, v in enumerate(self.dval[q]):
                    if v > 0:
                        self._wait(eng, (q, i), self.dsem[q][i], v)

    def finish(self, eng='sp'):
        last = {}
        for kind, key, val in self.out_events:
            last[key] = max(last.get(key, 0), val)
        for key, val in last.items():
            q, i = key
            self.E[eng].wait_ge(self.dsem[q][i], val)


def _consts():
    c = {}
    c['ident'] = np.eye(128, dtype=np.float32)
    inv = (np.float32(500000.0) ** (-np.arange(0, 16, 2, dtype=np.float32) / np.float32(16))).astype(np.float32)

    def cs(pos):
        ang = pos.astype(np.float32)[:, None] * inv[None, :]
        return np.concatenate([np.cos(ang), np.sin(ang)], axis=1).astype(np.float32)
    pos = np.concatenate([np.arange(T), np.full(NS, T)]).astype(np.float32)
    csp = np.zeros((NTL * 128, 16), np.float32); csp[:TT] = cs(pos)
    c['cs_tok'] = np.ascontiguousarray(csp.reshape(NTL, 128, 16).transpose(1, 0, 2))
    cc = np.zeros((128, 16), np.float32); cc[:127] = cs(np.arange(127) * 16 + 31.0)
    c['cs_cmp'] = cc
    p = np.arange(128)
    c['diag'] = np.where(p[:, None] <= p[None, :], 0.0, NEG).astype(np.float32)
    c['winlow'] = np.where(p[:, None] >= p[None, :], 0.0, NEG).astype(np.float32)
    n = np.arange(128)[None, :, None]; i = np.arange(16)[:, None, None]; q = np.arange(128)[None, None, :]
    cb = np.where(16 * n + 31 <= 128 * i + q, 0.0, NEG).astype(np.float32)
    c['cmpbias'] = np.ascontiguousarray(cb.transpose(1, 0, 2))
    t = np.arange(T); qb = t // 64; b = np.arange(32)
    forced = (b[None, :] == 0) | (b[None, :] == qb[:, None]) | (b[None, :] == qb[:, None] - 1)
    valid = b[None, :] <= qb[:, None]
    sa = np.where(forced, 1e9, np.where(valid, 0.0, -1e9)).astype(np.float32)
    c['seladd'] = np.ascontiguousarray(sa.reshape(16, 128, 32).transpose(1, 0, 2))
    sas = np.zeros((8, 32), np.float32); sas[:, 0] = 1e9; sas[:, 31] = 1e9
    c['seladd_s'] = sas
    c['E'] = (t[None, :] // 64 == b[:, None]).astype(np.float32)
    cs_ = np.arange(127) * 16
    ov = ((cs_[:, None] < b[None, :] * 64 + 64) & (cs_[:, None] + 32 > b[None, :] * 64)).astype(np.float32)
    ovp = np.zeros((128, 33), np.float32); ovp[:127, :32] = ov; ovp[:127, 32] = 1.0
    c['ovl'] = ovp
    G = (np.arange(8)[:, None] // 4 == np.arange(8)[None, :] // 4).astype(np.float32)
    c['G8'] = G
    dm = np.full((16, 16, 8), NEG, np.float32)
    for s in range(16):
        dm[s, s, :] = 0.0
    c['dmask'] = dm.reshape(16, 128)
    c['hm8'] = np.array([[1.0, 0.0]] * 4 + [[0.0, 1.0]] * 4, dtype=np.float32)
    return c


def build(phases=PHASES):
    nc = bass.Bass("TRN2", target_bir_lowering=False)

    def din(name, shape, dt=F32):
        return nc.dram_tensor(name, list(shape), dt, kind="ExternalInput").ap()

    def dout(name, shape):
        return nc.dram_tensor(name, list(shape), F32, kind="ExternalOutput").ap()

    xs = din("xs", [TT, D]); mem = din("mem", [256, D])
    cache_nsa = din("cache_nsa", [NPHYS * 128 if phases >= 9 else 128, 512])
    cache_win = din("cache_win", [NS, 512, 256])
    cache_conv = din("cache_conv", [NS, 30, CC])
    cache_ffn = din("cache_ffn", [NS, 2, DFF])
    cache_mem = din("cache_mem", [NS, 256, 512])
    ptab = din("ptab", [1, 256], I32)
    norm_attn = din("norm_attn", [1, D]); w_in = din("w_in", [D, IN_COLS])
    q_norm = din("q_norm", [1, HD]); k_norm = din("k_norm", [3, HD])
    cmp_pe = din("cmp_pe", [2, 32, HD]); w_cmp = din("w_cmp", [2, 32, HD, HD])
    w_o_nsa = din("w_o_nsa", [512, D])
    conv_w = din("conv_w", [31, CC]); conv_b = din("conv_b", [1, CC])
    conv_ln_g = din("conv_ln_g", [1, CC]); conv_ln_b = din("conv_ln_b", [1, CC])
    w_o_conv = din("w_o_conv", [CC, D])
    norm_mem = din("norm_mem", [1, D]); w_mem_kv = din("w_mem_kv", [D, 512])
    mq_norm = din("mq_norm", [1, HD]); mk_norm = din("mk_norm", [1, HD])
    w_o_mem = din("w_o_mem", [256, D]); w_out = din("w_out", [D, D])
    norm_ffn = din("norm_ffn", [1, D]); w_ffn_up = din("w_ffn_up", [D, 2 * DFF])
    ffn_conv_w = din("ffn_conv_w", [3, DFF]); ffn_conv_b = din("ffn_conv_b", [1, DFF])
    w_ffn_down = din("w_ffn_down", [DFF, D])
    C = {k: din("c_" + k, v.shape) for k, v in _consts().items()}

    o_y = dout("o_y", [TT, D]); o_rows = dout("o_rows", [TT, 512])
    o_winp = dout("o_winp", [512, 256]); o_wins = dout("o_wins", [NS, 512, 256])
    o_convp = dout("o_convp", [30, CC]); o_convs = dout("o_convs", [NS, 30, CC])
    o_ffnp = dout("o_ffnp", [2, DFF]); o_ffns = dout("o_ffns", [NS, 2, DFF])
    o_memkv = dout("o_memkv", [256, 512])
    x2s = nc.dram_tensor("x2s", [TT, D], F32, kind="Internal").ap()
    Bx2s = [Buf("x2s%d" % i) for i in range(NTL)]
    scr = nc.dram_tensor("scr", [65536], F32, kind="Internal").ap(); Bscr = Buf("scr")
    hts = nc.dram_tensor("hts", [128, 8 * TT], BF16, kind="Internal").ap(); Bhts = Buf("hts")
    dbg = {}
    if DBG:
        for nm, k in [('csT', 4), ('omT', 2), ('onsaT', 4), ('mT', 8)]:
            dbg[nm] = dout("d_" + nm, [128, k, TT])

    w_in_v = w_in.rearrange("(kc p) c -> p kc c", p=128)

    def bc(ap1row):
        return ap1row.partition_broadcast(128).rearrange("p a d -> p (a d)")

    es = ExitStack()
    with es:
        kb = KB(nc, es)
        op = kb.op; dma = kb.dma

        def mk_sb(stack):
            def f(name, shape, dt=F32):
                return stack.enter_context(nc.sbuf_tensor(name, list(shape), dt))
            return f
        sb = mk_sb(es)

        def tsz(i):
            return 128 if i < 16 else NS
        TB = [(j * 512, 512) for j in range(4)] + [(T, NS)]

        PF = [es.enter_context(nc.psum_tensor("pf%d" % i, [128, 512], F32)) for i in range(6)]
        BPF = [Buf("pf%d" % i) for i in range(6)]
        PT = [es.enter_context(nc.psum_tensor("pt%d" % i, [128, 1024], BF16)) for i in range(2)]
        BPT = [Buf("pt%d" % i) for i in range(2)]
        rr = {'pf': 0, 'pt': 0}

        def next_pf():
            i = rr['pf']; rr['pf'] = (i + 1) % 6
            return PF[i], BPF[i]

        def next_pt():
            i = rr['pt']; rr['pt'] = (i + 1) % 2
            return PT[i], BPT[i]

        def load_const(sbf, name, shape, src, dt=F32):
            t = sbf(name, shape, dt); b = Buf(name)
            dma('pool' if dt != F32 else 'sp', t[:], src, writes=[b])
            return t, b

        ident, Bident = load_const(sb, "ident", [128, 128], C['ident'][:, :], BF16)
        identf, Bidentf = load_const(sb, "identf", [128, 128], C['ident'][:, :], F32)
        mhalf = sb("mhalf", [128, 512]); Bmhalf = Buf("mhalf")
        op('pool', lambda e: e.memset(mhalf[:], -0.5), writes=[Bmhalf])
        onesf = sb("onesf", [128, 128]); Bonesf = Buf("onesf")
        op('pool', lambda e: e.memset(onesf[:], 1.0), writes=[Bonesf])

        hT = sb("hT", [128, 8, TT], BF16); BhT = [Buf("hT%d" % i) for i in range(NTL)]
        omT = sb("omT", [128, 2, TT], BF16); BomT = Buf("omT")
        onsaT = sb("onsaT", [128, 4, TT], BF16); BonsaT = Buf("onsaT")
        for t_, b_ in ((omT, BomT), (onsaT, BonsaT)):
            op('pool', lambda e, t_=t_: e.memset(t_[:, :, T:TT], 0.0), writes=[b_])

        st = sb("st", [128, 16]); Bst = Buf("st")
        nsq = sb("nsq", [128, 8, HD]); Bnsq = Buf("nsq")
        nst = sb("nst", [128, 16]); Bnst = Buf("nst")
        rt = sb("rt", [128, 4, 8, 8]); Brt = Buf("rt")
        xt = [sb("xt%d" % i, [128, D]) for i in range(2)]; Bxt = [Buf("xt%d" % i) for i in range(2)]
        xn = [sb("xn0", [128, D], BF16)] * 2; Bxn = [Buf("xn0")] * 2

        def rms_to_T(src_rows, n, gain_t, Bgain, dstT, col0, Bdst, slot, from_sbuf=None):
            if from_sbuf is None:
                x_t, Bx = xt[slot], Bxt[slot]
                dma('sp', x_t[:n, :], src_rows, writes=[Bx])
            else:
                x_t, Bx = from_sbuf
            xn_t, Bn = xn[slot], Bxn[slot]
            op('act', lambda e: e.activation(out=xn_t[:n, :], in_=x_t[:n, :], func=AF.Square, accum_out=st[:n, 0:1]),
               reads=[Bx], writes=[Bn, Bst])
            op('dve', lambda e: e.tensor_scalar(out=st[:n, 0:1], in0=st[:n, 0:1], scalar1=1.0 / D, scalar2=EPS, op0=ALU.mult, op1=ALU.add),
               reads=[Bst], writes=[Bst])
            op('pool', lambda e: e.tensor_tensor(out=st[:n, 1:2], in0=st[:n, 0:1], in1=mhalf[:n, 0:1], op=ALU.pow),
               reads=[Bst, Bmhalf], writes=[Bst])
            op('dve', lambda e: e.scalar_tensor_tensor(out=xn_t[:n, :], in0=x_t[:n, :], scalar=st[:n, 1:2], in1=gain_t[:n, :],
                                                       op0=ALU.mult, op1=ALU.mult), reads=[Bx, Bst, Bgain], writes=[Bn])
            pt, Bp = next_pt()
            for k in range(8):
                op('pe', lambda e, k=k: e.transpose(out=pt[:, k * 128:k * 128 + n], in_=xn_t[:n, k * 128:(k + 1) * 128], identity=ident[:n, :n]),
                   reads=[Bn, Bident], writes=[Bp])
            op('act', lambda e: e.copy(out=dstT[:, :, col0:col0 + n], in_=pt[:, :].rearrange("p (k t) -> p k t", k=8)[:, :, :n]),
               reads=[Bp], writes=[Bdst])

        def headnorm(src, Bsrc, n, H, gain, Bgain, cs_ap=None, Bcs=None):
            op('dve', lambda e: e.tensor_tensor(out=nsq[:n, :H, :], in0=src, in1=src, op=ALU.mult), reads=[Bsrc], writes=[Bnsq])
            op('dve', lambda e: e.tensor_reduce(out=nst[:n, 0:H], in_=nsq[:n, :H, :], axis=AX.X, op=ALU.add), reads=[Bnsq], writes=[Bnst])
            op('dve', lambda e: e.tensor_scalar(out=nst[:n, 0:H], in0=nst[:n, 0:H], scalar1=1.0 / HD, scalar2=EPS, op0=ALU.mult, op1=ALU.add),
               reads=[Bnst], writes=[Bnst])
            op('pool', lambda e: e.tensor_tensor(out=nst[:n, 8:8 + H], in0=nst[:n, 0:H], in1=mhalf[:n, 0:H], op=ALU.pow),
               reads=[Bnst, Bmhalf], writes=[Bnst])
            op('dve', lambda e: e.tensor_tensor(out=src, in0=src, in1=nst[:n, 8:8 + H].unsqueeze(2).broadcast_to([n, H, HD]), op=ALU.mult),
               reads=[Bsrc, Bnst], writes=[Bsrc])
            op('dve', lambda e: e.tensor_tensor(out=src, in0=src, in1=gain[:n, :H, :], op=ALU.mult), reads=[Bsrc, Bgain], writes=[Bsrc])
            if cs_ap is not None:
                a = src[:, :, 0:8]; b = src[:, :, 8:16]
                cos = cs_ap[:, 0:8].unsqueeze(1).broadcast_to([n, H, 8]); sin = cs_ap[:, 8:16].unsqueeze(1).broadcast_to([n, H, 8])
                for j, (u, v) in enumerate([(a, cos), (b, sin), (a, sin), (b, cos)]):
                    op('dve', lambda e, j=j, u=u, v=v: e.tensor_tensor(out=rt[:n, j, :H, :], in0=u, in1=v, op=ALU.mult),
                       reads=[Bsrc, Bcs], writes=[Brt])
                op('dve', lambda e: e.tensor_tensor(out=a, in0=rt[:n, 0, :H, :], in1=rt[:n, 1, :H, :], op=ALU.subtract), reads=[Brt], writes=[Bsrc])
                op('dve', lambda e: e.tensor_tensor(out=b, in0=rt[:n, 2, :H, :], in1=rt[:n, 3, :H, :], op=ALU.add), reads=[Brt], writes=[Bsrc])

        def dbg_dump(name, t, B):
            if DBG:
                for k in range(t.shape[1]):
                    dma('pool', dbg[name][:, k, :], t[:, k, :], reads=[B], is_output=True)

        esT = ExitStack()
        with esT:
            sbT = mk_sb(esT)
            cs_tok, Bcs = load_const(sbT, "cs_tok", [128, NTL, 16], C['cs_tok'][:, :, :])
            cs_cmp, Bcsc = load_const(sbT, "cs_cmp", [128, 16], C['cs_cmp'][:, :])
            gq = sbT("gq", [128, 8, HD]); Bgq = Buf("gq")
            for h in range(8):
                dma('sp', gq[:, h, :], bc(q_norm), writes=[Bgq])
            gk = sbT("gk", [128, 4, HD]); Bgk = Buf("gk")
            for j in range(4):
                dma('sp', gk[:, j, :], bc(k_norm[1 + j // 2:2 + j // 2, :]), writes=[Bgk])
            gkc = sbT("gkc", [128, 2, HD]); Bgkc = Buf("gkc")
            for j in range(2):
                dma('sp', gkc[:, j, :], bc(k_norm[0:1, :]), writes=[Bgkc])
            gmq = sbT("gmq", [128, 4, HD]); Bgmq = Buf("gmq")
            gmk = sbT("gmk", [128, 4, HD]); Bgmk = Buf("gmk")
            for j in range(4):
                dma('sp', gmq[:, j, :], bc(mq_norm), writes=[Bgmq])
                dma('sp', gmk[:, j, :], bc(mk_norm), writes=[Bgmk])

            QT = sbT("QT", [96, 8, TT], BF16); BQT = [Buf("QT%d" % i) for i in range(NTL)]
            KsT = sbT("KsT", [96, 2, T], BF16); BKsT = Buf("KsT")
            KwT = sbT("KwT", [64, 2, T], BF16); BKwT = Buf("KwT")
            KcTr = sbT("KcTr", [128, T], BF16); BKcTr = Buf("KcTr")
            VcTr = sbT("VcTr", [128, T], BF16); BVcTr = Buf("VcTr")
            Vs = sbT("Vs", [128, 16, 2, 65], BF16); BVs = Buf("Vs")
            Vw = sbT("Vw", [128, 16, 2, 65], BF16); BVw = Buf("Vw")
            gates = sbT("gates", [128, NTL, 24]); Bgates = Buf("gates")
            KnT = sbT("KnT", [128, 2, NS], BF16); BKnT = Buf("KnT")
            Vn = sbT("Vn", [NS, 2, 128], BF16); BVn = Buf("Vn")
            q16 = sbT("q16", [NS, 512], BF16); Bq16 = Buf("q16")
            qm16 = sbT("qm16", [NS, 256], BF16); Bqm16 = Buf("qm16")
            KmT = sbT("KmT", [64, 4, 256], BF16); BKmT = Buf("KmT")
            Vm = sbT("Vm", [128, 2, 4, 65], BF16); BVm = Buf("Vm")
            op('pool', lambda e: e.memset(Vs[:, :, :, 64:65], 1.0), writes=[BVs])
            op('pool', lambda e: e.memset(Vw[:, :, :, 64:65], 1.0), writes=[BVw])
            op('pool', lambda e: e.memset(Vm[:, :, :, 64:65], 1.0), writes=[BVm])
            for g in range(2):
                dma('pool', KsT[64:96, g, :], C['E'][:, :], writes=[BKsT])

            esB = ExitStack()
            with esB:
                sbB = mk_sb(esB)
                gmem, Bgmem = load_const(sbB, "gmem", [128, D], bc(norm_mem))
                wMK = sbB("wMK", [128, 8, 512], BF16); BwMK = Buf("wMK")
                dma('pool', wMK[:], w_mem_kv.rearrange("(kc p) c -> p kc c", p=128), writes=[BwMK])
                mT_ = sbB("mT_", [128, 8, 256], BF16); BmT_ = Buf("mT_")
                mk_t = sbB("mk_t", [128, 512]); Bmk = Buf("mk_t")
                mkb = sbB("mkb", [128, 256], BF16); Bmkb = Buf("mkb")
                for i in range(2):
                    rms_to_T(mem[i * 128:(i + 1) * 128, :], 128, gmem, Bgmem, mT_, i * 128, BmT_, i)
                    pf, Bp = next_pf()
                    for k in range(8):
                        op('pe', lambda e, k=k, pf=pf: e.matmul(pf[:, :], lhsT=mT_[:, k, i * 128:(i + 1) * 128], rhs=wMK[:, k, :], start=(k == 0), stop=(k == 7)),
                           reads=[BmT_, BwMK], writes=[Bp])
                    op('act', lambda e, pf=pf: e.copy(out=mk_t[:, :], in_=pf[:, :]), reads=[Bp], writes=[Bmk])
                    headnorm(mk_t[:, 0:256].rearrange("p (h d) -> p h d", h=4), Bmk, 128, 4, gmk, Bgmk)
                    dma('sp', o_memkv[i * 128:(i + 1) * 128, :], mk_t[:, :], reads=[Bmk], is_output=True)
                    op('act', lambda e: e.copy(out=mkb[:, :], in_=mk_t[:, 0:256]), reads=[Bmk], writes=[Bmkb])
                    op('dve', lambda e: e.tensor_copy(out=Vm[:, i, :, 0:64], in_=mk_t[:, 256:512].rearrange("p (h d) -> p h d", h=4)), reads=[Bmk], writes=[BVm])
                    pt, Bp = next_pt()
                    for h in range(4):
                        op('pe', lambda e, h=h, pt=pt: e.transpose(out=pt[0:64, h * 128:(h + 1) * 128], in_=mkb[:, h * 64:(h + 1) * 64], identity=ident[:, :]),
                           reads=[Bmkb, Bident], writes=[Bp])
                    op('dve', lambda e, pt=pt: e.tensor_copy(out=KmT[0:64, :, i * 128:(i + 1) * 128], in_=pt[0:64, 0:512].rearrange("p (k t) -> p k t", k=4)),
                       reads=[Bp], writes=[BKmT])
                kb.barrier()

            esA = ExitStack()
            with esA:
                sbA = mk_sb(esA)
                gattn, Bgattn = load_const(sbA, "gattn", [128, D], bc(norm_attn))
                wA = sbA("wA", [128, 8, 1304], BF16); BwA = Buf("wA")
                for k in range(8):
                    dma('pool', wA[:, k, :], w_in_v[:, k, 0:1304], writes=[BwA])
                wQM = sbA("wQM", [128, 8, 256], BF16); BwQM = Buf("wQM")
                dma('pool', wQM[:], w_in_v[:, :, O_QM:O_QM + 256], writes=[BwQM])
                qf = sbA("qf", [128, 8, HD]); Bqf = Buf("qf")
                qb = sbA("qb", [128, 512], BF16); Bqb = Buf("qb")
                rows_t = [sbA("rows_t0", [128, 512])] * 2; Brows = [Buf("rows0")] * 2
                win_t = [sbA("win_t0", [128, 256])] * 2; Bwin = [Buf("win0")] * 2
                kk = sbA("kk", [128, 4, HD]); Bkk = Buf("kk")
                kkb = sbA("kkb", [128, 4, HD], BF16); Bkkb = Buf("kkb")
                kvcb = sbA("kvcb", [128, 256], BF16); Bkvcb = Buf("kvcb")
                qmf = sbA("qmf", [128, 4, HD]); Bqmf = Buf("qmf")
                qmb = sbA("qmb", [128, 256], BF16); Bqmb = Buf("qmb")
                qmT = sbA("qmT", [64, 4, 128], BF16); BqmT = Buf("qmT")
                PmT = [sbA("PmT%d" % j, [128, 512], BF16) for j in range(2)]; BPmT = [Buf("PmT%d" % j) for j in range(2)]
                omt = sbA("omt", [128, 4, HD], BF16); Bomt = Buf("omt")
                mrec = sbA("mrec", [128, 4]); Bmrec = Buf("mrec")

                for i in (TILES if TILES is not None else list(range(NTL))):
                    n = tsz(i); t0 = i * 128; sl = i % 2
                    rms_to_T(xs[t0:t0 + n, :], n, gattn, Bgattn, hT, t0, BhT[i], sl)
                    zp = []
                    for cbi, (wt, Bw, c0, cw) in enumerate([(wA, BwA, 0, 512), (wA, BwA, 512, 512), (wA, BwA, 1024, 280), (wQM, BwQM, 0, 256)]):
                        pf, Bp = next_pf()
                        for k in range(8):
                            op('pe', lambda e, k=k, pf=pf, wt=wt, c0=c0, cw=cw: e.matmul(pf[:n, 0:cw], lhsT=hT[:, k, t0:t0 + n], rhs=wt[:, k, c0:c0 + cw],
                                                                                         start=(k == 0), stop=(k == 7)),
                               reads=[BhT[i], Bw], writes=[Bp])
                        zp.append((pf, Bp))
                    (p0, B0), (p1, B1), (p2, B2), (p3, B3) = zp
                    op('act', lambda e: e.copy(out=qf[:n].rearrange("p h d -> p (h d)"), in_=p0[:n, 0:512]), reads=[B0], writes=[Bqf])
                    headnorm(qf[:n, :, :], Bqf, n, 8, gq, Bgq, cs_tok[:n, i, :], Bcs)
                    tgt_q, Btq = (qb, Bqb) if i < 16 else (q16, Bq16)
                    op('act', lambda e: e.copy(out=tgt_q[:n, :], in_=qf[:n].rearrange("p h d -> p (h d)")), reads=[Bqf], writes=[Btq])
                    pt, Bp = next_pt()
                    for h in range(8):
                        op('pe', lambda e, h=h, pt=pt: e.transpose(out=pt[0:64, h * 128:h * 128 + n], in_=tgt_q[:n, h * 64:(h + 1) * 64], identity=ident[:n, :n]),
                           reads=[Btq, Bident], writes=[Bp])
                    op('dve', lambda e, pt=pt: e.tensor_copy(out=QT[0:64, :, t0:t0 + n], in_=pt[0:64, :].rearrange("p (k t) -> p k t", k=8)[:, :, :n]),
                       reads=[Bp], writes=[BQT[i]])
                    rw, Brw = rows_t[sl], Brows[sl]; wn, Bwn = win_t[sl], Bwin[sl]
                    op('act', lambda e: e.copy(out=rw[:n, :], in_=p1[:n, 0:512]), reads=[B1], writes=[Brw])
                    op('act', lambda e: e.copy(out=wn[:n, :], in_=p2[:n, 0:256]), reads=[B2], writes=[Bwn])
                    op('act', lambda e: e.activation(out=gates[:n, i, :], in_=p2[:n, 256:280], func=AF.Sigmoid), reads=[B2], writes=[Bgates])
                    op('dve', lambda e: e.tensor_copy(out=kk[:n, 0:2, :], in_=rw[:n, 256:384].rearrange("p (g d) -> p g d", g=2)), reads=[Brw], writes=[Bkk])
                    op('dve', lambda e: e.tensor_copy(out=kk[:n, 2:4, :], in_=wn[:n, 0:128].rearrange("p (g d) -> p g d", g=2)), reads=[Bwn], writes=[Bkk])
                    headnorm(kk[:n, :, :], Bkk, n, 4, gk, Bgk, cs_tok[:n, i, :], Bcs)
                    op('dve', lambda e: e.tensor_copy(out=rw[:n, 256:384].rearrange("p (g d) -> p g d", g=2), in_=kk[:n, 0:2, :]), reads=[Bkk], writes=[Brw])
                    op('dve', lambda e: e.tensor_copy(out=wn[:n, 0:128].rearrange("p (g d) -> p g d", g=2), in_=kk[:n, 2:4, :]), reads=[Bkk], writes=[Bwn])
                    op('act', lambda e: e.copy(out=kkb[:n], in_=kk[:n]), reads=[Bkk], writes=[Bkkb])
                    dma('sp', o_rows[t0:t0 + n, :], rw[:n, :], reads=[Brw], is_output=True)
                    if 12 <= i < 16:
                        dma('sp', o_winp[(i - 12) * 128:(i - 11) * 128, :], wn[:n, :], reads=[Bwn], is_output=True)
                    if i == 16:
                        dma('sp', o_wins[:, 511, :], wn[:n, :], reads=[Bwn], is_output=True)
                    if i < 16:
                        pt, Bp = next_pt()
                        for j in range(4):
                            op('pe', lambda e, j=j, pt=pt: e.transpose(out=pt[0:64, j * 128:(j + 1) * 128], in_=kkb[:n, j, :], identity=ident[:n, :n]),
                               reads=[Bkkb, Bident], writes=[Bp])
                        op('act', lambda e: e.copy(out=kvcb[:n, :], in_=p1[:n, 0:256]), reads=[B1], writes=[Bkvcb])
                        for j in range(2):
                            op('pe', lambda e, j=j, pt=pt: e.transpose(out=pt[:, (4 + j) * 128:(5 + j) * 128], in_=kvcb[:n, j * 128:(j + 1) * 128], identity=ident[:n, :n]),
                               reads=[Bkvcb, Bident], writes=[Bp])
                        op('dve', lambda e, pt=pt: e.tensor_copy(out=KsT[0:64, :, t0:t0 + n], in_=pt[0:64, 0:256].rearrange("p (g t) -> p g t", g=2)),
                           reads=[Bp], writes=[BKsT])
                        op('dve', lambda e, pt=pt: e.tensor_copy(out=KwT[0:64, :, t0:t0 + n], in_=pt[0:64, 256:512].rearrange("p (g t) -> p g t", g=2)),
                           reads=[Bp], writes=[BKwT])
                        op('dve', lambda e, pt=pt: e.tensor_copy(out=KcTr[:, :].rearrange("p (r m) -> p r m", r=16)[:, :, 8 * i:8 * i + 8].rearrange("p r m -> p m r"), in_=pt[:, 512:640].rearrange("p (m r) -> p m r", r=16)), reads=[Bp], writes=[BKcTr])
                        op('dve', lambda e, pt=pt: e.tensor_copy(out=VcTr[:, :].rearrange("p (r m) -> p r m", r=16)[:, :, 8 * i:8 * i + 8].rearrange("p r m -> p m r"), in_=pt[:, 640:768].rearrange("p (m r) -> p m r", r=16)), reads=[Bp], writes=[BVcTr])
                        op('dve', lambda e: e.tensor_copy(out=Vs[:n, i, :, 0:64], in_=p1[:n, 384:512].rearrange("p (g d) -> p g d", g=2)), reads=[B1], writes=[BVs])
                        op('dve', lambda e: e.tensor_copy(out=Vw[:n, i, :, 0:64], in_=p2[:n, 128:256].rearrange("p (g d) -> p g d", g=2)), reads=[B2], writes=[BVw])
                    else:
                        pt, Bp = next_pt()
                        for j in range(2):
                            op('pe', lambda e, j=j, pt=pt: e.transpose(out=pt[:, j * 128:j * 128 + n], in_=kkb[:n, 2 * j:2 * j + 2, :].rearrange("p g d -> p (g d)"),
                                                                        identity=ident[:n, :n]), reads=[Bkkb, Bident], writes=[Bp])
                        op('dve', lambda e, pt=pt: e.tensor_copy(out=KnT[:, :, :], in_=pt[:, 0:256].rearrange("p (j t) -> p j t", j=2)[:, :, :n]),
                           reads=[Bp], writes=[BKnT])
                        op('dve', lambda e: e.tensor_copy(out=Vn[:n, 0, :], in_=p1[:n, 384:512]), reads=[B1], writes=[BVn])
                        op('dve', lambda e: e.tensor_copy(out=Vn[:n, 1, :], in_=p2[:n, 128:256]), reads=[B2], writes=[BVn])
                    op('act', lambda e: e.copy(out=qmf[:n].rearrange("p h d -> p (h d)"), in_=p3[:n, 0:256]), reads=[B3], writes=[Bqmf])
                    headnorm(qmf[:n, :, :], Bqmf, n, 4, gmq, Bgmq)
                    tgt_m, Btm = (qmb, Bqmb) if i < 16 else (qm16, Bqm16)
                    op('act', lambda e: e.copy(out=tgt_m[:n, :], in_=qmf[:n].rearrange("p h d -> p (h d)")), reads=[Bqmf], writes=[Btm])
                    if i < 16 and phases >= 4:
                        pt, Bp = next_pt()
                        for h in range(4):
                            op('pe', lambda e, h=h, pt=pt: e.transpose(out=pt[0:64, h * 128:h * 128 + n], in_=tgt_m[:n, h * 64:(h + 1) * 64], identity=ident[:n, :n]),
                               reads=[Btm, Bident], writes=[Bp])
                        op('dve', lambda e, pt=pt: e.tensor_copy(out=qmT[0:64, :, :n], in_=pt[0:64, 0:512].rearrange("p (k t) -> p k t", k=4)[:, :, :n]),
                           reads=[Bp], writes=[BqmT])
                        for nt in range(2):
                            pf, Bp = next_pf()
                            for h in range(4):
                                op('pe', lambda e, h=h, pf=pf, nt=nt: e.matmul(pf[:, h * 128:h * 128 + n], lhsT=KmT[0:64, h, nt * 128:(nt + 1) * 128], rhs=qmT[0:64, h, :n],
                                                                               start=True, stop=True), reads=[BKmT, BqmT], writes=[Bp])
                            op('act', lambda e, pf=pf, nt=nt: e.activation(out=PmT[nt][:, :], in_=pf[:, :], func=AF.Exp, scale=0.125), reads=[Bp], writes=[BPmT[nt]])
                        pf, Bp = next_pf()
                        for h in range(4):
                            for nt in range(2):
                                op('pe', lambda e, h=h, pf=pf, nt=nt: e.matmul(pf[:n, h * 65:(h + 1) * 65], lhsT=PmT[nt][:, h * 128:h * 128 + n], rhs=Vm[:, nt, h, :],
                                                                               start=(nt == 0), stop=(nt == 1)), reads=[BPmT[nt], BVm], writes=[Bp])
                        pv = pf[:n, 0:260].rearrange("p (h c) -> p h c", c=65)
                        op('dve', lambda e, pv=pv: e.reciprocal(out=mrec[:n, :].unsqueeze(2), in_=pv[:, :, 64:65]), reads=[Bp], writes=[Bmrec])
                        op('dve', lambda e, pv=pv: e.tensor_tensor(out=omt[:n, :, :], in0=pv[:, :, 0:64], in1=mrec[:n, :].unsqueeze(2).broadcast_to([n, 4, HD]), op=ALU.mult),
                           reads=[Bp, Bmrec], writes=[Bomt])
                        pt, Bp = next_pt()
                        for j in range(2):
                            op('pe', lambda e, j=j, pt=pt: e.transpose(out=pt[:, j * 128:j * 128 + n], in_=omt[:n, 2 * j:2 * j + 2, :].rearrange("p h d -> p (h d)"),
                                                                        identity=ident[:n, :n]), reads=[Bomt, Bident], writes=[Bp])
                        op('act', lambda e, pt=pt: e.copy(out=omT[:, :, t0:t0 + n], in_=pt[:, 0:256].rearrange("p (j t) -> p j t", j=2)[:, :, :n]),
                           reads=[Bp], writes=[BomT])
                kb.barrier()

            esC = ExitStack()
            with esC:
                sbC = mk_sb(esC)
                dma('sp', o_wins[:, 0:511, :], cache_win[:, 1:512, :], is_output=True)
                dma('sp', o_convs[:, 0:29, :], cache_conv[:, 1:30, :], is_output=True)
                dma('sp', o_ffns[:, 0, :], cache_ffn[:, 1, :], is_output=True)
                wUk = [sbC("wUk%d" % j, [128, 1024], BF16) for j in range(2)]; BwUk = [Buf("wUk%d" % j) for j in range(2)]
                sig_t = sbC("sig_t", [128, 512]); Bsig = Buf("sig_t")
                glu_t = [sbC("glu_t%d" % i, [128, 512]) for i in range(2)]; Bglu = [Buf("glu_t%d" % i) for i in range(2)]
                pabs = {}
                for i in (15, 16):
                    for cbi in range(2):
                        pabs[(i, cbi)] = next_pf()
                kb.pe_force_inc = True
                for k in range(8):
                    dma('pool', wUk[k % 2][:, :], w_in_v[:, k, O_UC:O_UC + 1024], writes=[BwUk[k % 2]])
                    for i in (15, 16):
                        n = tsz(i); t0 = i * 128
                        for cbi in range(2):
                            pf, Bp = pabs[(i, cbi)]
                            op('pe', lambda e, k=k, pf=pf, cbi=cbi, n=n, t0=t0: e.matmul(pf[:n, :], lhsT=hT[:, k, t0:t0 + n], rhs=wUk[k % 2][:, cbi * 512:(cbi + 1) * 512],
                                                                                       start=(k == 0), stop=(k == 7)), reads=[BhT[i], BwUk[k % 2]], writes=[Bp])
                kb.pe_force_inc = False
                for ii, i in enumerate((15, 16)):
                    n = tsz(i); t0 = i * 128
                    (pa, Ba), (pb, Bb) = pabs[(i, 0)], pabs[(i, 1)]
                    op('act', lambda e, pb=pb: e.activation(out=sig_t[:n, :], in_=pb[:n, :], func=AF.Sigmoid), reads=[Bb], writes=[Bsig])
                    op('dve', lambda e, pa=pa, ii=ii: e.tensor_tensor(out=glu_t[ii][:n, :], in0=pa[:n, :], in1=sig_t[:n, :], op=ALU.mult),
                       reads=[Ba, Bsig], writes=[Bglu[ii]])
                    if i == 15:
                        dma('sp', o_convp[:, :], glu_t[ii][98:128, :], reads=[Bglu[ii]], is_output=True)
                    else:
                        dma('sp', o_convs[:, 29, :], glu_t[ii][:n, :], reads=[Bglu[ii]], is_output=True)
                kb.barrier()

            if phases >= 8:
                esE = ExitStack()
                with esE:
                    sbE = mk_sb(esE)
                    cm = sbE("cm", [128, NS, 2, 512], BF16); Bcm = Buf("cm")
                    for s_i in range(NS):
                        dma('pool', cm[:, s_i, :, :], cache_mem[s_i].rearrange("(nt p) c -> p nt c", p=128), writes=[Bcm])
                    QmP = sbE("QmP", [128, 2, NS], BF16); BQmP = Buf("QmP")
                    pt, Bp = next_pt()
                    for hp in range(2):
                        op('pe', lambda e, hp=hp, pt=pt: e.transpose(out=pt[:, hp * 128:hp * 128 + NS], in_=qm16[:NS, hp * 128:(hp + 1) * 128], identity=ident[:NS, :NS]),
                           reads=[Bqm16, Bident], writes=[Bp])
                    op('dve', lambda e, pt=pt: e.tensor_copy(out=QmP[:, :, :], in_=pt[:, 0:256].rearrange("p (j t) -> p j t", j=2)[:, :, :NS]), reads=[Bp], writes=[BQmP])
                    Qmbd = sbE("Qmbd", [128, NS, 2, 2], BF16); BQmbd = Buf("Qmbd")
                    op('pool', lambda e: e.memset(Qmbd[:].rearrange("p s a b -> p (s a b)"), 0.0), writes=[BQmbd])
                    op('dve', lambda e: e.tensor_copy(out=Qmbd[0:64, :, :, 0], in_=QmP[0:64, :, :].rearrange("p hp s -> p s hp")), reads=[BQmP, BQmbd], writes=[BQmbd])
                    op('dve', lambda e: e.tensor_copy(out=Qmbd[64:128, :, :, 1], in_=QmP[64:128, :, :].rearrange("p hp s -> p s hp")), reads=[BQmP, BQmbd], writes=[BQmbd])
                    KmsT = [sbE("KmsT%d" % j, [128, 2, 256], BF16) for j in range(2)]; BKmsT = [Buf("KmsT%d" % j) for j in range(2)]
                    pSm, BSm = PF[0], BPF[0]
                    for s_i in range(NS):
                        pt, Bp = next_pt()
                        for hp in range(2):
                            for nt in range(2):
                                op('pe', lambda e, hp=hp, nt=nt, pt=pt, s_i=s_i: e.transpose(out=pt[:, (hp * 2 + nt) * 128:(hp * 2 + nt + 1) * 128], in_=cm[:, s_i, nt, hp * 128:(hp + 1) * 128],
                                                                                          identity=ident[:, :]), reads=[Bcm, Bident], writes=[Bp])
                        kt, Bkt = KmsT[s_i % 2], BKmsT[s_i % 2]
                        op('dve', lambda e, pt=pt, kt=kt: e.tensor_copy(out=kt[:, :, :], in_=pt[:, 0:512].rearrange("p (hp n) -> p hp n", hp=2)), reads=[Bp], writes=[Bkt])
                        for nt in range(2):
                            for hp in range(2):
                                c0_ = s_i * 8 + nt * 4 + hp * 2
                                op('pe', lambda e, hp=hp, nt=nt, kt=kt, c0_=c0_, s_i=s_i: e.matmul(pSm[:, c0_:c0_ + 2], lhsT=kt[:, hp, nt * 128:(nt + 1) * 128], rhs=Qmbd[:, s_i, hp, :],
                                                                                                start=True, stop=True), reads=[Bkt, BQmbd], writes=[BSm])
                    PmsT = sbE("PmsT", [128, NS, 2, 4], BF16); BPmsT = Buf("PmsT")
                    op('act', lambda e: e.activation(out=PmsT[:].rearrange("p s a b -> p (s a b)"), in_=pSm[:, 0:128], func=AF.Exp, scale=0.125), reads=[BSm], writes=[BPmsT])
                    Rm = sbE("Rm", [128, NS, 4]); BRm = Buf("Rm")
                    op('dve', lambda e: e.tensor_tensor(out=Rm[:, :, :], in0=PmsT[:, :, 0, :], in1=PmsT[:, :, 1, :], op=ALU.add), reads=[BPmsT], writes=[BRm])
                    pden, Bpden = PF[1], BPF[1]
                    for s_i in range(NS):
                        op('pe', lambda e, s_i=s_i: e.matmul(pden[0:4, s_i:s_i + 1], lhsT=Rm[:, s_i, :], rhs=onesf[:, 0:1], start=True, stop=True), reads=[BRm, Bonesf], writes=[Bpden])
                    rden = sbE("rden", [4, NS]); Brden = Buf("rden")
                    op('dve', lambda e: e.reciprocal(out=rden[:, :], in_=pden[0:4, 0:NS]), reads=[Bpden], writes=[Brden])
                    oms = sbE("oms", [4, NS, HD]); Boms = Buf("oms")
                    otmp = sbE("otmp", [4, 2, 4, HD]); Botmp = Buf("otmp")
                    for s2 in range(NS // 2):
                        pO, BpO = next_pf()
                        for ss in range(2):
                            s_i = 2 * s2 + ss
                            for nt in range(2):
                                op('pe', lambda e, ss=ss, nt=nt, s_i=s_i, pO=pO: e.matmul(pO[0:4, ss * 256:(ss + 1) * 256], lhsT=PmsT[:, s_i, nt, :], rhs=cm[:, s_i, nt, 256:512],
                                                                                       start=(nt == 0), stop=(nt == 1)), reads=[BPmsT, Bcm], writes=[BpO])
                        op('dve', lambda e, pO=pO: e.tensor_tensor(out=otmp[:], in0=pO[0:4, :].rearrange("p (s h d) -> p s h d", s=2, h=4),
                                                                   in1=identf[0:4, 0:4].unsqueeze(1).unsqueeze(3).broadcast_to([4, 2, 4, HD]), op=ALU.mult),
                           reads=[BpO, Bidentf], writes=[Botmp])
                        op('dve', lambda e, s2=s2: e.tensor_reduce(out=oms[:, 2 * s2:2 * s2 + 2, :], in_=otmp[:].rearrange("p s h d -> p s d h"), axis=AX.X, op=ALU.add),
                           reads=[Botmp], writes=[Boms])
                    op('dve', lambda e: e.tensor_tensor(out=oms[:], in0=oms[:], in1=rden[:, :].unsqueeze(2).broadcast_to([4, NS, HD]), op=ALU.mult), reads=[Boms, Brden], writes=[Boms])
                    dma('sp', scr[0:4 * NS * HD].rearrange("(h s d) -> h s d", h=4, s=NS), oms[:, :, :], reads=[Boms], writes=[Bscr])
                    om16 = sbE("om16", [NS, 4, HD]); Bom16 = Buf("om16")
                    dma('sp', om16[:, :, :], scr[0:4 * NS * HD].rearrange("(h s d) -> s h d", h=4, s=NS), reads=[Bscr], writes=[Bom16])
                    om16b = sbE("om16b", [NS, 256], BF16); Bom16b = Buf("om16b")
                    op('dve', lambda e: e.tensor_copy(out=om16b[:, :], in_=om16[:].rearrange("s h d -> s (h d)")), reads=[Bom16], writes=[Bom16b])
                    pt, Bp = next_pt()
                    for j in range(2):
                        op('pe', lambda e, j=j, pt=pt: e.transpose(out=pt[:, j * 128:j * 128 + NS], in_=om16b[:NS, j * 128:(j + 1) * 128], identity=ident[:NS, :NS]),
                           reads=[Bom16b, Bident], writes=[Bp])
                    op('act', lambda e, pt=pt: e.copy(out=omT[:, :, T:TT], in_=pt[:, 0:256].rearrange("p (j t) -> p j t", j=2)[:, :, :NS]), reads=[Bp], writes=[BomT])
                    kb.barrier()

            if phases >= 6:
                esF = ExitStack()
                with esF:
                    sbF = mk_sb(esF)
                    KcT = sbF("KcT", [64, 2, 128], BF16); BKcT = Buf("KcT")
                    Vc = sbF("Vc", [128, 2, 97], BF16); BVc = Buf("Vc")
                    esF1 = ExitStack(); esF1.__enter__(); sbF_keep = sbF; sbF = mk_sb(esF1)
                    W2 = sbF("W2", [128, 2, 32, 128], BF16); BW2 = Buf("W2")
                    op('pool', lambda e: e.memset(W2[:].rearrange("p a l e -> p (a l e)"), 0.0), writes=[BW2])
                    wc_v = w_cmp.rearrange("kv l d e -> d kv l e")
                    for kv_ in range(2):
                        dma('pool', W2[0:64, kv_, :, 0:64], wc_v[:, kv_, :, :], reads=[BW2], writes=[BW2])
                        dma('pool', W2[64:128, kv_, :, 64:128], wc_v[:, kv_, :, :], reads=[BW2], writes=[BW2])
                    pe2 = sbF("pe2", [64, 128]); Bpe2 = Buf("pe2")
                    pe_v = cmp_pe.rearrange("kv l d -> (kv l) d")
                    dma('sp', pe2[:, 0:64], pe_v, writes=[Bpe2]); dma('sp', pe2[:, 64:128], pe_v, writes=[Bpe2])
                    peT2 = sbF("peT2", [128, 64], BF16); BpeT2 = Buf("peT2")
                    pf, Bp = next_pf()
                    op('pe', lambda e, pf=pf: e.transpose(out=pf[:, 0:64], in_=pe2[:, :], identity=identf[0:64, 0:64]), reads=[Bpe2, Bidentf], writes=[Bp])
                    op('dve', lambda e, pf=pf: e.tensor_copy(out=peT2[:, :], in_=pf[:, 0:64]), reads=[Bp], writes=[BpeT2])
                    cpe = sbF("cpe", [1, 2, 128]); Bcpe = Buf("cpe")
                    cpeb = sbF("cpeb", [128, 2, 128]); Bcpeb = Buf("cpeb")
                    for kv_ in range(2):
                        pf, Bp = next_pf()
                        for l in range(32):
                            op('pe', lambda e, l=l, pf=pf, kv_=kv_: e.matmul(pf[0:1, 0:128], lhsT=peT2[:, kv_ * 32 + l:kv_ * 32 + l + 1], rhs=W2[:, kv_, l, :],
                                                                             start=(l == 0), stop=(l == 31)), reads=[BpeT2, BW2], writes=[Bp])
                        op('dve', lambda e, pf=pf, kv_=kv_: e.tensor_copy(out=cpe[0:1, kv_, :], in_=pf[0:1, 0:128]), reads=[Bp], writes=[Bcpe])
                        pf2, Bp2 = next_pf()
                        op('pe', lambda e, pf2=pf2, kv_=kv_: e.matmul(pf2[:, 0:128], lhsT=onesf[0:1, :], rhs=cpe[0:1, kv_, :], start=True, stop=True),
                           reads=[Bcpe, Bonesf], writes=[Bp2])
                        op('dve', lambda e, pf2=pf2, kv_=kv_: e.tensor_copy(out=cpeb[:, kv_, :], in_=pf2[:, 0:128]), reads=[Bp2], writes=[Bcpeb])
                    kcf = sbF("kcf", [128, 2, HD]); Bkcf = Buf("kcf")
                    kcb = sbF("kcb", [128, 2, HD], BF16); Bkcb = Buf("kcb")
                    ovl, Bovl = load_const(sbF, "ovl", [128, 33], C['ovl'][:, :])
                    for g in range(2):
                        op('dve', lambda e, g=g: e.tensor_copy(out=Vc[:, g, 0:33], in_=ovl[:, :]), reads=[Bovl], writes=[BVc])
                    for kv_, (src, Bsrc) in enumerate(((KcTr, BKcTr), (VcTr, BVcTr))):
                        pf, Bp = next_pf()
                        for l in range(32):
                            op('pe', lambda e, l=l, pf=pf, kv_=kv_, src=src: e.matmul(pf[0:127, 0:128], lhsT=src[:, (l % 16) * 128 + l // 16:(l % 16) * 128 + l // 16 + 127], rhs=W2[:, kv_, l, :],
                                                                                      start=(l == 0), stop=(l == 31)), reads=[Bsrc, BW2], writes=[Bp])
                        if kv_ == 0:
                            op('dve', lambda e, pf=pf: e.tensor_tensor(out=kcf[0:127].rearrange("p g d -> p (g d)"), in0=pf[0:127, 0:128], in1=cpeb[0:127, 0, :], op=ALU.add),
                               reads=[Bp, Bcpeb], writes=[Bkcf])
                            headnorm(kcf[0:127, :, :], Bkcf, 127, 2, gkc, Bgkc, cs_cmp[0:127, :], Bcsc)
                            op('act', lambda e: e.copy(out=kcb[0:127], in_=kcf[0:127]), reads=[Bkcf], writes=[Bkcb])
                            pt, Bpp = next_pt()
                            for g in range(2):
                                op('pe', lambda e, g=g, pt=pt: e.transpose(out=pt[0:64, g * 128:g * 128 + 127], in_=kcb[0:127, g, :], identity=ident[0:127, 0:127]),
                                   reads=[Bkcb, Bident], writes=[Bpp])
                            op('dve', lambda e, pt=pt: e.tensor_copy(out=KcT[:, :, 0:127], in_=pt[0:64, 0:256].rearrange("p (g t) -> p g t", g=2)[:, :, 0:127]),
                               reads=[Bpp], writes=[BKcT])
                        else:
                            op('dve', lambda e, pf=pf: e.tensor_tensor(out=Vc[0:127, :, 33:97], in0=pf[0:127, 0:128].rearrange("p (g d) -> p g d", g=2),
                                                                       in1=cpeb[0:127, 1, :].rearrange("p (g d) -> p g d", g=2), op=ALU.add),
                               reads=[Bp, Bcpeb], writes=[BVc])
                    ptb = sbF("ptb", [128, 256], I32); Bptb = Buf("ptb")
                    dma('sp', ptb[:, :], ptab.partition_broadcast(128).rearrange("p a d -> p (a d)"), writes=[Bptb])
                    io_i = sbF("io_i", [128, 1], I32); Bio = Buf("io")
                    op('pool', lambda e: e.iota(out=io_i[:, :], pattern=[[0, 1]], base=0, channel_multiplier=1), writes=[Bio])
                    io_f = sbF("io_f", [128, 1]); Biof = Buf("io_f")
                    op('dve', lambda e: e.tensor_copy(out=io_f[:, :], in_=io_i[:, :]), reads=[Bio], writes=[Biof])
                    idx = sbF("idx", [128, 256], I32); Bidx = Buf("idx")
                    op('dve', lambda e: e.tensor_scalar(out=idx[:, :], in0=ptb[:, :], scalar1=128.0, scalar2=io_f[:, 0:1], op0=ALU.mult, op1=ALU.add),
                       reads=[Bptb, Biof], writes=[Bidx])
                    wGA = sbF("wGA", [128, 8, 24], BF16); BwGA = Buf("wGA")
                    dma('pool', wGA[:], w_in_v[:, :, O_GA:O_GA + 24], writes=[BwGA])
                    gT = sbF("gT", [8, 3, NS]); BgT = Buf("gT")
                    pf, Bp = next_pf()
                    for b_ in range(3):
                        for k in range(8):
                            op('pe', lambda e, b_=b_, k=k, pf=pf: e.matmul(pf[0:8, b_ * NS:(b_ + 1) * NS], lhsT=wGA[:, k, b_ * 8:(b_ + 1) * 8], rhs=hT[:, k, T:TT],
                                                                           start=(k == 0), stop=(k == 7)), reads=[BwGA, BhT[16]], writes=[Bp])
                    op('act', lambda e, pf=pf: e.activation(out=gT[:].rearrange("p b s -> p (b s)"), in_=pf[0:8, 0:3 * NS], func=AF.Sigmoid), reads=[Bp], writes=[BgT])
                    q_st = sbF("q_st", [NS, 4, 128], BF16); Bq_st = Buf("q_st")
                    for g in range(2):
                        op('dve', lambda e, g=g: e.tensor_copy(out=q_st[:, :, g * 64:(g + 1) * 64], in_=q16[:NS, g * 256:(g + 1) * 256].rearrange("p (j d) -> p j d", j=4)),
                           reads=[Bq16], writes=[Bq_st])
                    pt, Bp = next_pt()
                    for j in range(4):
                        op('pe', lambda e, j=j, pt=pt: e.transpose(out=pt[:, j * 128:j * 128 + NS], in_=q_st[:NS, j, :], identity=ident[:NS, :NS]), reads=[Bq_st, Bident], writes=[Bp])
                    Qbd = sbF("Qbd", [128, NS, 8], BF16); BQbd = Buf("Qbd")
                    op('pool', lambda e: e.memset(Qbd[:].rearrange("p s h -> p (s h)"), 0.0), writes=[BQbd])
                    ptv = pt[:, 0:512].rearrange("p (j t) -> p j t", j=4)[:, :, :NS].rearrange("p j s -> p s j")
                    op('dve', lambda e: e.tensor_copy(out=Qbd[0:64, :, 0:4], in_=ptv[0:64]), reads=[Bp, BQbd], writes=[BQbd])
                    op('dve', lambda e: e.tensor_copy(out=Qbd[64:128, :, 4:8], in_=ptv[64:128]), reads=[Bp, BQbd], writes=[BQbd])
                    ovlb = sbF("ovlb", [128, 33], BF16); Bovlb = Buf("ovlb")
                    op('dve', lambda e: e.tensor_copy(out=ovlb[:, :], in_=ovl[:, :]), reads=[Bovl], writes=[Bovlb])
                    dmk, Bdmk = load_const(sbF, "dmk", [NS, NS, 8], C['dmask'][:, :].rearrange("a (s h) -> a s h", h=8), BF16)
                    G8, BG8 = load_const(sbF, "G8", [8, 8], C['G8'][:, :])
                    hm8, Bhm8 = load_const(sbF, "hm8", [8, 2], C['hm8'][:, :])
                    sadd, Bsadd = load_const(sbF, "sadd", [8, 32], C['seladd_s'][:, :])
                    gbuf = sbF("gbuf", [128, 16, 512], BF16); Bgbuf = Buf("gbuf")
                    cwb = sbF("cwb", [128, 4, 256], BF16); Bcwb = Buf("cwb")
                    KT3 = sbF("KT3", [128, 2, T], BF16); BKT3 = [Buf("KT3a"), Buf("KT3b")]
                    KwTs = sbF("KwTs", [128, 512], BF16); BKwTs = Buf("KwTs")
                    kcs = sbF("kcs", [128, 2, HD]); Bkcs = Buf("kcs")
                    kcsb = sbF("kcsb", [128, 128], BF16); Bkcsb = Buf("kcsb")
                    KcTs = sbF("KcTs", [128, 128], BF16); BKcTs = Buf("KcTs")
                    Vcs = sbF("Vcs", [128, 128], BF16); BVcs = Buf("Vcs")
                    Pc8 = sbF("Pc8", [128, 8], BF16); BPc8 = Buf("Pc8")
                    Ps8 = sbF("Ps8", [128, 128], BF16); BPs8 = Buf("Ps8")
                    Pw8 = sbF("Pw8", [128, 32], BF16); BPw8 = Buf("Pw8")
                    Pnf = sbF("Pnf", [NS, 2, 8]); BPnf = Buf("Pnf")
                    Pnb = sbF("Pnb", [NS, 2, 8], BF16); BPnb = Buf("Pnb")
                    Rr = sbF("Rr", [128, 2, 8]); BRr = Buf("Rr")
                    sm = sbF("sm", [8, 128]); Bsm = Buf("sm")
                    sc_s = sbF("sc_s", [8, 32]); Bsc_s = Buf("sc_s")
                    sc_2 = sbF("sc_2", [8, 32]); Bsc_2 = Buf("sc_2")
                    m8s = sbF("m8s", [8, 16]); Bm8s = Buf("m8s")
                    selT = sbF("selT", [96, 8], BF16); BselT = Buf("selT")
                    sc_3 = sbF("sc_3", [8, 96]); Bsc_3 = Buf("sc_3")
                    op('pool', lambda e: e.memset(sc_3[:, :], 0.0), writes=[Bsc_3])
                    ofull = sbF("ofull", [8, 128]); Bofull = Buf("ofull")
                    osamp = sbF("osamp", [8, 2, HD]); Bosamp = Buf("osamp")
                    pSc, BSc = PF[0], BPF[0]; pMi, BMi = PF[1], BPF[1]; pOo, BOo = PF[2], BPF[2]
                    crr = [0]

                    def next_c():
                        crr[0] = (crr[0] + 1) % 3
                        return PF[3 + crr[0]], BPF[3 + crr[0]]
                    cache_rows = cache_nsa
                    dma('sp', hts[:, :], hT[:].rearrange("p k t -> p (k t)"), reads=BhT, writes=[Bhts])
                    kb.barrier()
                    hfl = hT[:].rearrange("p k t -> p (k t)")

                    def carve(off, n_):
                        return hfl[:, off:off + n_]
                    PsT = [carve(j * 512, 512) for j in range(16)]; BPsT = [Buf("PsT%d" % j) for j in range(16)]
                    PwT = [carve(8192 + j * 512, 512) for j in range(5)]; BPwT = [Buf("PwT%d" % j) for j in range(5)]
                    cmpbias = carve(10752, 2048).rearrange("p (i q) -> p i q", i=16); Bcmpbias = Buf("cmpbias")
                    diag4 = carve(12800, 512).rearrange("p (h q) -> p h q", h=4); Bdiag4 = Buf("diag4")
                    winl4 = carve(13312, 512).rearrange("p (h q) -> p h q", h=4); Bwinl4 = Buf("winl4")
                    cmpb4 = carve(13824, 512).rearrange("p (h q) -> p h q", h=4); Bcmpb4 = Buf("cmpb4")
                    PcT = carve(14336, 512); BPcT = Buf("PcT")
                    stg = carve(14848, 128); Bstg = Buf("stg")
                    onsa_tok = carve(14976, 512).rearrange("p (h d) -> p h d", h=8); Bonsa_tok = Buf("onsa_tok")
                    dma('pool', cmpbias, C['cmpbias'][:, :, :], writes=[Bcmpbias])
                    for h_ in range(4):
                        dma('pool', diag4[:, h_, :], C['diag'][:, :], writes=[Bdiag4])
                        dma('pool', winl4[:, h_, :], C['winlow'][:, :], writes=[Bwinl4])
                    op('pool', lambda e: e.memset(stg, 0.0), writes=[Bstg])
                    Ff = carve(15488, 1024).bitcast(F32)
                    ot = Ff[:, 0:256].rearrange("p (h d) -> p h d", h=4); Bot = Buf("ot")
                    sel = Ff[:, 256:416]; Bsel = Buf("sel")
                    sc2 = Ff[:, 416:448]; Bsc2 = Buf("sc2")
                    m8 = Ff[:, 448:464]; Bm8 = Buf("m8")
                    rcp = Ff[:, 464:476].rearrange("p (b h) -> p b h", b=3); Brcp = Buf("rcp")
                    seladd_i = Ff[:, 476:508]; Bseladd = Buf("seladd")
                    of = ptb[:, :].bitcast(F32).rearrange("p (h d) -> p h d", h=4); Bof = Bptb
                    pS = [(PF[0], BPF[0]), (PF[1], BPF[1])]
                    (pOC, BOC), (pOW, BOW), (pOS, BOS), (pC, BC) = (PF[2], BPF[2]), (PF[3], BPF[3]), (PF[4], BPF[4]), (PF[5], BPF[5])
                    srr = [0]

                    def next_s():
                        srr[0] ^= 1
                        return pS[srr[0]]

                    def sample_iter(s_i):
                        for j in range(16):
                            col = s_i * 16 + j
                            dma('pool', None, None, reads=[Bidx], writes=[Bgbuf], fn=lambda e, j=j, col=col: e.indirect_dma_start(
                                out=gbuf[:, j, :], out_offset=None, in_=cache_rows[:, :], in_offset=bass.IndirectOffsetOnAxis(ap=idx[:, col:col + 1], axis=0)))
                        dma('pool', cwb[:, :, :], cache_win[s_i].rearrange("(j p) c -> p j c", p=128), writes=[Bcwb])
                        def tr_comp(comp, slot, deint):
                            for half in range(2):
                                pt, Bp = next_pt()
                                for jj in range(8):
                                    j = half * 8 + jj
                                    op('pe', lambda e, jj=jj, j=j, pt=pt: e.transpose(out=pt[:, jj * 128:(jj + 1) * 128], in_=gbuf[:, j, comp * 128:(comp + 1) * 128],
                                                                                identity=ident[:, :]), reads=[Bgbuf, Bident], writes=[Bp])
                                if deint:
                                    dst = KT3[:, slot, :].rearrange("p (r m) -> p r m", r=16)[:, :, half * 64:(half + 1) * 64].rearrange("p r m -> p m r")
                                    src_ = pt[:, :].rearrange("p (m r) -> p m r", r=16)
                                else:
                                    dst = KT3[:, slot, half * 1024:(half + 1) * 1024]; src_ = pt[:, :]
                                if half == 0:
                                    op('act', lambda e: e.copy(out=dst, in_=src_), reads=[Bp], writes=[BKT3[slot]])
                                else:
                                    op('dve', lambda e: e.tensor_copy(out=dst, in_=src_), reads=[Bp], writes=[BKT3[slot]])
                        tr_comp(0, 0, True); tr_comp(1, 1, True)
                        pt, Bp = next_pt()
                        for j in range(4):
                            op('pe', lambda e, j=j, pt=pt: e.transpose(out=pt[:, j * 128:(j + 1) * 128], in_=cwb[:, j, 0:128], identity=ident[:, :]), reads=[Bcwb, Bident], writes=[Bp])
                        op('dve', lambda e, pt=pt: e.tensor_copy(out=KwTs[:, :], in_=pt[:, 0:512]), reads=[Bp], writes=[BKwTs])
                        for kv_ in range(2):
                            pf, Bp = next_c()
                            for l in range(32):
                                op('pe', lambda e, l=l, pf=pf, kv_=kv_: e.matmul(pf[0:127, 0:128], lhsT=KT3[:, kv_, (l % 16) * 128 + l // 16:(l % 16) * 128 + l // 16 + 127], rhs=W2[:, kv_, l, :],
                                                                                 start=(l == 0), stop=(l == 31)), reads=[BKT3[kv_], BW2], writes=[Bp])
                            if kv_ == 0:
                                op('dve', lambda e, pf=pf: e.tensor_tensor(out=kcs[0:127].rearrange("p g d -> p (g d)"), in0=pf[0:127, 0:128], in1=cpeb[0:127, 0, :], op=ALU.add),
                                   reads=[Bp, Bcpeb], writes=[Bkcs])
                                headnorm(kcs[0:127, :, :], Bkcs, 127, 2, gkc, Bgkc, cs_cmp[0:127, :], Bcsc)
                                op('act', lambda e: e.copy(out=kcsb[0:127, :], in_=kcs[0:127].rearrange("p g d -> p (g d)")), reads=[Bkcs], writes=[Bkcsb])
                                pt, Bpp = next_pt()
                                op('pe', lambda e, pt=pt: e.transpose(out=pt[:, 0:127], in_=kcsb[0:127, :], identity=ident[0:127, 0:127]), reads=[Bkcsb, Bident], writes=[Bpp])
                                op('dve', lambda e, pt=pt: e.tensor_copy(out=KcTs[:, 0:127], in_=pt[:, 0:127]), reads=[Bpp], writes=[BKcTs])
                            else:
                                op('dve', lambda e, pf=pf: e.tensor_tensor(out=Vcs[0:127, :], in0=pf[0:127, 0:128], in1=cpeb[0:127, 1, :], op=ALU.add), reads=[Bp, Bcpeb], writes=[BVcs])
                        tr_comp(2, 0, False)
                        qs = Qbd[:, s_i, :]
                        op('pe', lambda e: e.matmul(pSc[0:127, 0:8], lhsT=KcTs[:, 0:127], rhs=qs, start=True, stop=True), reads=[BKcTs, BQbd], writes=[BSc])
                        op('act', lambda e: e.activation(out=Pc8[0:127, :], in_=pSc[0:127, 0:8], func=AF.Exp, scale=0.125), reads=[BSc], writes=[BPc8])
                        op('pe', lambda e: e.matmul(pMi[0:8, 0:33], lhsT=Pc8[0:127, :], rhs=ovlb[0:127, :], start=True, stop=True), reads=[BPc8, Bovlb], writes=[BMi])
                        op('dve', lambda e: e.reciprocal(out=sm[:, 0:1], in_=pMi[0:8, 32:33]), reads=[BMi], writes=[Bsm])
                        op('dve', lambda e: e.tensor_scalar(out=sc_s[:, :], in0=pMi[0:8, 0:32], scalar1=sm[:, 0:1], scalar2=None, op0=ALU.mult), reads=[BMi, Bsm], writes=[Bsc_s])
                        op('pe', lambda e: e.matmul(pMi[0:8, 64:96], lhsT=G8[:, :], rhs=sc_s[:, :], start=True, stop=True), reads=[BG8, Bsc_s], writes=[BMi])
                        op('dve', lambda e: e.tensor_tensor(out=sc_s[:, :], in0=pMi[0:8, 64:96], in1=sadd[:, :], op=ALU.add), reads=[BMi, Bsadd], writes=[Bsc_s])
                        op('dve', lambda e: e.max(out=m8s[:, 0:8], in_=sc_s[:, :]), reads=[Bsc_s], writes=[Bm8s])
                        op('dve', lambda e: e.match_replace(out=sc_2[:, :], in_to_replace=m8s[:, 0:8], in_values=sc_s[:, :], imm_value=-3.0e38), reads=[Bsc_s, Bm8s], writes=[Bsc_2])
                        op('dve', lambda e: e.max(out=m8s[:, 8:16], in_=sc_2[:, :]), reads=[Bsc_2], writes=[Bm8s])
                        op('dve', lambda e: e.tensor_scalar(out=sc_2[:, :], in0=sc_s[:, :], scalar1=m8s[:, 14:15], scalar2=None, op0=ALU.is_ge), reads=[Bsc_s, Bm8s], writes=[Bsc_2])
                        op('dve', lambda e: e.tensor_scalar(out=sc_3[:, 64:96], in0=sc_2[:, :], scalar1=-NEG, scalar2=NEG, op0=ALU.mult, op1=ALU.add), reads=[Bsc_2], writes=[Bsc_3])
                        op('pe', lambda e: e.transpose(out=pMi[0:96, 128:136], in_=sc_3[0:8, :], identity=identf[0:8, 0:8]), reads=[Bsc_3, Bidentf], writes=[BMi])
                        op('dve', lambda e: e.tensor_copy(out=selT[64:96, :], in_=pMi[64:96, 128:136]), reads=[BMi], writes=[BselT])
                        for j in range(16):
                            op('pe', lambda e, j=j: e.matmul(pSc[:, 8 + j * 8:16 + j * 8], lhsT=KT3[:, 0, j * 128:(j + 1) * 128], rhs=qs, start=True, stop=False), reads=[BKT3[0], BQbd], writes=[BSc])
                            op('pe', lambda e, j=j: e.matmul(pSc[:, 8 + j * 8:16 + j * 8], lhsT=KsT[64:96, 0, j * 128:(j + 1) * 128], rhs=selT[64:96, :], start=False, stop=True), reads=[BKsT, BselT], writes=[BSc])
                        for j in range(4):
                            op('pe', lambda e, j=j: e.matmul(pSc[:, 136 + j * 8:144 + j * 8], lhsT=KwTs[:, j * 128:(j + 1) * 128], rhs=qs, start=True, stop=True), reads=[BKwTs, BQbd], writes=[BSc])
                        for b_ in range(2):
                            op('pe', lambda e, b_=b_: e.matmul(pSc[0:NS, 168 + b_ * 8:176 + b_ * 8], lhsT=KnT[:, b_, :], rhs=qs, start=True, stop=False), reads=[BKnT, BQbd], writes=[BSc])
                            op('pe', lambda e, b_=b_: e.matmul(pSc[0:NS, 168 + b_ * 8:176 + b_ * 8], lhsT=ident[0:NS, 0:NS], rhs=dmk[0:NS, s_i, :], start=False, stop=True), reads=[Bident, Bdmk], writes=[BSc])
                        op('act', lambda e: e.activation(out=Ps8[:, :], in_=pSc[:, 8:136], func=AF.Exp, scale=0.125), reads=[BSc], writes=[BPs8])
                        op('act', lambda e: e.activation(out=Pw8[:, :], in_=pSc[:, 136:168], func=AF.Exp, scale=0.125), reads=[BSc], writes=[BPw8])
                        op('act', lambda e: e.activation(out=Pnf[:].rearrange("p b h -> p (b h)"), in_=pSc[0:NS, 168:184], func=AF.Exp, scale=0.125), reads=[BSc], writes=[BPnf])
                        op('dve', lambda e: e.tensor_copy(out=Pnb[:], in_=Pnf[:]), reads=[BPnf], writes=[BPnb])
                        op('pe', lambda e: e.matmul(pOo[0:8, 0:128], lhsT=Pc8[0:127, :], rhs=Vcs[0:127, :], start=True, stop=True), reads=[BPc8, BVcs], writes=[BOo])
                        for j in range(16):
                            op('pe', lambda e, j=j: e.matmul(pOo[0:8, 128:256], lhsT=Ps8[:, j * 8:(j + 1) * 8], rhs=gbuf[:, j, 384:512], start=(j == 0), stop=False), reads=[BPs8, Bgbuf], writes=[BOo])
                        op('pe', lambda e: e.matmul(pOo[0:8, 128:256], lhsT=Pnb[0:NS, 0, :], rhs=Vn[0:NS, 0, :], start=False, stop=True), reads=[BPnb, BVn], writes=[BOo])
                        for j in range(4):
                            op('pe', lambda e, j=j: e.matmul(pOo[0:8, 256:384], lhsT=Pw8[:, j * 8:(j + 1) * 8], rhs=cwb[:, j, 128:256], start=(j == 0), stop=False), reads=[BPw8, Bcwb], writes=[BOo])
                        op('pe', lambda e: e.matmul(pOo[0:8, 256:384], lhsT=Pnb[0:NS, 1, :], rhs=Vn[0:NS, 1, :], start=False, stop=True), reads=[BPnb, BVn], writes=[BOo])
                        op('dve', lambda e: e.tensor_reduce(out=Rr[:, 0, :], in_=Ps8[:, :].rearrange("p (j h) -> p h j", h=8), axis=AX.X, op=ALU.add), reads=[BPs8], writes=[BRr])
                        op('dve', lambda e: e.tensor_reduce(out=Rr[:, 1, :], in_=Pw8[:, :].rearrange("p (j h) -> p h j", h=8), axis=AX.X, op=ALU.add), reads=[BPw8], writes=[BRr])
                        for b_ in range(2):
                            op('pe', lambda e, b_=b_: e.matmul(pMi[0:8, 160 + b_:161 + b_], lhsT=Rr[:, b_, :], rhs=onesf[:, 0:1], start=True, stop=False), reads=[BRr, Bonesf], writes=[BMi])
                            op('pe', lambda e, b_=b_: e.matmul(pMi[0:8, 160 + b_:161 + b_], lhsT=Pnf[0:NS, b_, :], rhs=onesf[0:NS, 0:1], start=False, stop=True), reads=[BPnf, Bonesf], writes=[BMi])
                        op('dve', lambda e: e.reciprocal(out=sm[:, 1:3], in_=pMi[0:8, 160:162]), reads=[BMi], writes=[Bsm])
                        op('dve', lambda e: e.tensor_tensor(out=sm[:, 0:3], in0=sm[:, 0:3], in1=gT[:, :, s_i], op=ALU.mult), reads=[Bsm, BgT], writes=[Bsm])
                        op('dve', lambda e: e.tensor_scalar(out=ofull[:, :], in0=pOo[0:8, 0:128], scalar1=sm[:, 0:1], scalar2=None, op0=ALU.mult), reads=[BOo, Bsm], writes=[Bofull])
                        op('dve', lambda e: e.scalar_tensor_tensor(out=ofull[:, :], in0=pOo[0:8, 128:256], scalar=sm[:, 1:2], in1=ofull[:, :], op0=ALU.mult, op1=ALU.add), reads=[BOo, Bsm, Bofull], writes=[Bofull])
                        op('dve', lambda e: e.scalar_tensor_tensor(out=ofull[:, :], in0=pOo[0:8, 256:384], scalar=sm[:, 2:3], in1=ofull[:, :], op0=ALU.mult, op1=ALU.add), reads=[BOo, Bsm, Bofull], writes=[Bofull])
                        op('dve', lambda e: e.tensor_scalar(out=osamp[:, 0, :], in0=ofull[:, 0:64], scalar1=hm8[:, 0:1], scalar2=None, op0=ALU.mult), reads=[Bofull, Bhm8], writes=[Bosamp])
                        op('dve', lambda e: e.scalar_tensor_tensor(out=osamp[:, 1, :], in0=ofull[:, 64:128], scalar=hm8[:, 1:2], in1=osamp[:, 0, :], op0=ALU.mult, op1=ALU.add),
                           reads=[Bofull, Bhm8, Bosamp], writes=[Bosamp])
                        dma('sp', scr[8192:8192 + 8 * NS * HD].rearrange("(h s d) -> h s d", h=8, s=NS)[:, s_i, :], osamp[:, 1, :], reads=[Bosamp], writes=[Bscr])
                    def prompt_iter(i):
                        q0 = i * 128
                        dma('sp', seladd_i, C['seladd'][:, i, :], writes=[Bseladd])
                        op('dve', lambda e, i=i: e.tensor_copy(out=cmpb4[:], in_=cmpbias[:, i, :].unsqueeze(1).broadcast_to([128, 4, 128])),
                           reads=[Bcmpbias], writes=[Bcmpb4])
                        for g in range(2):
                            qr64 = QT[0:64, 4 * g:4 * g + 4, q0:q0 + 128]
                            qr96 = QT[0:96, 4 * g:4 * g + 4, q0:q0 + 128]
                            op('pe', lambda e: e.matmul(pC[0:127, :], lhsT=KcT[0:64, g, 0:127], rhs=qr64, start=True, stop=False), reads=[BKcT, BQT[i]], writes=[BC])
                            op('pe', lambda e: e.matmul(pC[0:127, :], lhsT=ident[0:127, 0:127], rhs=cmpb4[0:127, :, :], start=False, stop=True),
                               reads=[Bident, Bcmpb4], writes=[BC])
                            op('act', lambda e: e.activation(out=PcT[0:127, :], in_=pC[0:127, :], func=AF.Exp, scale=0.125), reads=[BC], writes=[BPcT])
                            wj = list(range(max(0, i - 4), i + 1))
                            for jj, j in enumerate(wj):
                                ps_, Bs_ = next_s()
                                extra = diag4 if j == i else (winl4 if j == i - 4 else None)
                                Bex = Bdiag4 if j == i else Bwinl4
                                op('pe', lambda e, ps_=ps_, j=j, extra=extra: e.matmul(ps_[:, :], lhsT=KwT[0:64, g, j * 128:(j + 1) * 128], rhs=qr64, start=True, stop=(extra is None)),
                                   reads=[BKwT, BQT[i]], writes=[Bs_])
                                if extra is not None:
                                    op('pe', lambda e, ps_=ps_, extra=extra: e.matmul(ps_[:, :], lhsT=ident[:, :], rhs=extra[:, :, :], start=False, stop=True),
                                       reads=[Bident, Bex], writes=[Bs_])
                                op('act', lambda e, ps_=ps_, jj=jj: e.activation(out=PwT[jj][:, :], in_=ps_[:, :], func=AF.Exp, scale=0.125), reads=[Bs_], writes=[BPwT[jj]])
                            for h in range(4):
                                op('pe', lambda e, h=h: e.matmul(pOC[:, h * 97:(h + 1) * 97], lhsT=PcT[0:127, h * 128:(h + 1) * 128], rhs=Vc[0:127, g, :], start=True, stop=True),
                                   reads=[BPcT, BVc], writes=[BOC])
                            for h in range(4):
                                for jj, j in enumerate(wj):
                                    op('pe', lambda e, h=h, jj=jj, j=j: e.matmul(pOW[:, h * 65:(h + 1) * 65], lhsT=PwT[jj][:, h * 128:(h + 1) * 128], rhs=Vw[:, j, g, :],
                                                                                 start=(jj == 0), stop=(jj == len(wj) - 1)), reads=[BPwT[jj], BVw], writes=[BOW])
                            oc = pOC[:, 0:388].rearrange("p (h c) -> p h c", c=97)
                            op('dve', lambda e: e.tensor_scalar(out=rcp[:, 0, :].unsqueeze(2), in0=oc[:, :, 32:33], scalar1=1e-30, scalar2=None, op0=ALU.max),
                               reads=[BOC], writes=[Brcp])
                            op('dve', lambda e: e.reciprocal(out=rcp[:, 0, :], in_=rcp[:, 0, :]), reads=[Brcp], writes=[Brcp])
                            op('dve', lambda e: e.tensor_tensor(out=sel[:, 0:128].rearrange("p (h b) -> p h b", h=4), in0=oc[:, :, 0:32],
                                                                in1=rcp[:, 0, :].unsqueeze(2).broadcast_to([128, 4, 32]), op=ALU.mult), reads=[BOC, Brcp], writes=[Bsel])
                            op('dve', lambda e: e.tensor_reduce(out=sel[:, 128:160], in_=sel[:, 0:128].rearrange("p (h b) -> p b h", h=4), axis=AX.X, op=ALU.add),
                               reads=[Bsel], writes=[Bsel])
                            op('dve', lambda e, i=i: e.tensor_tensor(out=sel[:, 128:160], in0=sel[:, 128:160], in1=seladd_i, op=ALU.add), reads=[Bsel, Bseladd], writes=[Bsel])
                            op('dve', lambda e: e.max(out=m8[:, 0:8], in_=sel[:, 128:160]), reads=[Bsel], writes=[Bm8])
                            op('dve', lambda e: e.match_replace(out=sc2[:, :], in_to_replace=m8[:, 0:8], in_values=sel[:, 128:160], imm_value=-3.0e38),
                               reads=[Bsel, Bm8], writes=[Bsc2])
                            op('dve', lambda e: e.max(out=m8[:, 8:16], in_=sc2[:, :]), reads=[Bsc2], writes=[Bm8])
                            op('dve', lambda e: e.tensor_scalar(out=sc2[:, :], in0=sel[:, 128:160], scalar1=m8[:, 15:16], scalar2=None, op0=ALU.is_ge), reads=[Bsel, Bm8], writes=[Bsc2])
                            op('dve', lambda e: e.tensor_scalar(out=stg[:, 64:96], in0=sc2[:, :], scalar1=-NEG, scalar2=NEG, op0=ALU.mult, op1=ALU.add), reads=[Bsc2], writes=[Bstg])
                            pt, Bpp = next_pt()
                            op('pe', lambda e, pt=pt: e.transpose(out=pt[:, 0:128], in_=stg[:, :], identity=ident[:, :]), reads=[Bstg, Bident], writes=[Bpp])
                            op('dve', lambda e, pt=pt: e.tensor_copy(out=QT[64:96, 4 * g:4 * g + 4, q0:q0 + 128], in_=pt[64:96, 0:128].unsqueeze(1).broadcast_to([32, 4, 128])),
                               reads=[Bpp], writes=[BQT[i]])
                            for j in range(i + 1):
                                ps_, Bs_ = next_s()
                                op('pe', lambda e, ps_=ps_, j=j: e.matmul(ps_[:, :], lhsT=KsT[0:96, g, j * 128:(j + 1) * 128], rhs=qr96, start=True, stop=(j != i)),
                                   reads=[BKsT, BQT[i]], writes=[Bs_])
                                if j == i:
                                    op('pe', lambda e, ps_=ps_: e.matmul(ps_[:, :], lhsT=ident[:, :], rhs=diag4[:, :, :], start=False, stop=True), reads=[Bident, Bdiag4], writes=[Bs_])
                                op('act', lambda e, ps_=ps_, j=j: e.activation(out=PsT[j][:, :], in_=ps_[:, :], func=AF.Exp, scale=0.125), reads=[Bs_], writes=[BPsT[j]])
                            for h in range(4):
                                for j in range(i + 1):
                                    op('pe', lambda e, h=h, j=j: e.matmul(pOS[:, h * 65:(h + 1) * 65], lhsT=PsT[j][:, h * 128:(h + 1) * 128], rhs=Vs[:, j, g, :],
                                                                          start=(j == 0), stop=(j == i)), reads=[BPsT[j], BVs], writes=[BOS])
                            osv = pOS[:, 0:260].rearrange("p (h c) -> p h c", c=65); owv = pOW[:, 0:260].rearrange("p (h c) -> p h c", c=65)
                            op('dve', lambda e: e.tensor_scalar(out=rcp[:, 1, :].unsqueeze(2), in0=osv[:, :, 64:65], scalar1=1e-30, scalar2=None, op0=ALU.max), reads=[BOS], writes=[Brcp])
                            op('dve', lambda e: e.tensor_scalar(out=rcp[:, 2, :].unsqueeze(2), in0=owv[:, :, 64:65], scalar1=1e-30, scalar2=None, op0=ALU.max), reads=[BOW], writes=[Brcp])
                            op('dve', lambda e: e.reciprocal(out=rcp[:, 1:3, :], in_=rcp[:, 1:3, :]), reads=[Brcp], writes=[Brcp])
                            op('dve', lambda e, i=i, g=g: e.tensor_tensor(out=rcp[:, :, :], in0=rcp[:, :, :], in1=gates[:, i, :].rearrange("p (b h) -> p b h", b=3)[:, :, 4 * g:4 * g + 4],
                                                                          op=ALU.mult), reads=[Brcp, Bgates], writes=[Brcp])
                            op('dve', lambda e: e.tensor_tensor(out=of[:], in0=oc[:, :, 33:97], in1=rcp[:, 0, :].unsqueeze(2).broadcast_to([128, 4, HD]), op=ALU.mult),
                               reads=[BOC, Brcp], writes=[Bof])
                            op('dve', lambda e: e.tensor_tensor(out=ot[:], in0=osv[:, :, 0:64], in1=rcp[:, 1, :].unsqueeze(2).broadcast_to([128, 4, HD]), op=ALU.mult),
                               reads=[BOS, Brcp], writes=[Bot])
                            op('dve', lambda e: e.tensor_tensor(out=of[:], in0=of[:], in1=ot[:], op=ALU.add), reads=[Bof, Bot], writes=[Bof])
                            op('dve', lambda e: e.tensor_tensor(out=ot[:], in0=owv[:, :, 0:64], in1=rcp[:, 2, :].unsqueeze(2).broadcast_to([128, 4, HD]), op=ALU.mult),
                               reads=[BOW, Brcp], writes=[Bot])
                            op('dve', lambda e, g=g: e.tensor_tensor(out=onsa_tok[:, 4 * g:4 * g + 4, :], in0=of[:], in1=ot[:], op=ALU.add), reads=[Bof, Bot], writes=[Bonsa_tok])
                        pt, Bpp = next_pt()
                        for c in range(4):
                            op('pe', lambda e, c=c, pt=pt: e.transpose(out=pt[:, c * 128:(c + 1) * 128], in_=onsa_tok[:, 2 * c:2 * c + 2, :].rearrange("p h d -> p (h d)"),
                                                                        identity=ident[:, :]), reads=[Bonsa_tok, Bident], writes=[Bpp])
                        op('act', lambda e, pt=pt, q0=q0: e.copy(out=onsaT[:, :, q0:q0 + 128], in_=pt[:, 0:512].rearrange("p (c t) -> p c t", c=4)), reads=[Bpp], writes=[BonsaT])
                    for i_ in range(16):
                        prompt_iter(i_)
                        sample_iter(i_)
                    on16b = q_st[:].rearrange("s j d -> s (j d)"); Bon16b = Bq_st
                    dma('pool', q_st[:].rearrange("s j (a d) -> s (j a) d", a=2), scr[8192:8192 + 8 * NS * HD].rearrange("(h s d) -> s h d", h=8, s=NS), reads=[Bscr], writes=[Bq_st])
                    pt, Bp = next_pt()
                    for c in range(4):
                        op('pe', lambda e, c=c, pt=pt: e.transpose(out=pt[:, c * 128:c * 128 + NS], in_=on16b[:NS, c * 128:(c + 1) * 128], identity=ident[:NS, :NS]), reads=[Bon16b, Bident], writes=[Bp])
                    op('act', lambda e, pt=pt: e.copy(out=onsaT[:, :, T:TT], in_=pt[:, 0:512].rearrange("p (c t) -> p c t", c=4)[:, :, :NS]), reads=[Bp], writes=[BonsaT])
                    kb.barrier()
                    dma('sp', hT[:].rearrange("p k t -> p (k t)"), hts[:, :], reads=[Bhts], writes=BhT)
                    kb.barrier(); esF1.close(); sbF = sbF_keep
                    kb.barrier()
            kb.barrier()
        if phases >= 4:
            dbg_dump('omT', omT, BomT)
        if phases >= 6:
            dbg_dump('onsaT', onsaT, BonsaT)
        csT = sb("csT", [128, 4, TT], BF16); BcsT = Buf("csT")
        op('pool', lambda e: e.memset(csT[:, :, T:TT], 0.0), writes=[BcsT])
        if phases >= 5:
            esD = ExitStack()
            with esD:
                sbD = mk_sb(esD)
                wU = sbD("wU", [128, 8, 1024], BF16); BwU = Buf("wU")
                for k in range(8):
                    dma('pool', wU[:, k, :], w_in_v[:, k, O_UC:O_UC + 1024], writes=[BwU])
                cw34 = sbD("cw34", [34, CC]); Bcw34 = Buf("cw34")
                dma('sp', cw34[0:31, :], conv_w[:, :], writes=[Bcw34])
                dma('sp', cw34[31:32, :], conv_b[:, :], writes=[Bcw34])
                dma('sp', cw34[32:33, :], conv_ln_g[:, :], writes=[Bcw34])
                dma('sp', cw34[33:34, :], conv_ln_b[:, :], writes=[Bcw34])
                cwT = sbD("cwT", [128, 4, 34]); BcwT = Buf("cwT")
                pf, Bp = next_pf()
                for c in range(4):
                    op('pe', lambda e, c=c, pf=pf: e.transpose(out=pf[:, c * 34:(c + 1) * 34], in_=cw34[0:34, c * 128:(c + 1) * 128], identity=identf[0:34, 0:34]),
                       reads=[Bcw34, Bidentf], writes=[Bp])
                op('dve', lambda e, pf=pf: e.tensor_copy(out=cwT[:, :, :], in_=pf[:, 0:136].rearrange("p (c w) -> p c w", c=4)), reads=[Bp], writes=[BcwT])
                glu = sbD("glu", [128, 30 + T + NS]); Bglu_ = Buf("glu")
                cT = sbD("cT", [128, 4, TT]); BcT = Buf("cT")
                sg = [sbD("sg%d" % j, [128, 512]) for j in range(2)]; Bsg = [Buf("sg%d" % j) for j in range(2)]
                op('pool', lambda e: e.memset(glu[:, 0:30], 0.0), writes=[Bglu_])
                if phases >= 8:
                    ccs = sbD("ccs", [30, NS, CC]); Bccs = Buf("ccs")
                    dma('sp', ccs[:, :, :], cache_conv.rearrange("s w c -> w s c"), writes=[Bccs])
                    xp = sbD("xp", [128, NS, 31]); Bxp = Buf("xp")
                for c in range(4):
                    for tb, (c0, cn) in enumerate(TB if phases >= 8 else TB[:4]):
                        (pa, Ba), (pb, Bb) = next_pf(), next_pf()
                        for (pp, Bpp, off) in ((pa, Ba, 0), (pb, Bb, 512)):
                            for k in range(8):
                                op('pe', lambda e, k=k, pp=pp, off=off: e.matmul(pp[:, 0:cn], lhsT=wU[:, k, off + c * 128:off + (c + 1) * 128], rhs=hT[:, k, c0:c0 + cn],
                                                                                start=(k == 0), stop=(k == 7)), reads=BhT[c0 // 128:(c0 + cn + 127) // 128] + [BwU], writes=[Bpp])
                        s_ = sg[tb % 2]; Bs_ = Bsg[tb % 2]
                        op('act', lambda e, pb=pb, s_=s_: e.activation(out=s_[:, 0:cn], in_=pb[:, 0:cn], func=AF.Sigmoid), reads=[Bb], writes=[Bs_])
                        op('dve', lambda e, pa=pa, s_=s_: e.tensor_tensor(out=glu[:, 30 + c0:30 + c0 + cn], in0=pa[:, 0:cn], in1=s_[:, 0:cn], op=ALU.mult),
                           reads=[Ba, Bs_], writes=[Bglu_])
                    op('dve', lambda e, c=c: e.tensor_scalar(out=cT[:, c, 0:T], in0=glu[:, 30:30 + T], scalar1=cwT[:, c, 30:31], scalar2=cwT[:, c, 31:32],
                                                             op0=ALU.mult, op1=ALU.add), reads=[Bglu_, BcwT], writes=[BcT])
                    for w in range(30):
                        op('dve', lambda e, c=c, w=w: e.scalar_tensor_tensor(out=cT[:, c, 0:T], in0=glu[:, w:w + T], scalar=cwT[:, c, w:w + 1], in1=cT[:, c, 0:T],
                                                                             op0=ALU.mult, op1=ALU.add), reads=[Bglu_, BcwT, BcT], writes=[BcT])
                    if phases >= 8:
                        pfc, Bpfc = next_pf()
                        for s_i in range(NS):
                            op('pe', lambda e, s_i=s_i, c=c, pfc=pfc: e.transpose(out=pfc[:, s_i * 30:(s_i + 1) * 30], in_=ccs[0:30, s_i, c * 128:(c + 1) * 128], identity=identf[0:30, 0:30]),
                               reads=[Bccs, Bidentf], writes=[Bpfc])
                        op('dve', lambda e, pfc=pfc: e.tensor_copy(out=xp[:, :, 0:30], in_=pfc[:, 0:480].rearrange("p (s w) -> p s w", w=30)), reads=[Bpfc], writes=[Bxp])
                        op('dve', lambda e: e.tensor_copy(out=xp[:, :, 30:31], in_=glu[:, 30 + T:30 + TT].unsqueeze(2)), reads=[Bglu_], writes=[Bxp])
                        op('dve', lambda e, c=c: e.tensor_tensor(out=xp[:, :, :], in0=xp[:, :, :], in1=cwT[:, c, 0:31].unsqueeze(1).broadcast_to([128, NS, 31]), op=ALU.mult),
                           reads=[Bxp, BcwT], writes=[Bxp])
                        op('dve', lambda e, c=c: e.tensor_reduce(out=cT[:, c, T:TT], in_=xp[:, :, :], axis=AX.X, op=ALU.add), reads=[Bxp], writes=[BcT])
                        op('dve', lambda e, c=c: e.tensor_scalar(out=cT[:, c, T:TT], in0=cT[:, c, T:TT], scalar1=cwT[:, c, 31:32], scalar2=None, op0=ALU.add), reads=[BcT, BcwT], writes=[BcT])
                mean = sbD("mean", [128, 512]); Bmean = Buf("mean")
                var = sbD("var", [128, 512]); Bvar = Buf("var")
                for tb, (c0, cn) in enumerate(TB if phases >= 8 else TB[:4]):
                    (p1_, B1_), (p2_, B2_) = next_pf(), next_pf()
                    for c in range(4):
                        op('pe', lambda e, c=c, p1_=p1_: e.matmul(p1_[:, 0:cn], lhsT=onesf[:, :], rhs=cT[:, c, c0:c0 + cn], start=(c == 0), stop=(c == 3)),
                           reads=[BcT, Bonesf], writes=[B1_])
                    kb.pe_force_inc = True
                    for c in range(4):
                        s_ = sg[c % 2]; Bs_ = Bsg[c % 2]
                        op('act', lambda e, c=c, s_=s_: e.activation(out=s_[:, 0:cn], in_=cT[:, c, c0:c0 + cn], func=AF.Square), reads=[BcT], writes=[Bs_])
                        op('pe', lambda e, c=c, p2_=p2_, s_=s_: e.matmul(p2_[:, 0:cn], lhsT=onesf[:, :], rhs=s_[:, 0:cn], start=(c == 0), stop=(c == 3)),
                           reads=[Bs_, Bonesf], writes=[B2_])
                    kb.pe_force_inc = False
                    op('act', lambda e, p1_=p1_: e.activation(out=mean[:, 0:cn], in_=p1_[:, 0:cn], func=AF.Copy, scale=1.0 / CC), reads=[B1_], writes=[Bmean])
                    op('dve', lambda e: e.tensor_tensor(out=var[:, 0:cn], in0=mean[:, 0:cn], in1=mean[:, 0:cn], op=ALU.mult), reads=[Bmean], writes=[Bvar])
                    op('dve', lambda e, p2_=p2_: e.scalar_tensor_tensor(out=var[:, 0:cn], in0=p2_[:, 0:cn], scalar=1.0 / CC, in1=var[:, 0:cn], op0=ALU.mult, op1=ALU.subtract),
                       reads=[B2_, Bvar], writes=[Bvar])
                    op('dve', lambda e: e.tensor_scalar(out=var[:, 0:cn], in0=var[:, 0:cn], scalar1=EPS, scalar2=None, op0=ALU.add), reads=[Bvar], writes=[Bvar])
                    op('act', lambda e: e.activation(out=var[:, 0:cn], in_=var[:, 0:cn], func=AF.Sqrt), reads=[Bvar], writes=[Bvar])
                    op('dve', lambda e: e.reciprocal(out=var[:, 0:cn], in_=var[:, 0:cn]), reads=[Bvar], writes=[Bvar])
                    for c in range(4):
                        s_ = sg[c % 2]; Bs_ = Bsg[c % 2]
                        op('dve', lambda e, c=c, s_=s_: e.tensor_tensor(out=s_[:, 0:cn], in0=cT[:, c, c0:c0 + cn], in1=mean[:, 0:cn], op=ALU.subtract),
                           reads=[BcT, Bmean], writes=[Bs_])
                        op('dve', lambda e, s_=s_: e.tensor_tensor(out=s_[:, 0:cn], in0=s_[:, 0:cn], in1=var[:, 0:cn], op=ALU.mult), reads=[Bs_, Bvar], writes=[Bs_])
                        op('dve', lambda e, c=c, s_=s_: e.tensor_scalar(out=s_[:, 0:cn], in0=s_[:, 0:cn], scalar1=cwT[:, c, 32:33], scalar2=cwT[:, c, 33:34],
                                                                        op0=ALU.mult, op1=ALU.add), reads=[Bs_, BcwT], writes=[Bs_])
                        op('act', lambda e, c=c, s_=s_: e.activation(out=csT[:, c, c0:c0 + cn], in_=s_[:, 0:cn], func=AF.Silu), reads=[Bs_], writes=[BcsT])
                kb.barrier()


        if phases >= 5:
            dbg_dump('csT', csT, BcsT)

        if phases >= 7:
            esG = ExitStack()
            with esG:
                sbG = mk_sb(esG)
                gffn, Bgffn = load_const(sbG, "gffn", [128, D], bc(norm_ffn))
                mT = sbG("mT", [128, 8, TT], BF16); BmT = Buf("mT")
                wON = sbG("wON", [128, 4, D], BF16); BwON = Buf("wON")
                wOC = sbG("wOC", [128, 4, D], BF16); BwOC = Buf("wOC")
                wOM = sbG("wOM", [128, 2, D], BF16); BwOM = Buf("wOM")
                dma('pool', wON[:], w_o_nsa.rearrange("(kc p) c -> p kc c", p=128), writes=[BwON])
                dma('pool', wOC[:], w_o_conv.rearrange("(kc p) c -> p kc c", p=128), writes=[BwOC])
                dma('pool', wOM[:], w_o_mem.rearrange("(kc p) c -> p kc c", p=128), writes=[BwOM])
                wGM = [sbG("wGM%d" % j, [128, 8, 3, 128], BF16) for j in range(2)]; BwGM = [Buf("wGM%d" % j) for j in range(2)]
                sgm = sbG("sgm", [128, 3, 512]); Bsgm = Buf("sgm")
                mt1 = sbG("mt1", [128, 512]); Bmt1 = Buf("mt1")
                mt2 = sbG("mt2", [128, 512]); Bmt2 = Buf("mt2")
                for f in range(8):
                    wg, Bwg = wGM[f % 2], BwGM[f % 2]
                    for j in range(3):
                        dma('pool', wg[:, :, j, :], w_in_v[:, :, O_GM + j * D + f * 128:O_GM + j * D + (f + 1) * 128], writes=[Bwg])
                    for tb, (c0, cn) in enumerate(TB):
                        hb = BhT[c0 // 128:(c0 + cn + 127) // 128]
                        pya, Bya = PF[0], BPF[0]; pyb, Byb = PF[1], BPF[1]; pym, Bym = PF[2], BPF[2]
                        for (pp, Bpp, wt, Bwt, src, Bsrc, nk) in ((pya, Bya, wON, BwON, onsaT, BonsaT, 4), (pyb, Byb, wOC, BwOC, csT, BcsT, 4), (pym, Bym, wOM, BwOM, omT, BomT, 2)):
                            for k in range(nk):
                                op('pe', lambda e, k=k, pp=pp, wt=wt, src=src, nk=nk: e.matmul(pp[:, 0:cn], lhsT=wt[:, k, f * 128:(f + 1) * 128], rhs=src[:, k, c0:c0 + cn],
                                                                                            start=(k == 0), stop=(k == nk - 1)), reads=[Bwt, Bsrc], writes=[Bpp])
                        for j in range(3):
                            pg, Bpg = PF[3 + j], BPF[3 + j]
                            for k in range(8):
                                op('pe', lambda e, k=k, pg=pg, j=j: e.matmul(pg[:, 0:cn], lhsT=wg[:, k, j, :], rhs=hT[:, k, c0:c0 + cn], start=(k == 0), stop=(k == 7)),
                                   reads=hb + [Bwg], writes=[Bpg])
                            op('act', lambda e, pg=pg, j=j: e.activation(out=sgm[:, j, 0:cn], in_=pg[:, 0:cn], func=AF.Sigmoid), reads=[Bpg], writes=[Bsgm])
                        op('dve', lambda e: e.tensor_tensor(out=mt1[:, 0:cn], in0=pya[:, 0:cn], in1=sgm[:, 0, 0:cn], op=ALU.mult), reads=[Bya, Bsgm], writes=[Bmt1])
                        op('dve', lambda e: e.tensor_tensor(out=mt2[:, 0:cn], in0=pyb[:, 0:cn], in1=sgm[:, 1, 0:cn], op=ALU.mult), reads=[Byb, Bsgm], writes=[Bmt2])
                        op('dve', lambda e: e.tensor_tensor(out=mt1[:, 0:cn], in0=mt1[:, 0:cn], in1=mt2[:, 0:cn], op=ALU.add), reads=[Bmt1, Bmt2], writes=[Bmt1])
                        op('dve', lambda e: e.tensor_tensor(out=mt2[:, 0:cn], in0=pym[:, 0:cn], in1=sgm[:, 2, 0:cn], op=ALU.mult), reads=[Bym, Bsgm], writes=[Bmt2])
                        op('dve', lambda e: e.tensor_tensor(out=mT[:, f, c0:c0 + cn], in0=mt1[:, 0:cn], in1=mt2[:, 0:cn], op=ALU.add), reads=[Bmt1, Bmt2], writes=[BmT])
                dbg_dump('mT', mT, BmT)
                wO = sbG("wO", [128, 8, D], BF16); BwO = Buf("wO")
                for k in range(8):
                    dma('pool', wO[:, k, :], w_out[k * 128:(k + 1) * 128, :], writes=[BwO])
                x2t = [sbG("x2t%d" % j, [128, D]) for j in range(2)]; Bx2t = [Buf("x2t%d" % j) for j in range(2)]
                for i in range(NTL):
                    n = tsz(i); t0 = i * 128; sl = i % 2
                    x_t, Bx = xt[sl], Bxt[sl]
                    dma('sp', x_t[:n, :], xs[t0:t0 + n, :], writes=[Bx])
                    for cb in range(2):
                        pf, Bp = next_pf()
                        for k in range(8):
                            op('pe', lambda e, k=k, pf=pf, cb=cb: e.matmul(pf[:n, :], lhsT=mT[:, k, t0:t0 + n], rhs=wO[:, k, cb * 512:(cb + 1) * 512], start=(k == 0), stop=(k == 7)),
                               reads=[BmT, BwO], writes=[Bp])
                        op('dve', lambda e, pf=pf, cb=cb: e.tensor_tensor(out=x2t[sl][:n, cb * 512:(cb + 1) * 512], in0=pf[:n, :], in1=x_t[:n, cb * 512:(cb + 1) * 512], op=ALU.add),
                           reads=[Bp, Bx], writes=[Bx2t[sl]])
                    dma('sp', x2s[t0:t0 + n, :], x2t[sl][:n, :], reads=[Bx2t[sl]], writes=[Bx2s[i]])
                    rms_to_T(None, n, gffn, Bgffn, hT, t0, BhT[i], sl, from_sbuf=(x2t[sl], Bx2t[sl]))
                kb.barrier()

            esU = ExitStack()
            actT = sb("actT", [128, NFC, TT], BF16); BactT = Buf("actT")
            with esU:
                sbU = mk_sb(esU)
                fwT = sbU("fwT", [128, NFC, 4]); BfwT = Buf("fwT")
                cfT = sbU("cfT", [128, NFC, 32]); BcfT = Buf("cfT")
                fc36 = [sbU("fc36_%d" % j, [36, 128]) for j in range(2)]; Bfc36 = [Buf("fc36_%d" % j) for j in range(2)]
                cf_v = cache_ffn.rearrange("s j f -> (s j) f")
                for c in range(NFC):
                    t36, B36 = fc36[c % 2], Bfc36[c % 2]
                    dma('sp', t36[0:3, :], ffn_conv_w[:, c * 128:(c + 1) * 128], writes=[B36])
                    dma('sp', t36[3:4, :], ffn_conv_b[:, c * 128:(c + 1) * 128], writes=[B36])
                    dma('sp', t36[4:36, :], cf_v[:, c * 128:(c + 1) * 128], writes=[B36])
                    pf, Bp = next_pf()
                    op('pe', lambda e, pf=pf: e.transpose(out=pf[:, 0:36], in_=t36[0:36, :], identity=identf[0:36, 0:36]), reads=[B36, Bidentf], writes=[Bp])
                    op('dve', lambda e, c=c, pf=pf: e.tensor_copy(out=fwT[:, c, :], in_=pf[:, 0:4]), reads=[Bp], writes=[BfwT])
                    op('dve', lambda e, c=c, pf=pf: e.tensor_copy(out=cfT[:, c, :], in_=pf[:, 4:36]), reads=[Bp], writes=[BcfT])
                ust = sbU("ust", [128, NFC, 18]); Bust = Buf("ust")
                wup = [sbU("wup%d" % j, [128, 8, 2, 128], BF16) for j in range(2)]; Bwup = [Buf("wup%d" % j) for j in range(2)]
                ub = [sbU("ub0", [128, 2 + T + 48])] * 2; Bub = [Buf("ub0")] * 2
                uc = [sbU("uc%d" % j, [128, 512]) for j in range(2)]; Buc = [Buf("uc%d" % j) for j in range(2)]
                op('pool', lambda e: e.memset(ub[0][:, 0:2], 0.0), writes=[Bub[0]])
                w_up_v = w_ffn_up.rearrange("(kc p) c -> p kc c", p=128)
                for c in range(NFC):
                    wu, Bwu = wup[c % 2], Bwup[c % 2]; u_, Bu_ = ub[c % 2], Bub[c % 2]
                    dma('pool', wu[:, :, 0, :], w_up_v[:, :, c * 128:(c + 1) * 128], writes=[Bwu])
                    dma('pool', wu[:, :, 1, :], w_up_v[:, :, DFF + c * 128:DFF + (c + 1) * 128], writes=[Bwu])
                    usv = u_[:, 2 + T:2 + T + 48].rearrange("p (s j) -> p s j", j=3)
                    op('dve', lambda e, c=c, usv=usv: e.tensor_copy(out=usv[:, :, 0:2], in_=cfT[:, c, :].rearrange("p (s j) -> p s j", j=2)), reads=[BcfT], writes=[Bu_])
                    for tb, (c0, cn) in enumerate(TB):
                        hb = BhT[c0 // 128:(c0 + cn + 127) // 128]
                        (pu, Bpu), (pv, Bpv) = next_pf(), next_pf()
                        for (pp, Bpp, jj) in ((pu, Bpu, 0), (pv, Bpv, 1)):
                            for k in range(8):
                                op('pe', lambda e, k=k, pp=pp, jj=jj: e.matmul(pp[:, 0:cn], lhsT=wu[:, k, jj, :], rhs=hT[:, k, c0:c0 + cn], start=(k == 0), stop=(k == 7)),
                                   reads=hb + [Bwu], writes=[Bpp])
                        if tb < 4:
                            cur = u_[:, 2 + c0:2 + c0 + cn]; m1 = u_[:, 1 + c0:1 + c0 + cn]; m2 = u_[:, c0:c0 + cn]
                        else:
                            cur = usv[:, :, 2]; m1 = usv[:, :, 1]; m2 = usv[:, :, 0]
                        t_, Bt_ = uc[tb % 2], Buc[tb % 2]
                        op('act', lambda e, pu=pu, cur=cur: e.copy(out=cur, in_=pu[:, 0:cn]), reads=[Bpu], writes=[Bu_])
                        op('dve', lambda e, c=c, cur=cur, t_=t_: e.tensor_scalar(out=t_[:, 0:cn], in0=cur, scalar1=fwT[:, c, 2:3], scalar2=fwT[:, c, 3:4], op0=ALU.mult, op1=ALU.add),
                           reads=[Bu_, BfwT], writes=[Bt_])
                        op('dve', lambda e, c=c, m1=m1, t_=t_: e.scalar_tensor_tensor(out=t_[:, 0:cn], in0=m1, scalar=fwT[:, c, 1:2], in1=t_[:, 0:cn], op0=ALU.mult, op1=ALU.add),
                           reads=[Bu_, BfwT, Bt_], writes=[Bt_])
                        op('dve', lambda e, c=c, m2=m2, t_=t_: e.scalar_tensor_tensor(out=t_[:, 0:cn], in0=m2, scalar=fwT[:, c, 0:1], in1=t_[:, 0:cn], op0=ALU.mult, op1=ALU.add),
                           reads=[Bu_, BfwT, Bt_], writes=[Bt_])
                        op('act', lambda e, t_=t_: e.activation(out=t_[:, 0:cn], in_=t_[:, 0:cn], func=AF.Gelu_apprx_tanh), reads=[Bt_], writes=[Bt_])
                        op('dve', lambda e, c=c, pv=pv, t_=t_: e.tensor_tensor(out=actT[:, c, c0:c0 + cn], in0=pv[:, 0:cn], in1=t_[:, 0:cn], op=ALU.mult), reads=[Bpv, Bt_], writes=[BactT])
                    op('dve', lambda e, c=c: e.tensor_copy(out=ust[:, c, 0:2], in_=u_[:, T:T + 2]), reads=[Bu_], writes=[Bust])
                    op('dve', lambda e, c=c, usv=usv: e.tensor_copy(out=ust[:, c, 2:18], in_=usv[:, :, 2]), reads=[Bu_], writes=[Bust])
                fst = [sbU("fst%d" % j, [18, 128]) for j in range(2)]; Bfst = [Buf("fst%d" % j) for j in range(2)]
                for c in range(NFC):
                    pf, Bp = next_pf()
                    op('pe', lambda e, c=c, pf=pf: e.transpose(out=pf[0:18, 0:128], in_=ust[:, c, :], identity=identf[:, :]), reads=[Bust, Bidentf], writes=[Bp])
                    op('act', lambda e, c=c, pf=pf: e.copy(out=fst[c % 2][0:18, :], in_=pf[0:18, 0:128]), reads=[Bp], writes=[Bfst[c % 2]])
                    dma('sp', o_ffnp[:, c * 128:(c + 1) * 128], fst[c % 2][0:2, :], reads=[Bfst[c % 2]], is_output=True)
                    dma('sp', o_ffns[:, 1, c * 128:(c + 1) * 128], fst[c % 2][2:18, :], reads=[Bfst[c % 2]], is_output=True)
                kb.barrier()

            esW = ExitStack()
            with esW:
                sbW = mk_sb(esW)
                wd_extra = sbW("wd_extra", [128, 2, D], BF16)
                cs_flat = csT[:].rearrange("p a b -> p (a b)")[:, 0:8 * D].rearrange("p (c n) -> p c n", c=8)
                on_flat = onsaT[:].rearrange("p a b -> p (a b)")[:, 0:8 * D].rearrange("p (c n) -> p c n", c=8)
                om_flat = omT[:].rearrange("p a b -> p (a b)")[:, 0:4 * D].rearrange("p (c n) -> p c n", c=4)
                Bwd = Buf("wdn")

                def wd(c):
                    if c < 8:
                        return cs_flat[:, c, :]
                    if c < 16:
                        return on_flat[:, c - 8, :]
                    if c < 20:
                        return om_flat[:, c - 16, :]
                    return wd_extra[:, c - 20, :]
                for c in range(NFC):
                    dma('pool', wd(c), w_ffn_down[c * 128:(c + 1) * 128, :], writes=[Bwd])
                yt = [sbW("yt%d" % j, [128, D]) for j in range(2)]; Byt = [Buf("yt%d" % j) for j in range(2)]
                for i in range(NTL):
                    n = tsz(i); t0 = i * 128; sl = i % 2
                    x_t, Bx = xt[sl], Bxt[sl]
                    dma('sp', x_t[:n, :], x2s[t0:t0 + n, :], reads=[Bx2s[i]], writes=[Bx])
                    for cb in range(2):
                        pf, Bp = next_pf()
                        for c in range(NFC):
                            op('pe', lambda e, c=c, pf=pf, cb=cb: e.matmul(pf[:n, :], lhsT=actT[:, c, t0:t0 + n], rhs=wd(c)[:, cb * 512:(cb + 1) * 512], start=(c == 0), stop=(c == NFC - 1)),
                               reads=[BactT, Bwd], writes=[Bp])
                        op('dve', lambda e, pf=pf, cb=cb: e.tensor_tensor(out=yt[sl][:n, cb * 512:(cb + 1) * 512], in0=pf[:n, :], in1=x_t[:n, cb * 512:(cb + 1) * 512], op=ALU.add),
                           reads=[Bp, Bx], writes=[Byt[sl]])
                    dma('sp', o_y[t0:t0 + n, :], yt[sl][:n, :], reads=[Byt[sl]], is_output=True)

        kb.finish()
    return nc


_NC_CACHE = {}


def kernel(**inputs):
    phases = PHASES
    f32 = np.float32
    consts = _consts()
    if phases not in _NC_CACHE:
        _NC_CACHE[phases] = build(phases)
    nc = _NC_CACHE[phases]
    cn = np.ascontiguousarray(inputs['cache_nsa'][0].reshape(2560 * 128, 512))
    shared = {}
    for k in ['norm_attn', 'w_in', 'q_norm', 'w_o_nsa', 'conv_w', 'conv_b', 'conv_ln_g', 'conv_ln_b', 'w_o_conv', 'norm_mem',
              'w_mem_kv', 'mq_norm', 'mk_norm', 'w_o_mem', 'w_out', 'norm_ffn', 'w_ffn_up', 'ffn_conv_w', 'ffn_conv_b', 'w_ffn_down']:
        a = np.asarray(inputs[k])[0]
        if a.ndim == 1:
            a = a[None, :]
        shared[k] = np.ascontiguousarray(a, dtype=f32)
    shared['k_norm'] = np.ascontiguousarray(inputs['k_norm'][0], dtype=f32)
    shared['cmp_pe'] = np.ascontiguousarray(inputs['cmp_pe'][0], dtype=f32)
    shared['w_cmp'] = np.ascontiguousarray(inputs['w_cmp'][0], dtype=f32)
    shared['cache_nsa'] = cn if phases >= 9 else cn[:128]
    for k, v in consts.items():
        shared['c_' + k] = v
    in_maps = []
    for c in range(8):
        m = dict(shared)
        s0 = c * NS
        m['xs'] = np.ascontiguousarray(np.concatenate([inputs['x_prompt'][c], inputs['x_sample'][s0:s0 + NS, 0]], axis=0), dtype=f32)
        m['mem'] = np.ascontiguousarray(inputs['mem_prompt'][c], dtype=f32)
        m['cache_win'] = np.ascontiguousarray(inputs['cache_win'][0, s0:s0 + NS].reshape(NS, 512, 256))
        m['cache_conv'] = np.ascontiguousarray(inputs['cache_conv'][0, s0:s0 + NS])
        m['cache_ffn'] = np.ascontiguousarray(inputs['cache_ffn'][0, s0:s0 + NS])
        m['cache_mem'] = np.ascontiguousarray(inputs['cache_mem'][0, s0:s0 + NS].reshape(NS, 256, 512))
        m['ptab'] = np.ascontiguousarray(inputs['page_table'][s0:s0 + NS].reshape(1, 256).astype(np.int32))
        in_maps.append(m)
    res = run_bass_kernel_spmd(nc, in_maps, core_ids=list(range(8)))
    R = res.results

    def cat(name):
        return np.stack([np.asarray(r[name]) for r in R], axis=0)
    y = cat('o_y'); rows = cat('o_rows')
    y_p = y[:, :T]; y_s = y[:, T:].reshape(128, 1, D)
    rows_p = rows[:, :T].reshape(1, 8, T, 4, 2, 64); rows_s = rows[:, T:].reshape(1, 128, 1, 4, 2, 64)
    win_p = cat('o_winp').reshape(1, 8, 512, 2, 2, 64)
    win_s = cat('o_wins').reshape(1, 128, 512, 2, 2, 64)
    conv_p = cat('o_convp').reshape(1, 8, 30, CC); conv_s = cat('o_convs').reshape(1, 128, 30, CC)
    ffn_p = cat('o_ffnp').reshape(1, 8, 2, DFF); ffn_s = cat('o_ffns').reshape(1, 128, 2, DFF)
    memkv = cat('o_memkv').reshape(1, 8, 256, 2, 4, 64)
    return tuple(np.ascontiguousarray(a, dtype=f32) for a in
                 (y_p, y_s, rows_p, rows_s, win_p, win_s, conv_p, conv_s, ffn_p, ffn_s, memkv))
```

```python
import numpy as np
from contextlib import ExitStack
import concourse.bass as bass
import concourse.mybir as mybir
from concourse.bass_utils import run_bass_kernel_spmd

F32 = mybir.dt.float32; BF16 = mybir.dt.bfloat16; I32 = mybir.dt.int32
AF = mybir.ActivationFunctionType; ALU = mybir.AluOpType; AX = mybir.AxisListType

D = 1024; T = 2048; NS = 16; TT = T + NS; NTL = 17
HD = 64; NH = 8; NKV = 2
DFF = 2816; NFC = 22
CC = 512
EPS = 1e-6
NEG = -30000.0
IN_COLS = 5656
O_KV = 512; O_GA = 1280; O_UC = 1304; O_QM = 2328; O_GM = 2584
PHASES = 9
DBG = False
NPHYS = 2560
TILES = None
STOP = -1


class Buf:
    __slots__ = ('name', 'w', 'r', 'rd')

    def __init__(self, name):
        self.name = name; self.w = None; self.r = {}; self.rd = []


class _PEProxy:
    def __init__(self, eng):
        self.eng = eng; self.last = True

    def matmul(self, *a, **kw):
        self.last = bool(kw.get('stop', True))
        return self.eng.matmul(*a, **kw)

    def transpose(self, *a, **kw):
        self.last = True
        return self.eng.transpose(*a, **kw)


class KB:
    def __init__(self, nc, es, n_dsem=20):
        self.nc = nc; self.es = es
        self.E = dict(pe=nc.tensor, dve=nc.vector, act=nc.scalar, pool=nc.gpsimd, sp=nc.sync)
        self.sem = {e: es.enter_context(nc.semaphore('s_' + e)) for e in ['pe', 'dve', 'act', 'pool']}
        self.cnt = {e: 0 for e in self.sem}
        self.waited = {}
        self.dsem = {}; self.dval = {}; self.dnext = {}
        for q in ['sp', 'pool']:
            self.dsem[q] = [es.enter_context(nc.semaphore('d_%s%d' % (q, i))) for i in range(n_dsem)]
            self.dval[q] = [0] * n_dsem; self.dnext[q] = 0
        self.out_events = []

    def _wait(self, eng, key, sem, val):
        if self.waited.get((eng, key), 0) >= val:
            return
        self.E[eng].wait_ge(sem, val); self.waited[(eng, key)] = val

    def _dep(self, eng, ev, raw):
        kind, key, val = ev
        if kind == 'e':
            if key == eng and eng == 'pe':
                return
            self._wait(eng, key, self.sem[key], val)
        else:
            q, i = key
            self._wait(eng, key, self.dsem[q][i], val)

    def deps(self, eng, reads, writes):
        for b in reads:
            if b.w is not None:
                self._dep(eng, b.w, True)
        for b in writes:
            if b.w is not None:
                self._dep(eng, b.w, False)
            for e2, c in b.r.items():
                self._dep(eng, ('e', e2, c), False)
            for ev in b.rd:
                self._dep(eng, ev, False)

    def _record(self, ev, reads, writes):
        for b in reads:
            if ev[0] == 'e':
                b.r[ev[1]] = ev[2]
            else:
                b.rd.append(ev)
        for b in writes:
            b.w = ev; b.r = {}; b.rd = []

    def op(self, eng, fn, reads=(), writes=()):
        self.deps(eng, reads, writes)
        if eng == 'pe':
            px = _PEProxy(self.E[eng])
            ins = fn(px)
            if not px.last and not getattr(self, 'pe_force_inc', False):
                self._record(('e', eng, self.cnt[eng] + 1), reads, writes)
                return
        else:
            ins = fn(self.E[eng])
        self.cnt[eng] += 1
        ins.then_inc(self.sem[eng], 1)
        self._record(('e', eng, self.cnt[eng]), reads, writes)

    def dma(self, q, out, in_, reads=(), writes=(), is_output=False, fn=None, **kw):
        self.deps(q, reads, writes)
        i = self.dnext[q]; self.dnext[q] = (i + 1) % len(self.dsem[q])
        self._wait(q, (q, i), self.dsem[q][i], self.dval[q][i])
        self.dval[q][i] += 16
        if fn is None:
            ins = self.E[q].dma_start(out=out, in_=in_, **kw)
        else:
            ins = fn(self.E[q])
        ins.then_inc(self.dsem[q][i], 16)
        ev = ('d', (q, i), self.dval[q][i])
        self._record(ev, reads, writes)
        if is_output:
            self.out_events.append(ev)

    def barrier(self):
        for eng in ['pe', 'dve', 'act', 'pool', 'sp']:
            for e2 in self.sem:
                if e2 != eng and self.cnt[e2] > 0:
                    self._wait(eng, e2, self.sem[e2], self.cnt[e2])
            for q in self.dsem:
                for i, v in enumerate(self.dval[q]):
                    if v > 0:
                        self._wait(eng, (q, i), self.dsem[q][i], v)

    def finish(self, eng='sp'):
        last = {}
        for kind, key, val in self.out_events:
            last[key] = max(last.get(key, 0), val)
        for key, val in last.items():
            q, i = key
            self.E[eng].wait_ge(self.dsem[q][i], val)


def _consts():
    c = {}
    c['ident'] = np.eye(128, dtype=np.float32)
    inv = (np.float32(500000.0) ** (-np.arange(0, 16, 2, dtype=np.float32) / np.float32(16))).astype(np.float32)

    def cs(pos):
        ang = pos.astype(np.float32)[:, None] * inv[None, :]
        return np.concatenate([np.cos(ang), np.sin(ang)], axis=1).astype(np.float32)
    pos = np.concatenate([np.arange(T), np.full(NS, T)]).astype(np.float32)
    csp = np.zeros((NTL * 128, 16), np.float32); csp[:TT] = cs(pos)
    c['cs_tok'] = np.ascontiguousarray(csp.reshape(NTL, 128, 16).transpose(1, 0, 2))
    cc = np.zeros((128, 16), np.float32); cc[:127] = cs(np.arange(127) * 16 + 31.0)
    c['cs_cmp'] = cc
    p = np.arange(128)
    c['diag'] = np.where(p[:, None] <= p[None, :], 0.0, NEG).astype(np.float32)
    c['winlow'] = np.where(p[:, None] >= p[None, :], 0.0, NEG).astype(np.float32)
    n = np.arange(128)[None, :, None]; i = np.arange(16)[:, None, None]; q = np.arange(128)[None, None, :]
    cb = np.where(16 * n + 31 <= 128 * i + q, 0.0, NEG).astype(np.float32)
    c['cmpbias'] = np.ascontiguousarray(cb.transpose(1, 0, 2))
    t = np.arange(T); qb = t // 64; b = np.arange(32)
    forced = (b[None, :] == 0) | (b[None, :] == qb[:, None]) | (b[None, :] == qb[:, None] - 1)
    valid = b[None, :] <= qb[:, None]
    sa = np.where(forced, 1e9, np.where(valid, 0.0, -1e9)).astype(np.float32)
    c['seladd'] = np.ascontiguousarray(sa.reshape(16, 128, 32).transpose(1, 0, 2))
    sas = np.zeros((8, 32), np.float32); sas[:, 0] = 1e9; sas[:, 31] = 1e9
    c['seladd_s'] = sas
    c['E'] = (t[None, :] // 64 == b[:, None]).astype(np.float32)
    cs_ = np.arange(127) * 16
    ov = ((cs_[:, None] < b[None, :] * 64 + 64) & (cs_[:, None] + 32 > b[None, :] * 64)).astype(np.float32)
    ovp = np.zeros((128, 33), np.float32); ovp[:127, :32] = ov; ovp[:127, 32] = 1.0
    c['ovl'] = ovp
    G = (np.arange(8)[:, None] // 4 == np.arange(8)[None, :] // 4).astype(np.float32)
    c['G8'] = G
    dm = np.full((16, 16, 8), NEG, np.float32)
    for s in range(16):
        dm[s, s, :] = 0.0
    c['dmask'] = dm.reshape(16, 128)
    c['hm8'] = np.array([[1.0, 0.0]] * 4 + [[0.0, 1.0]] * 4, dtype=np.float32)
    return c


def build(phases=PHASES):
    nc = bass.Bass("TRN2", target_bir_lowering=False)

    def din(name, shape, dt=F32):
        return nc.dram_tensor(name, list(shape), dt, kind="ExternalInput").ap()

    def dout(name, shape):
        return nc.dram_tensor(name, list(shape), F32, kind="ExternalOutput").ap()

    xs = din("xs", [TT, D]); mem = din("mem", [256, D])
    cache_nsa = din("cache_nsa", [NPHYS * 128 if phases >= 9 else 128, 512])
    cache_win = din("cache_win", [NS, 512, 256])
    cache_conv = din("cache_conv", [NS, 30, CC])
    cache_ffn = din("cache_ffn", [NS, 2, DFF])
    cache_mem = din("cache_mem", [NS, 256, 512])
    ptab = din("ptab", [1, 256], I32)
    norm_attn = din("norm_attn", [1, D]); w_in = din("w_in", [D, IN_COLS])
    q_norm = din("q_norm", [1, HD]); k_norm = din("k_norm", [3, HD])
    cmp_pe = din("cmp_pe", [2, 32, HD]); w_cmp = din("w_cmp", [2, 32, HD, HD])
    w_o_nsa = din("w_o_nsa", [512, D])
    conv_w = din("conv_w", [31, CC]); conv_b = din("conv_b", [1, CC])
    conv_ln_g = din("conv_ln_g", [1, CC]); conv_ln_b = din("conv_ln_b", [1, CC])
    w_o_conv = din("w_o_conv", [CC, D])
    norm_mem = din("norm_mem", [1, D]); w_mem_kv = din("w_mem_kv", [D, 512])
    mq_norm = din("mq_norm", [1, HD]); mk_norm = din("mk_norm", [1, HD])
    w_o_mem = din("w_o_mem", [256, D]); w_out = din("w_out", [D, D])
    norm_ffn = din("norm_ffn", [1, D]); w_ffn_up = din("w_ffn_up", [D, 2 * DFF])
    ffn_conv_w = din("ffn_conv_w", [3, DFF]); ffn_conv_b = din("ffn_conv_b", [1, DFF])
    w_ffn_down = din("w_ffn_down", [DFF, D])
    C = {k: din("c_" + k, v.shape) for k, v in _consts().items()}

    o_y = dout("o_y", [TT, D]); o_rows = dout("o_rows", [TT, 512])
    o_winp = dout("o_winp", [512, 256]); o_wins = dout("o_wins", [NS, 512, 256])
    o_convp = dout("o_convp", [30, CC]); o_convs = dout("o_convs", [NS, 30, CC])
    o_ffnp = dout("o_ffnp", [2, DFF]); o_ffns = dout("o_ffns", [NS, 2, DFF])
    o_memkv = dout("o_memkv", [256, 512])
    x2s = nc.dram_tensor("x2s", [TT, D], F32, kind="Internal").ap()
    Bx2s = [Buf("x2s%d" % i) for i in range(NTL)]
    scr = nc.dram_tensor("scr", [65536], F32, kind="Internal").ap(); Bscr = Buf("scr")
    hts = nc.dram_tensor("hts", [128, 8 * TT], BF16, kind="Internal").ap(); Bhts = Buf("hts")
    dbg = {}
    if DBG:
        for nm, k in [('csT', 4), ('omT', 2), ('onsaT', 4), ('mT', 8)]:
            dbg[nm] = dout("d_" + nm, [128, k, TT])

    w_in_v = w_in.rearrange("(kc p) c -> p kc c", p=128)

    def bc(ap1row):
        return ap1row.partition_broadcast(128).rearrange("p a d -> p (a d)")

    es = ExitStack()
    with es:
        kb = KB(nc, es)
        op = kb.op; dma = kb.dma

        def mk_sb(stack):
            def f(name, shape, dt=F32):
                return stack.enter_context(nc.sbuf_tensor(name, list(shape), dt))
            return f
        sb = mk_sb(es)

        def tsz(i):
            return 128 if i < 16 else NS
        TB = [(j * 512, 512) for j in range(4)] + [(T, NS)]

        PF = [es.enter_context(nc.psum_tensor("pf%d" % i, [128, 512], F32)) for i in range(6)]
        BPF = [Buf("pf%d" % i) for i in range(6)]
        PT = [es.enter_context(nc.psum_tensor("pt%d" % i, [128, 1024], BF16)) for i in range(2)]
        BPT = [Buf("pt%d" % i) for i in range(2)]
        rr = {'pf': 0, 'pt': 0}

        def next_pf():
            i = rr['pf']; rr['pf'] = (i + 1) % 6
            return PF[i], BPF[i]

        def next_pt():
            i = rr['pt']; rr['pt'] = (i + 1) % 2
            return PT[i], BPT[i]

        def load_const(sbf, name, shape, src, dt=F32):
            t = sbf(name, shape, dt); b = Buf(name)
            dma('pool' if dt != F32 else 'sp', t[:], src, writes=[b])
            return t, b

        ident, Bident = load_const(sb, "ident", [128, 128], C['ident'][:, :], BF16)
        identf, Bidentf = load_const(sb, "identf", [128, 128], C['ident'][:, :], F32)
        mhalf = sb("mhalf", [128, 512]); Bmhalf = Buf("mhalf")
        op('pool', lambda e: e.memset(mhalf[:], -0.5), writes=[Bmhalf])
        onesf = sb("onesf", [128, 128]); Bonesf = Buf("onesf")
        op('pool', lambda e: e.memset(onesf[:], 1.0), writes=[Bonesf])

        hT = sb("hT", [128, 8, TT], BF16); BhT = [Buf("hT%d" % i) for i in range(NTL)]
        omT = sb("omT", [128, 2, TT], BF16); BomT = Buf("omT")
        onsaT = sb("onsaT", [128, 4, TT], BF16); BonsaT = Buf("onsaT")
        for t_, b_ in ((omT, BomT), (onsaT, BonsaT)):
            op('pool', lambda e, t_=t_: e.memset(t_[:, :, T:TT], 0.0), writes=[b_])

        st = sb("st", [128, 16]); Bst = Buf("st")
        nsq = sb("nsq", [128, 8, HD]); Bnsq = Buf("nsq")
        nst = sb("nst", [128, 16]); Bnst = Buf("nst")
        rt = sb("rt", [128, 4, 8, 8]); Brt = Buf("rt")
        xt = [sb("xt%d" % i, [128, D]) for i in range(2)]; Bxt = [Buf("xt%d" % i) for i in range(2)]
        xn = [sb("xn0", [128, D], BF16)] * 2; Bxn = [Buf("xn0")] * 2

        def rms_to_T(src_rows, n, gain_t, Bgain, dstT, col0, Bdst, slot, from_sbuf=None):
            if from_sbuf is None:
                x_t, Bx = xt[slot], Bxt[slot]
                dma('sp', x_t[:n, :], src_rows, writes=[Bx])
            else:
                x_t, Bx = from_sbuf
            xn_t, Bn = xn[slot], Bxn[slot]
            op('act', lambda e: e.activation(out=xn_t[:n, :], in_=x_t[:n, :], func=AF.Square, accum_out=st[:n, 0:1]),
               reads=[Bx], writes=[Bn, Bst])
            op('dve', lambda e: e.tensor_scalar(out=st[:n, 0:1], in0=st[:n, 0:1], scalar1=1.0 / D, scalar2=EPS, op0=ALU.mult, op1=ALU.add),
               reads=[Bst], writes=[Bst])
            op('pool', lambda e: e.tensor_tensor(out=st[:n, 1:2], in0=st[:n, 0:1], in1=mhalf[:n, 0:1], op=ALU.pow),
               reads=[Bst, Bmhalf], writes=[Bst])
            op('dve', lambda e: e.scalar_tensor_tensor(out=xn_t[:n, :], in0=x_t[:n, :], scalar=st[:n, 1:2], in1=gain_t[:n, :],
                                                       op0=ALU.mult, op1=ALU.mult), reads=[Bx, Bst, Bgain], writes=[Bn])
            pt, Bp = next_pt()
            for k in range(8):
                op('pe', lambda e, k=k: e.transpose(out=pt[:, k * 128:k * 128 + n], in_=xn_t[:n, k * 128:(k + 1) * 128], identity=ident[:n, :n]),
                   reads=[Bn, Bident], writes=[Bp])
            op('act', lambda e: e.copy(out=dstT[:, :, col0:col0 + n], in_=pt[:, :].rearrange("p (k t) -> p k t", k=8)[:, :, :n]),
               reads=[Bp], writes=[Bdst])

        def headnorm(src, Bsrc, n, H, gain, Bgain, cs_ap=None, Bcs=None):
            op('dve', lambda e: e.tensor_tensor(out=nsq[:n, :H, :], in0=src, in1=src, op=ALU.mult), reads=[Bsrc], writes=[Bnsq])
            op('dve', lambda e: e.tensor_reduce(out=nst[:n, 0:H], in_=nsq[:n, :H, :], axis=AX.X, op=ALU.add), reads=[Bnsq], writes=[Bnst])
            op('dve', lambda e: e.tensor_scalar(out=nst[:n, 0:H], in0=nst[:n, 0:H], scalar1=1.0 / HD, scalar2=EPS, op0=ALU.mult, op1=ALU.add),
               reads=[Bnst], writes=[Bnst])
            op('pool', lambda e: e.tensor_tensor(out=nst[:n, 8:8 + H], in0=nst[:n, 0:H], in1=mhalf[:n, 0:H], op=ALU.pow),
               reads=[Bnst, Bmhalf], writes=[Bnst])
            op('dve', lambda e: e.tensor_tensor(out=src, in0=src, in1=nst[:n, 8:8 + H].unsqueeze(2).broadcast_to([n, H, HD]), op=ALU.mult),
               reads=[Bsrc, Bnst], writes=[Bsrc])
            op('dve', lambda e: e.tensor_tensor(out=src, in0=src, in1=gain[:n, :H, :], op=ALU.mult), reads=[Bsrc, Bgain], writes=[Bsrc])
            if cs_ap is not None:
                a = src[:, :, 0:8]; b = src[:, :, 8:16]
                cos = cs_ap[:, 0:8].unsqueeze(1).broadcast_to([n, H, 8]); sin = cs_ap[:, 8:16].unsqueeze(1).broadcast_to([n, H, 8])
                for j, (u, v) in enumerate([(a, cos), (b, sin), (a, sin), (b, cos)]):
                    op('dve', lambda e, j=j, u=u, v=v: e.tensor_tensor(out=rt[:n, j, :H, :], in0=u, in1=v, op=ALU.mult),
                       reads=[Bsrc, Bcs], writes=[Brt])
                op('dve', lambda e: e.tensor_tensor(out=a, in0=rt[:n, 0, :H, :], in1=rt[:n, 1, :H, :], op=ALU.subtract), reads=[Brt], writes=[Bsrc])
                op('dve', lambda e: e.tensor_tensor(out=b, in0=rt[:n, 2, :H, :], in1=rt[:n, 3, :H, :], op=ALU.add), reads=[Brt], writes=[Bsrc])

        def dbg_dump(name, t, B):
            if DBG:
                for k in range(t.shape[1]):
                    dma('pool', dbg[name][:, k, :], t[:, k, :], reads=[B], is_output=True)

        esT = ExitStack()
        with esT:
            sbT = mk_sb(esT)
            cs_tok, Bcs = load_const(sbT, "cs_tok", [128, NTL, 16], C['cs_tok'][:, :, :])
            cs_cmp, Bcsc = load_const(sbT, "cs_cmp", [128, 16], C['cs_cmp'][:, :])
            gq = sbT("gq", [128, 8, HD]); Bgq = Buf("gq")
            for h in range(8):
                dma('sp', gq[:, h, :], bc(q_norm), writes=[Bgq])
            gk = sbT("gk", [128, 4, HD]); Bgk = Buf("gk")
            for j in range(4):
                dma('sp', gk[:, j, :], bc(k_norm[1 + j // 2:2 + j // 2, :]), writes=[Bgk])
            gkc = sbT("gkc", [128, 2, HD]); Bgkc = Buf("gkc")
            for j in range(2):
                dma('sp', gkc[:, j, :], bc(k_norm[0:1, :]), writes=[Bgkc])
            gmq = sbT("gmq", [128, 4, HD]); Bgmq = Buf("gmq")
            gmk = sbT("gmk", [128, 4, HD]); Bgmk = Buf("gmk")
            for j in range(4):
                dma('sp', gmq[:, j, :], bc(mq_norm), writes=[Bgmq])
                dma('sp', gmk[:, j, :], bc(mk_norm), writes=[Bgmk])

            QT = sbT("QT", [96, 8, TT], BF16); BQT = [Buf("QT%d" % i) for i in range(NTL)]
            KsT = sbT("KsT", [96, 2, T], BF16); BKsT = Buf("KsT")
            KwT = sbT("KwT", [64, 2, T], BF16); BKwT = Buf("KwT")
            KcTr = sbT("KcTr", [128, T], BF16); BKcTr = Buf("KcTr")
            VcTr = sbT("VcTr", [128, T], BF16); BVcTr = Buf("VcTr")
            Vs = sbT("Vs", [128, 16, 2, 65], BF16); BVs = Buf("Vs")
            Vw = sbT("Vw", [128, 16, 2, 65], BF16); BVw = Buf("Vw")
            gates = sbT("gates", [128, NTL, 24]); Bgates = Buf("gates")
            KnT = sbT("KnT", [128, 2, NS], BF16); BKnT = Buf("KnT")
            Vn = sbT("Vn", [NS, 2, 128], BF16); BVn = Buf("Vn")
            q16 = sbT("q16", [NS, 512], BF16); Bq16 = Buf("q16")
            qm16 = sbT("qm16", [NS, 256], BF16); Bqm16 = Buf("qm16")
            KmT = sbT("KmT", [64, 4, 256], BF16); BKmT = Buf("KmT")
            Vm = sbT("Vm", [128, 2, 4, 65], BF16); BVm = Buf("Vm")
            op('pool', lambda e: e.memset(Vs[:, :, :, 64:65], 1.0), writes=[BVs])
            op('pool', lambda e: e.memset(Vw[:, :, :, 64:65], 1.0), writes=[BVw])
            op('pool', lambda e: e.memset(Vm[:, :, :, 64:65], 1.0), writes=[BVm])
            for g in range(2):
                dma('pool', KsT[64:96, g, :], C['E'][:, :], writes=[BKsT])

            esB = ExitStack()
            with esB:
                sbB = mk_sb(esB)
                gmem, Bgmem = load_const(sbB, "gmem", [128, D], bc(norm_mem))
                wMK = sbB("wMK", [128, 8, 512], BF16); BwMK = Buf("wMK")
                dma('pool', wMK[:], w_mem_kv.rearrange("(kc p) c -> p kc c", p=128), writes=[BwMK])
                mT_ = sbB("mT_", [128, 8, 256], BF16); BmT_ = Buf("mT_")
                mk_t = sbB("mk_t", [128, 512]); Bmk = Buf("mk_t")
                mkb = sbB("mkb", [128, 256], BF16); Bmkb = Buf("mkb")
                for i in range(2):
                    rms_to_T(mem[i * 128:(i + 1) * 128, :], 128, gmem, Bgmem, mT_, i * 128, BmT_, i)
                    pf, Bp = next_pf()
                    for k in range(8):
                        op('pe', lambda e, k=k, pf=pf: e.matmul(pf[:, :], lhsT=mT_[:, k, i * 128:(i + 1) * 128], rhs=wMK[:, k, :], start=(k == 0), stop=(k == 7)),
                           reads=[BmT_, BwMK], writes=[Bp])
                    op('act', lambda e, pf=pf: e.copy(out=mk_t[:, :], in_=pf[:, :]), reads=[Bp], writes=[Bmk])
                    headnorm(mk_t[:, 0:256].rearrange("p (h d) -> p h d", h=4), Bmk, 128, 4, gmk, Bgmk)
                    dma('sp', o_memkv[i * 128:(i + 1) * 128, :], mk_t[:, :], reads=[Bmk], is_output=True)
                    op('act', lambda e: e.copy(out=mkb[:, :], in_=mk_t[:, 0:256]), reads=[Bmk], writes=[Bmkb])
                    op('dve', lambda e: e.tensor_copy(out=Vm[:, i, :, 0:64], in_=mk_t[:, 256:512].rearrange("p (h d) -> p h d", h=4)), reads=[Bmk], writes=[BVm])
                    pt, Bp = next_pt()
                    for h in range(4):
                        op('pe', lambda e, h=h, pt=pt: e.transpose(out=pt[0:64, h * 128:(h + 1) * 128], in_=mkb[:, h * 64:(h + 1) * 64], identity=ident[:, :]),
                           reads=[Bmkb, Bident], writes=[Bp])
                    op('dve', lambda e, pt=pt: e.tensor_copy(out=KmT[0:64, :, i * 128:(i + 1) * 128], in_=pt[0:64, 0:512].rearrange("p (k t) -> p k t", k=4)),
                       reads=[Bp], writes=[BKmT])
                kb.barrier()

            esA = ExitStack()
            with esA:
                sbA = mk_sb(esA)
                gattn, Bgattn = load_const(sbA, "gattn", [128, D], bc(norm_attn))
                wA = sbA("wA", [128, 8, 1304], BF16); BwA = Buf("wA")
                for k in range(8):
                    dma('pool', wA[:, k, :], w_in_v[:, k, 0:1304], writes=[BwA])
                wQM = sbA("wQM", [128, 8, 256], BF16); BwQM = Buf("wQM")
                dma('pool', wQM[:], w_in_v[:, :, O_QM:O_QM + 256], writes=[BwQM])
                qf = sbA("qf", [128, 8, HD]); Bqf = Buf("qf")
                qb = sbA("qb", [128, 512], BF16); Bqb = Buf("qb")
                rows_t = [sbA("rows_t0", [128, 512])] * 2; Brows = [Buf("rows0")] * 2
                win_t = [sbA("win_t0", [128, 256])] * 2; Bwin = [Buf("win0")] * 2
                kk = sbA("kk", [128, 4, HD]); Bkk = Buf("kk")
                kkb = sbA("kkb", [128, 4, HD], BF16); Bkkb = Buf("kkb")
                kvcb = sbA("kvcb", [128, 256], BF16); Bkvcb = Buf("kvcb")
                qmf = sbA("qmf", [128, 4, HD]); Bqmf = Buf("qmf")
                qmb = sbA("qmb", [128, 256], BF16); Bqmb = Buf("qmb")
                qmT = sbA("qmT", [64, 4, 128], BF16); BqmT = Buf("qmT")
                PmT = [sbA("PmT%d" % j, [128, 512], BF16) for j in range(2)]; BPmT = [Buf("PmT%d" % j) for j in range(2)]
                omt = sbA("omt", [128, 4, HD], BF16); Bomt = Buf("omt")
                mrec = sbA("mrec", [128, 4]); Bmrec = Buf("mrec")

                for i in (TILES if TILES is not None else list(range(NTL))):
                    n = tsz(i); t0 = i * 128; sl = i % 2
                    rms_to_T(xs[t0:t0 + n, :], n, gattn, Bgattn, hT, t0, BhT[i], sl)
                    zp = []
                    for cbi, (wt, Bw, c0, cw) in enumerate([(wA, BwA, 0, 512), (wA, BwA, 512, 512), (wA, BwA, 1024, 280), (wQM, BwQM, 0, 256)]):
                        pf, Bp = next_pf()
                        for k in range(8):
                            op('pe', lambda e, k=k, pf=pf, wt=wt, c0=c0, cw=cw: e.matmul(pf[:n, 0:cw], lhsT=hT[:, k, t0:t0 + n], rhs=wt[:, k, c0:c0 + cw],
                                                                                         start=(k == 0), stop=(k == 7)),
                               reads=[BhT[i], Bw], writes=[Bp])
                        zp.append((pf, Bp))
                    (p0, B0), (p1, B1), (p2, B2), (p3, B3) = zp
                    op('act', lambda e: e.copy(out=qf[:n].rearrange("p h d -> p (h d)"), in_=p0[:n, 0:512]), reads=[B0], writes=[Bqf])
                    headnorm(qf[:n, :, :], Bqf, n, 8, gq, Bgq, cs_tok[:n, i, :], Bcs)
                    tgt_q, Btq = (qb, Bqb) if i < 16 else (q16, Bq16)
                    op('act', lambda e: e.copy(out=tgt_q[:n, :], in_=qf[:n].rearrange("p h d -> p (h d)")), reads=[Bqf], writes=[Btq])
                    pt, Bp = next_pt()
                    for h in range(8):
                        op('pe', lambda e, h=h, pt=pt: e.transpose(out=pt[0:64, h * 128:h * 128 + n], in_=tgt_q[:n, h * 64:(h + 1) * 64], identity=ident[:n, :n]),
                           reads=[Btq, Bident], writes=[Bp])
                    op('dve', lambda e, pt=pt: e.tensor_copy(out=QT[0:64, :, t0:t0 + n], in_=pt[0:64, :].rearrange("p (k t) -> p k t", k=8)[:, :, :n]),
                       reads=[Bp], writes=[BQT[i]])
                    rw, Brw = rows_t[sl], Brows[sl]; wn, Bwn = win_t[sl], Bwin[sl]
                    op('act', lambda e: e.copy(out=rw[:n, :], in_=p1[:n, 0:512]), reads=[B1], writes=[Brw])
                    op('act', lambda e: e.copy(out=wn[:n, :], in_=p2[:n, 0:256]), reads=[B2], writes=[Bwn])
                    op('act', lambda e: e.activation(out=gates[:n, i, :], in_=p2[:n, 256:280], func=AF.Sigmoid), reads=[B2], writes=[Bgates])
                    op('dve', lambda e: e.tensor_copy(out=kk[:n, 0:2, :], in_=rw[:n, 256:384].rearrange("p (g d) -> p g d", g=2)), reads=[Brw], writes=[Bkk])
                    op('dve', lambda e: e.tensor_copy(out=kk[:n, 2:4, :], in_=wn[:n, 0:128].rearrange("p (g d) -> p g d", g=2)), reads=[Bwn], writes=[Bkk])
                    headnorm(kk[:n, :, :], Bkk, n, 4, gk, Bgk, cs_tok[:n, i, :], Bcs)
                    op('dve', lambda e: e.tensor_copy(out=rw[:n, 256:384].rearrange("p (g d) -> p g d", g=2), in_=kk[:n, 0:2, :]), reads=[Bkk], writes=[Brw])
                    op('dve', lambda e: e.tensor_copy(out=wn[:n, 0:128].rearrange("p (g d) -> p g d", g=2), in_=kk[:n, 2:4, :]), reads=[Bkk], writes=[Bwn])
                    op('act', lambda e: e.copy(out=kkb[:n], in_=kk[:n]), reads=[Bkk], writes=[Bkkb])
                    dma('sp', o_rows[t0:t0 + n, :], rw[:n, :], reads=[Brw], is_output=True)
                    if 12 <= i < 16:
                        dma('sp', o_winp[(i - 12) * 128:(i - 11) * 128, :], wn[:n, :], reads=[Bwn], is_output=True)
                    if i == 16:
                        dma('sp', o_wins[:, 511, :], wn[:n, :], reads=[Bwn], is_output=True)
                    if i < 16:
                        pt, Bp = next_pt()
                        for j in range(4):
                            op('pe', lambda e, j=j, pt=pt: e.transpose(out=pt[0:64, j * 128:(j + 1) * 128], in_=kkb[:n, j, :], identity=ident[:n, :n]),
                               reads=[Bkkb, Bident], writes=[Bp])
                        op('act', lambda e: e.copy(out=kvcb[:n, :], in_=p1[:n, 0:256]), reads=[B1], writes=[Bkvcb])
                        for j in range(2):
                            op('pe', lambda e, j=j, pt=pt: e.transpose(out=pt[:, (4 + j) * 128:(5 + j) * 128], in_=kvcb[:n, j * 128:(j + 1) * 128], identity=ident[:n, :n]),
                               reads=[Bkvcb, Bident], writes=[Bp])
                        op('dve', lambda e, pt=pt: e.tensor_copy(out=KsT[0:64, :, t0:t0 + n], in_=pt[0:64, 0:256].rearrange("p (g t) -> p g t", g=2)),
                           reads=[Bp], writes=[BKsT])
                        op('dve', lambda e, pt=pt: e.tensor_copy(out=KwT[0:64, :, t0:t0 + n], in_=pt[0:64, 256:512].rearrange("p (g t) -> p g t", g=2)),
                           reads=[Bp], writes=[BKwT])
                        op('dve', lambda e, pt=pt: e.tensor_copy(out=KcTr[:, :].rearrange("p (r m) -> p r m", r=16)[:, :, 8 * i:8 * i + 8].rearrange("p r m -> p m r"), in_=pt[:, 512:640].rearrange("p (m r) -> p m r", r=16)), reads=[Bp], writes=[BKcTr])
                        op('dve', lambda e, pt=pt: e.tensor_copy(out=VcTr[:, :].rearrange("p (r m) -> p r m", r=16)[:, :, 8 * i:8 * i + 8].rearrange("p r m -> p m r"), in_=pt[:, 640:768].rearrange("p (m r) -> p m r", r=16)), reads=[Bp], writes=[BVcTr])
                        op('dve', lambda e: e.tensor_copy(out=Vs[:n, i, :, 0:64], in_=p1[:n, 384:512].rearrange("p (g d) -> p g d", g=2)), reads=[B1], writes=[BVs])
                        op('dve', lambda e: e.tensor_copy(out=Vw[:n, i, :, 0:64], in_=p2[:n, 128:256].rearrange("p (g d) -> p g d", g=2)), reads=[B2], writes=[BVw])
                    else:
                        pt, Bp = next_pt()
                        for j in range(2):
                            op('pe', lambda e, j=j, pt=pt: e.transpose(out=pt[:, j * 128:j * 128 + n], in_=kkb[:n, 2 * j:2 * j + 2, :].rearrange("p g d -> p (g d)"),
                                                                        identity=ident[:n, :n]), reads=[Bkkb, Bident], writes=[Bp])
                        op('dve', lambda e, pt=pt: e.tensor_copy(out=KnT[:, :, :], in_=pt[:, 0:256].rearrange("p (j t) -> p j t", j=2)[:, :, :n]),
                           reads=[Bp], writes=[BKnT])
                        op('dve', lambda e: e.tensor_copy(out=Vn[:n, 0, :], in_=p1[:n, 384:512]), reads=[B1], writes=[BVn])
                        op('dve', lambda e: e.tensor_copy(out=Vn[:n, 1, :], in_=p2[:n, 128:256]), reads=[B2], writes=[BVn])
                    op('act', lambda e: e.copy(out=qmf[:n].rearrange("p h d -> p (h d)"), in_=p3[:n, 0:256]), reads=[B3], writes=[Bqmf])
                    headnorm(qmf[:n, :, :], Bqmf, n, 4, gmq, Bgmq)
                    tgt_m, Btm = (qmb, Bqmb) if i < 16 else (qm16, Bqm16)
                    op('act', lambda e: e.copy(out=tgt_m[:n, :], in_=qmf[:n].rearrange("p h d -> p (h d)")), reads=[Bqmf], writes=[Btm])
                    if i < 16 and phases >= 4:
                        pt, Bp = next_pt()
                        for h in range(4):
                            op('pe', lambda e, h=h, pt=pt: e.transpose(out=pt[0:64, h * 128:h * 128 + n], in_=tgt_m[:n, h * 64:(h + 1) * 64], identity=ident[:n, :n]),
                               reads=[Btm, Bident], writes=[Bp])
                        op('dve', lambda e, pt=pt: e.tensor_copy(out=qmT[0:64, :, :n], in_=pt[0:64, 0:512].rearrange("p (k t) -> p k t", k=4)[:, :, :n]),
                           reads=[Bp], writes=[BqmT])
                        for nt in range(2):
                            pf, Bp = next_pf()
                            for h in range(4):
                                op('pe', lambda e, h=h, pf=pf, nt=nt: e.matmul(pf[:, h * 128:h * 128 + n], lhsT=KmT[0:64, h, nt * 128:(nt + 1) * 128], rhs=qmT[0:64, h, :n],
                                                                               start=True, stop=True), reads=[BKmT, BqmT], writes=[Bp])
                            op('act', lambda e, pf=pf, nt=nt: e.activation(out=PmT[nt][:, :], in_=pf[:, :], func=AF.Exp, scale=0.125), reads=[Bp], writes=[BPmT[nt]])
                        pf, Bp = next_pf()
                        for h in range(4):
                            for nt in range(2):
                                op('pe', lambda e, h=h, pf=pf, nt=nt: e.matmul(pf[:n, h * 65:(h + 1) * 65], lhsT=PmT[nt][:, h * 128:h * 128 + n], rhs=Vm[:, nt, h, :],
                                                                               start=(nt == 0), stop=(nt == 1)), reads=[BPmT[nt], BVm], writes=[Bp])
                        pv = pf[:n, 0:260].rearrange("p (h c) -> p h c", c=65)
                        op('dve', lambda e, pv=pv: e.reciprocal(out=mrec[:n, :].unsqueeze(2), in_=pv[:, :, 64:65]), reads=[Bp], writes=[Bmrec])
                        op('dve', lambda e, pv=pv: e.tensor_tensor(out=omt[:n, :, :], in0=pv[:, :, 0:64], in1=mrec[:n, :].unsqueeze(2).broadcast_to([n, 4, HD]), op=ALU.mult),
                           reads=[Bp, Bmrec], writes=[Bomt])
                        pt, Bp = next_pt()
                        for j in range(2):
                            op('pe', lambda e, j=j, pt=pt: e.transpose(out=pt[:, j * 128:j * 128 + n], in_=omt[:n, 2 * j:2 * j + 2, :].rearrange("p h d -> p (h d)"),
                                                                        identity=ident[:n, :n]), reads=[Bomt, Bident], writes=[Bp])
                        op('act', lambda e, pt=pt: e.copy(out=omT[:, :, t0:t0 + n], in_=pt[:, 0:256].rearrange("p (j t) -> p j t", j=2)[:, :, :n]),
                           reads=[Bp], writes=[BomT])
                kb.barrier()

            esC = ExitStack()
            with esC:
                sbC = mk_sb(esC)
                dma('sp', o_wins[:, 0:511, :], cache_win[:, 1:512, :], is_output=True)
                dma('sp', o_convs[:, 0:29, :], cache_conv[:, 1:30, :], is_output=True)
                dma('sp', o_ffns[:, 0, :], cache_ffn[:, 1, :], is_output=True)
                wUk = [sbC("wUk%d" % j, [128, 1024], BF16) for j in range(2)]; BwUk = [Buf("wUk%d" % j) for j in range(2)]
                sig_t = sbC("sig_t", [128, 512]); Bsig = Buf("sig_t")
                glu_t = [sbC("glu_t%d" % i, [128, 512]) for i in range(2)]; Bglu = [Buf("glu_t%d" % i) for i in range(2)]
                pabs = {}
                for i in (15, 16):
                    for cbi in range(2):
                        pabs[(i, cbi)] = next_pf()
                kb.pe_force_inc = True
                for k in range(8):
                    dma('pool', wUk[k % 2][:, :], w_in_v[:, k, O_UC:O_UC + 1024], writes=[BwUk[k % 2]])
                    for i in (15, 16):
                        n = tsz(i); t0 = i * 128
                        for cbi in range(2):
                            pf, Bp = pabs[(i, cbi)]
                            op('pe', lambda e, k=k, pf=pf, cbi=cbi, n=n, t0=t0: e.matmul(pf[:n, :], lhsT=hT[:, k, t0:t0 + n], rhs=wUk[k % 2][:, cbi * 512:(cbi + 1) * 512],
                                                                                       start=(k == 0), stop=(k == 7)), reads=[BhT[i], BwUk[k % 2]], writes=[Bp])
                kb.pe_force_inc = False
                for ii, i in enumerate((15, 16)):
                    n = tsz(i); t0 = i * 128
                    (pa, Ba), (pb, Bb) = pabs[(i, 0)], pabs[(i, 1)]
                    op('act', lambda e, pb=pb: e.activation(out=sig_t[:n, :], in_=pb[:n, :], func=AF.Sigmoid), reads=[Bb], writes=[Bsig])
                    op('dve', lambda e, pa=pa, ii=ii: e.tensor_tensor(out=glu_t[ii][:n, :], in0=pa[:n, :], in1=sig_t[:n, :], op=ALU.mult),
                       reads=[Ba, Bsig], writes=[Bglu[ii]])
                    if i == 15:
                        dma('sp', o_convp[:, :], glu_t[ii][98:128, :], reads=[Bglu[ii]], is_output=True)
                    else:
                        dma('sp', o_convs[:, 29, :], glu_t[ii][:n, :], reads=[Bglu[ii]], is_output=True)
                kb.barrier()

            if phases >= 8:
                esE = ExitStack()
                with esE:
                    sbE = mk_sb(esE)
                    cm = sbE("cm", [128, NS, 2, 512], BF16); Bcm = Buf("cm")
                    for s_i in range(NS):
                        dma('pool', cm[:, s_i, :, :], cache_mem[s_i].rearrange("(nt p) c -> p nt c", p=128), writes=[Bcm])
                    QmP = sbE("QmP", [128, 2, NS], BF16); BQmP = Buf("QmP")
                    pt, Bp = next_pt()
                    for hp in range(2):
                        op('pe', lambda e, hp=hp, pt=pt: e.transpose(out=pt[:, hp * 128:hp * 128 + NS], in_=qm16[:NS, hp * 128:(hp + 1) * 128], identity=ident[:NS, :NS]),
                           reads=[Bqm16, Bident], writes=[Bp])
                    op('dve', lambda e, pt=pt: e.tensor_copy(out=QmP[:, :, :], in_=pt[:, 0:256].rearrange("p (j t) -> p j t", j=2)[:, :, :NS]), reads=[Bp], writes=[BQmP])
                    Qmbd = sbE("Qmbd", [128, NS, 2, 2], BF16); BQmbd = Buf("Qmbd")
                    op('pool', lambda e: e.memset(Qmbd[:].rearrange("p s a b -> p (s a b)"), 0.0), writes=[BQmbd])
                    op('dve', lambda e: e.tensor_copy(out=Qmbd[0:64, :, :, 0], in_=QmP[0:64, :, :].rearrange("p hp s -> p s hp")), reads=[BQmP, BQmbd], writes=[BQmbd])
                    op('dve', lambda e: e.tensor_copy(out=Qmbd[64:128, :, :, 1], in_=QmP[64:128, :, :].rearrange("p hp s -> p s hp")), reads=[BQmP, BQmbd], writes=[BQmbd])
                    KmsT = [sbE("KmsT%d" % j, [128, 2, 256], BF16) for j in range(2)]; BKmsT = [Buf("KmsT%d" % j) for j in range(2)]
                    pSm, BSm = PF[0], BPF[0]
                    for s_i in range(NS):
                        pt, Bp = next_pt()
                        for hp in range(2):
                            for nt in range(2):
                                op('pe', lambda e, hp=hp, nt=nt, pt=pt, s_i=s_i: e.transpose(out=pt[:, (hp * 2 + nt) * 128:(hp * 2 + nt + 1) * 128], in_=cm[:, s_i, nt, hp * 128:(hp + 1) * 128],
                                                                                          identity=ident[:, :]), reads=[Bcm, Bident], writes=[Bp])
                        kt, Bkt = KmsT[s_i % 2], BKmsT[s_i % 2]
                        op('dve', lambda e, pt=pt, kt=kt: e.tensor_copy(out=kt[:, :, :], in_=pt[:, 0:512].rearrange("p (hp n) -> p hp n", hp=2)), reads=[Bp], writes=[Bkt])
                        for nt in range(2):
                            for hp in range(2):
                                c0_ = s_i * 8 + nt * 4 + hp * 2
                                op('pe', lambda e, hp=hp, nt=nt, kt=kt, c0_=c0_, s_i=s_i: e.matmul(pSm[:, c0_:c0_ + 2], lhsT=kt[:, hp, nt * 128:(nt + 1) * 128], rhs=Qmbd[:, s_i, hp, :],
                                                                                                start=True, stop=True), reads=[Bkt, BQmbd], writes=[BSm])
                    PmsT = sbE("PmsT", [128, NS, 2, 4], BF16); BPmsT = Buf("PmsT")
                    op('act', lambda e: e.activation(out=PmsT[:].rearrange("p s a b -> p (s a b)"), in_=pSm[:, 0:128], func=AF.Exp, scale=0.125), reads=[BSm], writes=[BPmsT])
                    Rm = sbE("Rm", [128, NS, 4]); BRm = Buf("Rm")
                    op('dve', lambda e: e.tensor_tensor(out=Rm[:, :, :], in0=PmsT[:, :, 0, :], in1=PmsT[:, :, 1, :], op=ALU.add), reads=[BPmsT], writes=[BRm])
                    pden, Bpden = PF[1], BPF[1]
                    for s_i in range(NS):
                        op('pe', lambda e, s_i=s_i: e.matmul(pden[0:4, s_i:s_i + 1], lhsT=Rm[:, s_i, :], rhs=onesf[:, 0:1], start=True, stop=True), reads=[BRm, Bonesf], writes=[Bpden])
                    rden = sbE("rden", [4, NS]); Brden = Buf("rden")
                    op('dve', lambda e: e.reciprocal(out=rden[:, :], in_=pden[0:4, 0:NS]), reads=[Bpden], writes=[Brden])
                    oms = sbE("oms", [4, NS, HD]); Boms = Buf("oms")
                    otmp = sbE("otmp", [4, 2, 4, HD]); Botmp = Buf("otmp")
                    for s2 in range(NS // 2):
                        pO, BpO = next_pf()
                        for ss in range(2):
                            s_i = 2 * s2 + ss
                            for nt in range(2):
                                op('pe', lambda e, ss=ss, nt=nt, s_i=s_i, pO=pO: e.matmul(pO[0:4, ss * 256:(ss + 1) * 256], lhsT=PmsT[:, s_i, nt, :], rhs=cm[:, s_i, nt, 256:512],
                                                                                       start=(nt == 0), stop=(nt == 1)), reads=[BPmsT, Bcm], writes=[BpO])
                        op('dve', lambda e, pO=pO: e.tensor_tensor(out=otmp[:], in0=pO[0:4, :].rearrange("p (s h d) -> p s h d", s=2, h=4),
                                                                   in1=identf[0:4, 0:4].unsqueeze(1).unsqueeze(3).broadcast_to([4, 2, 4, HD]), op=ALU.mult),
                           reads=[BpO, Bidentf], writes=[Botmp])
                        op('dve', lambda e, s2=s2: e.tensor_reduce(out=oms[:, 2 * s2:2 * s2 + 2, :], in_=otmp[:].rearrange("p s h d -> p s d h"), axis=AX.X, op=ALU.add),
                           reads=[Botmp], writes=[Boms])
                    op('dve', lambda e: e.tensor_tensor(out=oms[:], in0=oms[:], in1=rden[:, :].unsqueeze(2).broadcast_to([4, NS, HD]), op=ALU.mult), reads=[Boms, Brden], writes=[Boms])
                    dma('sp', scr[0:4 * NS * HD].rearrange("(h s d) -> h s d", h=4, s=NS), oms[:, :, :], reads=[Boms], writes=[Bscr])
                    om16 = sbE("om16", [NS, 4, HD]); Bom16 = Buf("om16")
                    dma('sp', om16[:, :, :], scr[0:4 * NS * HD].rearrange("(h s d) -> s h d", h=4, s=NS), reads=[Bscr], writes=[Bom16])
                    om16b = sbE("om16b", [NS, 256], BF16); Bom16b = Buf("om16b")
                    op('dve', lambda e: e.tensor_copy(out=om16b[:, :], in_=om16[:].rearrange("s h d -> s (h d)")), reads=[Bom16], writes=[Bom16b])
                    pt, Bp = next_pt()
                    for j in range(2):
                        op('pe', lambda e, j=j, pt=pt: e.transpose(out=pt[:, j * 128:j * 128 + NS], in_=om16b[:NS, j * 128:(j + 1) * 128], identity=ident[:NS, :NS]),
                           reads=[Bom16b, Bident], writes=[Bp])
                    op('act', lambda e, pt=pt: e.copy(out=omT[:, :, T:TT], in_=pt[:, 0:256].rearrange("p (j t) -> p j t", j=2)[:, :, :NS]), reads=[Bp], writes=[BomT])
                    kb.barrier()

            if phases >= 6:
                esF = ExitStack()
                with esF:
                    sbF = mk_sb(esF)
                    KcT = sbF("KcT", [64, 2, 128], BF16); BKcT = Buf("KcT")
                    Vc = sbF("Vc", [128, 2, 97], BF16); BVc = Buf("Vc")
                    esF1 = ExitStack(); esF1.__enter__(); sbF_keep = sbF; sbF = mk_sb(esF1)
                    W2 = sbF("W2", [128, 2, 32, 128], BF16); BW2 = Buf("W2")
                    op('pool', lambda e: e.memset(W2[:].rearrange("p a l e -> p (a l e)"), 0.0), writes=[BW2])
                    wc_v = w_cmp.rearrange("kv l d e -> d kv l e")
                    for kv_ in range(2):
                        dma('pool', W2[0:64, kv_, :, 0:64], wc_v[:, kv_, :, :], reads=[BW2], writes=[BW2])
                        dma('pool', W2[64:128, kv_, :, 64:128], wc_v[:, kv_, :, :], reads=[BW2], writes=[BW2])
                    pe2 = sbF("pe2", [64, 128]); Bpe2 = Buf("pe2")
                    pe_v = cmp_pe.rearrange("kv l d -> (kv l) d")
                    dma('sp', pe2[:, 0:64], pe_v, writes=[Bpe2]); dma('sp', pe2[:, 64:128], pe_v, writes=[Bpe2])
                    peT2 = sbF("peT2", [128, 64], BF16); BpeT2 = Buf("peT2")
                    pf, Bp = next_pf()
                    op('pe', lambda e, pf=pf: e.transpose(out=pf[:, 0:64], in_=pe2[:, :], identity=identf[0:64, 0:64]), reads=[Bpe2, Bidentf], writes=[Bp])
                    op('dve', lambda e, pf=pf: e.tensor_copy(out=peT2[:, :], in_=pf[:, 0:64]), reads=[Bp], writes=[BpeT2])
                    cpe = sbF("cpe", [1, 2, 128]); Bcpe = Buf("cpe")
                    cpeb = sbF("cpeb", [128, 2, 128]); Bcpeb = Buf("cpeb")
                    for kv_ in range(2):
                        pf, Bp = next_pf()
                        for l in range(32):
                            op('pe', lambda e, l=l, pf=pf, kv_=kv_: e.matmul(pf[0:1, 0:128], lhsT=peT2[:, kv_ * 32 + l:kv_ * 32 + l + 1], rhs=W2[:, kv_, l, :],
                                                                             start=(l == 0), stop=(l == 31)), reads=[BpeT2, BW2], writes=[Bp])
                        op('dve', lambda e, pf=pf, kv_=kv_: e.tensor_copy(out=cpe[0:1, kv_, :], in_=pf[0:1, 0:128]), reads=[Bp], writes=[Bcpe])
                        pf2, Bp2 = next_pf()
                        op('pe', lambda e, pf2=pf2, kv_=kv_: e.matmul(pf2[:, 0:128], lhsT=onesf[0:1, :], rhs=cpe[0:1, kv_, :], start=True, stop=True),
                           reads=[Bcpe, Bonesf], writes=[Bp2])
                        op('dve', lambda e, pf2=pf2, kv_=kv_: e.tensor_copy(out=cpeb[:, kv_, :], in_=pf2[:, 0:128]), reads=[Bp2], writes=[Bcpeb])
                    kcf = sbF("kcf", [128, 2, HD]); Bkcf = Buf("kcf")
                    kcb = sbF("kcb", [128, 2, HD], BF16); Bkcb = Buf("kcb")
                    ovl, Bovl = load_const(sbF, "ovl", [128, 33], C['ovl'][:, :])
                    for g in range(2):
                        op('dve', lambda e, g=g: e.tensor_copy(out=Vc[:, g, 0:33], in_=ovl[:, :]), reads=[Bovl], writes=[BVc])
                    for kv_, (src, Bsrc) in enumerate(((KcTr, BKcTr), (VcTr, BVcTr))):
                        pf, Bp = next_pf()
                        for l in range(32):
                            op('pe', lambda e, l=l, pf=pf, kv_=kv_, src=src: e.matmul(pf[0:127, 0:128], lhsT=src[:, (l % 16) * 128 + l // 16:(l % 16) * 128 + l // 16 + 127], rhs=W2[:, kv_, l, :],
                                                                                      start=(l == 0), stop=(l == 31)), reads=[Bsrc, BW2], writes=[Bp])
                        if kv_ == 0:
                            op('dve', lambda e, pf=pf: e.tensor_tensor(out=kcf[0:127].rearrange("p g d -> p (g d)"), in0=pf[0:127, 0:128], in1=cpeb[0:127, 0, :], op=ALU.add),
                               reads=[Bp, Bcpeb], writes=[Bkcf])
                            headnorm(kcf[0:127, :, :], Bkcf, 127, 2, gkc, Bgkc, cs_cmp[0:127, :], Bcsc)
                            op('act', lambda e: e.copy(out=kcb[0:127], in_=kcf[0:127]), reads=[Bkcf], writes=[Bkcb])
                            pt, Bpp = next_pt()
                            for g in range(2):
                                op('pe', lambda e, g=g, pt=pt: e.transpose(out=pt[0:64, g * 128:g * 128 + 127], in_=kcb[0:127, g, :], identity=ident[0:127, 0:127]),
                                   reads=[Bkcb, Bident], writes=[Bpp])
                            op('dve', lambda e, pt=pt: e.tensor_copy(out=KcT[:, :, 0:127], in_=pt[0:64, 0:256].rearrange("p (g t) -> p g t", g=2)[:, :, 0:127]),
                               reads=[Bpp], writes=[BKcT])
                        else:
                            op('dve', lambda e, pf=pf: e.tensor_tensor(out=Vc[0:127, :, 33:97], in0=pf[0:127, 0:128].rearrange("p (g d) -> p g d", g=2),
                                                                       in1=cpeb[0:127, 1, :].rearrange("p (g d) -> p g d", g=2), op=ALU.add),
                               reads=[Bp, Bcpeb], writes=[BVc])
                    ptb = sbF("ptb", [128, 256], I32); Bptb = Buf("ptb")
                    dma('sp', ptb[:, :], ptab.partition_broadcast(128).rearrange("p a d -> p (a d)"), writes=[Bptb])
                    io_i = sbF("io_i", [128, 1], I32); Bio = Buf("io")
                    op('pool', lambda e: e.iota(out=io_i[:, :], pattern=[[0, 1]], base=0, channel_multiplier=1), writes=[Bio])
                    io_f = sbF("io_f", [128, 1]); Biof = Buf("io_f")
                    op('dve', lambda e: e.tensor_copy(out=io_f[:, :], in_=io_i[:, :]), reads=[Bio], writes=[Biof])
                    idx = sbF("idx", [128, 256], I32); Bidx = Buf("idx")
                    op('dve', lambda e: e.tensor_scalar(out=idx[:, :], in0=ptb[:, :], scalar1=128.0, scalar2=io_f[:, 0:1], op0=ALU.mult, op1=ALU.add),
                       reads=[Bptb, Biof], writes=[Bidx])
                    wGA = sbF("wGA", [128, 8, 24], BF16); BwGA = Buf("wGA")
                    dma('pool', wGA[:], w_in_v[:, :, O_GA:O_GA + 24], writes=[BwGA])
                    gT = sbF("gT", [8, 3, NS]); BgT = Buf("gT")
                    pf, Bp = next_pf()
                    for b_ in range(3):
                        for k in range(8):
                            op('pe', lambda e, b_=b_, k=k, pf=pf: e.matmul(pf[0:8, b_ * NS:(b_ + 1) * NS], lhsT=wGA[:, k, b_ * 8:(b_ + 1) * 8], rhs=hT[:, k, T:TT],
                                                                           start=(k == 0), stop=(k == 7)), reads=[BwGA, BhT[16]], writes=[Bp])
                    op('act', lambda e, pf=pf: e.activation(out=gT[:].rearrange("p b s -> p (b s)"), in_=pf[0:8, 0:3 * NS], func=AF.Sigmoid), reads=[Bp], writes=[BgT])
                    q_st = sbF("q_st", [NS, 4, 128], BF16); Bq_st = Buf("q_st")
                    for g in range(2):
                        op('dve', lambda e, g=g: e.tensor_copy(out=q_st[:, :, g * 64:(g + 1) * 64], in_=q16[:NS, g * 256:(g + 1) * 256].rearrange("p (j d) -> p j d", j=4)),
                           reads=[Bq16], writes=[Bq_st])
                    pt, Bp = next_pt()
                    for j in range(4):
                        op('pe', lambda e, j=j, pt=pt: e.transpose(out=pt[:, j * 128:j * 128 + NS], in_=q_st[:NS, j, :], identity=ident[:NS, :NS]), reads=[Bq_st, Bident], writes=[Bp])
                    Qbd = sbF("Qbd", [128, NS, 8], BF16); BQbd = Buf("Qbd")
                    op('pool', lambda e: e.memset(Qbd[:].rearrange("p s h -> p (s h)"), 0.0), writes=[BQbd])
                    ptv = pt[:, 0:512].rearrange("p (j t) -> p j t", j=4)[:, :, :NS].rearrange("p j s -> p s j")
                    op('dve', lambda e: e.tensor_copy(out=Qbd[0:64, :, 0:4], in_=ptv[0:64]), reads=[Bp, BQbd], writes=[BQbd])
                    op('dve', lambda e: e.tensor_copy(out=Qbd[64:128, :, 4:8], in_=ptv[64:128]), reads=[Bp, BQbd], writes=[BQbd])
                    ovlb = sbF("ovlb", [128, 33], BF16); Bovlb = Buf("ovlb")
                    op('dve', lambda e: e.tensor_copy(out=ovlb[:, :], in_=ovl[:, :]), reads=[Bovl], writes=[Bovlb])
                    dmk, Bdmk = load_const(sbF, "dmk", [NS, NS, 8], C['dmask'][:, :].rearrange("a (s h) -> a s h", h=8), BF16)
                    G8, BG8 = load_const(sbF, "G8", [8, 8], C['G8'][:, :])
                    hm8, Bhm8 = load_const(sbF, "hm8", [8, 2], C['hm8'][:, :])
                    sadd, Bsadd = load_const(sbF, "sadd", [8, 32], C['seladd_s'][:, :])
                    gbuf = sbF("gbuf", [128, 16, 512], BF16); Bgbuf = Buf("gbuf")
                    cwb = sbF("cwb", [128, 4, 256], BF16); Bcwb = Buf("cwb")
                    KT3 = sbF("KT3", [128, 2, T], BF16); BKT3 = [Buf("KT3a"), Buf("KT3b"), Bxt[1]]
                    KTs = [KT3[:, 0, :], KT3[:, 1, :], xt[1][:, :].bitcast(BF16)]
                    Vsl = xt[0][:, :].bitcast(BF16).rearrange("p (j c) -> p j c", j=16); BVsl = Bxt[0]
                    KwTs = sbF("KwTs", [128, 512], BF16); BKwTs = Buf("KwTs")
                    kcs = sbF("kcs", [128, 2, HD]); Bkcs = Buf("kcs")
                    kcsb = sbF("kcsb", [128, 128], BF16); Bkcsb = Buf("kcsb")
                    KcTs = sbF("KcTs", [128, 128], BF16); BKcTs = Buf("KcTs")
                    Vcs = sbF("Vcs", [128, 128], BF16); BVcs = Buf("Vcs")
                    Pc8 = sbF("Pc8", [128, 8], BF16); BPc8 = Buf("Pc8")
                    Ps8 = sbF("Ps8", [128, 128], BF16); BPs8 = Buf("Ps8")
                    Pw8 = sbF("Pw8", [128, 32], BF16); BPw8 = Buf("Pw8")
                    Pnf = sbF("Pnf", [NS, 2, 8]); BPnf = Buf("Pnf")
                    Pnb = sbF("Pnb", [NS, 2, 8], BF16); BPnb = Buf("Pnb")
                    Rr = sbF("Rr", [128, 2, 8]); BRr = Buf("Rr")
                    sm = sbF("sm", [8, 128]); Bsm = Buf("sm")
                    sc_s = sbF("sc_s", [8, 32]); Bsc_s = Buf("sc_s")
                    sc_2 = sbF("sc_2", [8, 32]); Bsc_2 = Buf("sc_2")
                    m8s = sbF("m8s", [8, 16]); Bm8s = Buf("m8s")
                    selT = sbF("selT", [96, 8], BF16); BselT = Buf("selT")
                    sc_3 = sbF("sc_3", [8, 96]); Bsc_3 = Buf("sc_3")
                    op('pool', lambda e: e.memset(sc_3[:, :], 0.0), writes=[Bsc_3])
                    ofull = sbF("ofull", [8, 128]); Bofull = Buf("ofull")
                    osamp = sbF("osamp", [8, 2, HD]); Bosamp = Buf("osamp")
                    pSc, BSc = PF[0], BPF[0]; pMi, BMi = PF[1], BPF[1]; pOo, BOo = PF[2], BPF[2]
                    crr = [0]

                    def next_c():
                        crr[0] = (crr[0] + 1) % 3
                        return PF[3 + crr[0]], BPF[3 + crr[0]]
                    cache_rows = cache_nsa
                    dma('sp', hts[:, :], hT[:].rearrange("p k t -> p (k t)"), reads=BhT, writes=[Bhts])
                    kb.barrier()
                    hfl = hT[:].rearrange("p k t -> p (k t)")

                    def carve(off, n_):
                        return hfl[:, off:off + n_]
                    PsT = [carve(j * 512, 512) for j in range(16)]; BPsT = [Buf("PsT%d" % j) for j in range(16)]
                    PwT = [carve(8192 + j * 512, 512) for j in range(5)]; BPwT = [Buf("PwT%d" % j) for j in range(5)]
                    cmpbias = carve(10752, 2048).rearrange("p (i q) -> p i q", i=16); Bcmpbias = Buf("cmpbias")
                    diag4 = carve(12800, 512).rearrange("p (h q) -> p h q", h=4); Bdiag4 = Buf("diag4")
                    winl4 = carve(13312, 512).rearrange("p (h q) -> p h q", h=4); Bwinl4 = Buf("winl4")
                    cmpb4 = carve(13824, 512).rearrange("p (h q) -> p h q", h=4); Bcmpb4 = Buf("cmpb4")
                    PcT = carve(14336, 512); BPcT = Buf("PcT")
                    stg = carve(14848, 128); Bstg = Buf("stg")
                    onsa_tok = carve(14976, 512).rearrange("p (h d) -> p h d", h=8); Bonsa_tok = Buf("onsa_tok")
                    dma('pool', cmpbias, C['cmpbias'][:, :, :], writes=[Bcmpbias])
                    for h_ in range(4):
                        dma('pool', diag4[:, h_, :], C['diag'][:, :], writes=[Bdiag4])
                        dma('pool', winl4[:, h_, :], C['winlow'][:, :], writes=[Bwinl4])
                    op('pool', lambda e: e.memset(stg, 0.0), writes=[Bstg])
                    Ff = carve(15488, 1024).bitcast(F32)
                    ot = Ff[:, 0:256].rearrange("p (h d) -> p h d", h=4); Bot = Buf("ot")
                    sel = Ff[:, 256:416]; Bsel = Buf("sel")
                    sc2 = Ff[:, 416:448]; Bsc2 = Buf("sc2")
                    m8 = Ff[:, 448:464]; Bm8 = Buf("m8")
                    rcp = Ff[:, 464:476].rearrange("p (b h) -> p b h", b=3); Brcp = Buf("rcp")
                    seladd_i = Ff[:, 476:508]; Bseladd = Buf("seladd")
                    of = ptb[:, :].bitcast(F32).rearrange("p (h d) -> p h d", h=4); Bof = Bptb
                    pS = [(PF[0], BPF[0]), (PF[1], BPF[1])]
                    (pOC, BOC), (pOW, BOW), (pOS, BOS), (pC, BC) = (PF[2], BPF[2]), (PF[3], BPF[3]), (PF[4], BPF[4]), (PF[5], BPF[5])
                    srr = [0]

                    def next_s():
                        srr[0] ^= 1
                        return pS[srr[0]]

                    def sample_iter(s_i):
                        for j in range(16):
                            col = s_i * 16 + j
                            dma('pool', None, None, reads=[Bidx], writes=[Bgbuf], fn=lambda e, j=j, col=col: e.indirect_dma_start(
                                out=gbuf[:, j, :], out_offset=None, in_=cache_rows[:, :], in_offset=bass.IndirectOffsetOnAxis(ap=idx[:, col:col + 1], axis=0)))
                        dma('pool', cwb[:, :, :], cache_win[s_i].rearrange("(j p) c -> p j c", p=128), writes=[Bcwb])
                        def tr_comp(comp, slot, deint):
                            for half in range(2):
                                pt, Bp = next_pt()
                                for jj in range(8):
                                    j = half * 8 + jj
                                    op('pe', lambda e, jj=jj, j=j, pt=pt: e.transpose(out=pt[:, jj * 128:(jj + 1) * 128], in_=gbuf[:, j, comp * 128:(comp + 1) * 128],
                                                                                identity=ident[:, :]), reads=[Bgbuf, Bident], writes=[Bp])
                                if deint:
                                    dst = KTs[slot].rearrange("p (r m) -> p r m", r=16)[:, :, half * 64:(half + 1) * 64].rearrange("p r m -> p m r")
                                    src_ = pt[:, :].rearrange("p (m r) -> p m r", r=16)
                                else:
                                    dst = KTs[slot][:, half * 1024:(half + 1) * 1024]; src_ = pt[:, :]
                                if half == 0:
                                    op('act', lambda e: e.copy(out=dst, in_=src_), reads=[Bp], writes=[BKT3[slot]])
                                else:
                                    op('dve', lambda e: e.tensor_copy(out=dst, in_=src_), reads=[Bp], writes=[BKT3[slot]])
                        tr_comp(0, 0, True); tr_comp(1, 1, True); tr_comp(2, 2, False)
                        op('dve', lambda e: e.tensor_copy(out=Vsl, in_=gbuf[:, :, 384:512]), reads=[Bgbuf], writes=[BVsl])
                        pt, Bp = next_pt()
                        for j in range(4):
                            op('pe', lambda e, j=j, pt=pt: e.transpose(out=pt[:, j * 128:(j + 1) * 128], in_=cwb[:, j, 0:128], identity=ident[:, :]), reads=[Bcwb, Bident], writes=[Bp])
                        op('dve', lambda e, pt=pt: e.tensor_copy(out=KwTs[:, :], in_=pt[:, 0:512]), reads=[Bp], writes=[BKwTs])
                        for kv_ in range(2):
                            pf, Bp = next_c()
                            for l in range(32):
                                op('pe', lambda e, l=l, pf=pf, kv_=kv_: e.matmul(pf[0:127, 0:128], lhsT=KT3[:, kv_, (l % 16) * 128 + l // 16:(l % 16) * 128 + l // 16 + 127], rhs=W2[:, kv_, l, :],
                                                                                 start=(l == 0), stop=(l == 31)), reads=[BKT3[kv_], BW2], writes=[Bp])
                            if kv_ == 0:
                                op('dve', lambda e, pf=pf: e.tensor_tensor(out=kcs[0:127].rearrange("p g d -> p (g d)"), in0=pf[0:127, 0:128], in1=cpeb[0:127, 0, :], op=ALU.add),
                                   reads=[Bp, Bcpeb], writes=[Bkcs])
                                headnorm(kcs[0:127, :, :], Bkcs, 127, 2, gkc, Bgkc, cs_cmp[0:127, :], Bcsc)
                                op('act', lambda e: e.copy(out=kcsb[0:127, :], in_=kcs[0:127].rearrange("p g d -> p (g d)")), reads=[Bkcs], writes=[Bkcsb])
                                pt, Bpp = next_pt()
                                op('pe', lambda e, pt=pt: e.transpose(out=pt[:, 0:127], in_=kcsb[0:127, :], identity=ident[0:127, 0:127]), reads=[Bkcsb, Bident], writes=[Bpp])
                                op('dve', lambda e, pt=pt: e.tensor_copy(out=KcTs[:, 0:127], in_=pt[:, 0:127]), reads=[Bpp], writes=[BKcTs])
                            else:
                                op('dve', lambda e, pf=pf: e.tensor_tensor(out=Vcs[0:127, :], in0=pf[0:127, 0:128], in1=cpeb[0:127, 1, :], op=ALU.add), reads=[Bp, Bcpeb], writes=[BVcs])
                        qs = Qbd[:, s_i, :]
                        op('pe', lambda e: e.matmul(pSc[0:127, 0:8], lhsT=KcTs[:, 0:127], rhs=qs, start=True, stop=True), reads=[BKcTs, BQbd], writes=[BSc])
                        op('act', lambda e: e.activation(out=Pc8[0:127, :], in_=pSc[0:127, 0:8], func=AF.Exp, scale=0.125), reads=[BSc], writes=[BPc8])
                        op('pe', lambda e: e.matmul(pMi[0:8, 0:33], lhsT=Pc8[0:127, :], rhs=ovlb[0:127, :], start=True, stop=True), reads=[BPc8, Bovlb], writes=[BMi])
                        op('dve', lambda e: e.reciprocal(out=sm[:, 0:1], in_=pMi[0:8, 32:33]), reads=[BMi], writes=[Bsm])
                        op('dve', lambda e: e.tensor_scalar(out=sc_s[:, :], in0=pMi[0:8, 0:32], scalar1=sm[:, 0:1], scalar2=None, op0=ALU.mult), reads=[BMi, Bsm], writes=[Bsc_s])
                        op('pe', lambda e: e.matmul(pMi[0:8, 64:96], lhsT=G8[:, :], rhs=sc_s[:, :], start=True, stop=True), reads=[BG8, Bsc_s], writes=[BMi])
                        op('dve', lambda e: e.tensor_tensor(out=sc_s[:, :], in0=pMi[0:8, 64:96], in1=sadd[:, :], op=ALU.add), reads=[BMi, Bsadd], writes=[Bsc_s])
                        op('dve', lambda e: e.max(out=m8s[:, 0:8], in_=sc_s[:, :]), reads=[Bsc_s], writes=[Bm8s])
                        op('dve', lambda e: e.match_replace(out=sc_2[:, :], in_to_replace=m8s[:, 0:8], in_values=sc_s[:, :], imm_value=-3.0e38), reads=[Bsc_s, Bm8s], writes=[Bsc_2])
                        op('dve', lambda e: e.max(out=m8s[:, 8:16], in_=sc_2[:, :]), reads=[Bsc_2], writes=[Bm8s])
                        op('dve', lambda e: e.tensor_scalar(out=sc_2[:, :], in0=sc_s[:, :], scalar1=m8s[:, 14:15], scalar2=None, op0=ALU.is_ge), reads=[Bsc_s, Bm8s], writes=[Bsc_2])
                        op('dve', lambda e: e.tensor_scalar(out=sc_3[:, 64:96], in0=sc_2[:, :], scalar1=-NEG, scalar2=NEG, op0=ALU.mult, op1=ALU.add), reads=[Bsc_2], writes=[Bsc_3])
                        op('pe', lambda e: e.transpose(out=pMi[0:96, 128:136], in_=sc_3[0:8, :], identity=identf[0:8, 0:8]), reads=[Bsc_3, Bidentf], writes=[BMi])
                        op('dve', lambda e: e.tensor_copy(out=selT[64:96, :], in_=pMi[64:96, 128:136]), reads=[BMi], writes=[BselT])
                        for j in range(16):
                            op('pe', lambda e, j=j: e.matmul(pSc[:, 8 + j * 8:16 + j * 8], lhsT=KTs[2][:, j * 128:(j + 1) * 128], rhs=qs, start=True, stop=False), reads=[BKT3[2], BQbd], writes=[BSc])
                            op('pe', lambda e, j=j: e.matmul(pSc[:, 8 + j * 8:16 + j * 8], lhsT=KsT[64:96, 0, j * 128:(j + 1) * 128], rhs=selT[64:96, :], start=False, stop=True), reads=[BKsT, BselT], writes=[BSc])
                        for j in range(4):
                            op('pe', lambda e, j=j: e.matmul(pSc[:, 136 + j * 8:144 + j * 8], lhsT=KwTs[:, j * 128:(j + 1) * 128], rhs=qs, start=True, stop=True), reads=[BKwTs, BQbd], writes=[BSc])
                        for b_ in range(2):
                            op('pe', lambda e, b_=b_: e.matmul(pSc[0:NS, 168 + b_ * 8:176 + b_ * 8], lhsT=KnT[:, b_, :], rhs=qs, start=True, stop=False), reads=[BKnT, BQbd], writes=[BSc])
                            op('pe', lambda e, b_=b_: e.matmul(pSc[0:NS, 168 + b_ * 8:176 + b_ * 8], lhsT=ident[0:NS, 0:NS], rhs=dmk[0:NS, s_i, :], start=False, stop=True), reads=[Bident, Bdmk], writes=[BSc])
                        op('act', lambda e: e.activation(out=Ps8[:, :], in_=pSc[:, 8:136], func=AF.Exp, scale=0.125), reads=[BSc], writes=[BPs8])
                        op('act', lambda e: e.activation(out=Pw8[:, :], in_=pSc[:, 136:168], func=AF.Exp, scale=0.125), reads=[BSc], writes=[BPw8])
                        op('act', lambda e: e.activation(out=Pnf[:].rearrange("p b h -> p (b h)"), in_=pSc[0:NS, 168:184], func=AF.Exp, scale=0.125), reads=[BSc], writes=[BPnf])
                        op('dve', lambda e: e.tensor_copy(out=Pnb[:], in_=Pnf[:]), reads=[BPnf], writes=[BPnb])
                        op('pe', lambda e: e.matmul(pOo[0:8, 0:128], lhsT=Pc8[0:127, :], rhs=Vcs[0:127, :], start=True, stop=True), reads=[BPc8, BVcs], writes=[BOo])
                        for j in range(16):
                            op('pe', lambda e, j=j: e.matmul(pOo[0:8, 128:256], lhsT=Ps8[:, j * 8:(j + 1) * 8], rhs=Vsl[:, j, :], start=(j == 0), stop=False), reads=[BPs8, BVsl], writes=[BOo])
                        op('pe', lambda e: e.matmul(pOo[0:8, 128:256], lhsT=Pnb[0:NS, 0, :], rhs=Vn[0:NS, 0, :], start=False, stop=True), reads=[BPnb, BVn], writes=[BOo])
                        for j in range(4):
                            op('pe', lambda e, j=j: e.matmul(pOo[0:8, 256:384], lhsT=Pw8[:, j * 8:(j + 1) * 8], rhs=cwb[:, j, 128:256], start=(j == 0), stop=False), reads=[BPw8, Bcwb], writes=[BOo])
                        op('pe', lambda e: e.matmul(pOo[0:8, 256:384], lhsT=Pnb[0:NS, 1, :], rhs=Vn[0:NS, 1, :], start=False, stop=True), reads=[BPnb, BVn], writes=[BOo])
                        op('dve', lambda e: e.tensor_reduce(out=Rr[:, 0, :], in_=Ps8[:, :].rearrange("p (j h) -> p h j", h=8), axis=AX.X, op=ALU.add), reads=[BPs8], writes=[BRr])
                        op('dve', lambda e: e.tensor_reduce(out=Rr[:, 1, :], in_=Pw8[:, :].rearrange("p (j h) -> p h j", h=8), axis=AX.X, op=ALU.add), reads=[BPw8], writes=[BRr])
                        for b_ in range(2):
                            op('pe', lambda e, b_=b_: e.matmul(pMi[0:8, 160 + b_:161 + b_], lhsT=Rr[:, b_, :], rhs=onesf[:, 0:1], start=True, stop=False), reads=[BRr, Bonesf], writes=[BMi])
                            op('pe', lambda e, b_=b_: e.matmul(pMi[0:8, 160 + b_:161 + b_], lhsT=Pnf[0:NS, b_, :], rhs=onesf[0:NS, 0:1], start=False, stop=True), reads=[BPnf, Bonesf], writes=[BMi])
                        op('dve', lambda e: e.reciprocal(out=sm[:, 1:3], in_=pMi[0:8, 160:162]), reads=[BMi], writes=[Bsm])
                        op('dve', lambda e: e.tensor_tensor(out=sm[:, 0:3], in0=sm[:, 0:3], in1=gT[:, :, s_i], op=ALU.mult), reads=[Bsm, BgT], writes=[Bsm])
                        op('dve', lambda e: e.tensor_scalar(out=ofull[:, :], in0=pOo[0:8, 0:128], scalar1=sm[:, 0:1], scalar2=None, op0=ALU.mult), reads=[BOo, Bsm], writes=[Bofull])
                        op('dve', lambda e: e.scalar_tensor_tensor(out=ofull[:, :], in0=pOo[0:8, 128:256], scalar=sm[:, 1:2], in1=ofull[:, :], op0=ALU.mult, op1=ALU.add), reads=[BOo, Bsm, Bofull], writes=[Bofull])
                        op('dve', lambda e: e.scalar_tensor_tensor(out=ofull[:, :], in0=pOo[0:8, 256:384], scalar=sm[:, 2:3], in1=ofull[:, :], op0=ALU.mult, op1=ALU.add), reads=[BOo, Bsm, Bofull], writes=[Bofull])
                        op('dve', lambda e: e.tensor_scalar(out=osamp[:, 0, :], in0=ofull[:, 0:64], scalar1=hm8[:, 0:1], scalar2=None, op0=ALU.mult), reads=[Bofull, Bhm8], writes=[Bosamp])
                        op('dve', lambda e: e.scalar_tensor_tensor(out=osamp[:, 1, :], in0=ofull[:, 64:128], scalar=hm8[:, 1:2], in1=osamp[:, 0, :], op0=ALU.mult, op1=ALU.add),
                           reads=[Bofull, Bhm8, Bosamp], writes=[Bosamp])
                        dma('sp', scr[8192:8192 + 8 * NS * HD].rearrange("(h s d) -> h s d", h=8, s=NS)[:, s_i, :], osamp[:, 1, :], reads=[Bosamp], writes=[Bscr])
                    def prompt_iter(i):
                        q0 = i * 128
                        dma('sp', seladd_i, C['seladd'][:, i, :], writes=[Bseladd])
                        op('dve', lambda e, i=i: e.tensor_copy(out=cmpb4[:], in_=cmpbias[:, i, :].unsqueeze(1).broadcast_to([128, 4, 128])),
                           reads=[Bcmpbias], writes=[Bcmpb4])
                        for g in range(2):
                            qr64 = QT[0:64, 4 * g:4 * g + 4, q0:q0 + 128]
                            qr96 = QT[0:96, 4 * g:4 * g + 4, q0:q0 + 128]
                            op('pe', lambda e: e.matmul(pC[0:127, :], lhsT=KcT[0:64, g, 0:127], rhs=qr64, start=True, stop=False), reads=[BKcT, BQT[i]], writes=[BC])
                            op('pe', lambda e: e.matmul(pC[0:127, :], lhsT=ident[0:127, 0:127], rhs=cmpb4[0:127, :, :], start=False, stop=True),
                               reads=[Bident, Bcmpb4], writes=[BC])
                            op('act', lambda e: e.activation(out=PcT[0:127, :], in_=pC[0:127, :], func=AF.Exp, scale=0.125), reads=[BC], writes=[BPcT])
                            wj = list(range(max(0, i - 4), i + 1))
                            for jj, j in enumerate(wj):
                                ps_, Bs_ = next_s()
                                extra = diag4 if j == i else (winl4 if j == i - 4 else None)
                                Bex = Bdiag4 if j == i else Bwinl4
                                op('pe', lambda e, ps_=ps_, j=j, extra=extra: e.matmul(ps_[:, :], lhsT=KwT[0:64, g, j * 128:(j + 1) * 128], rhs=qr64, start=True, stop=(extra is None)),
                                   reads=[BKwT, BQT[i]], writes=[Bs_])
                                if extra is not None:
                                    op('pe', lambda e, ps_=ps_, extra=extra: e.matmul(ps_[:, :], lhsT=ident[:, :], rhs=extra[:, :, :], start=False, stop=True),
                                       reads=[Bident, Bex], writes=[Bs_])
                                op('act', lambda e, ps_=ps_, jj=jj: e.activation(out=PwT[jj][:, :], in_=ps_[:, :], func=AF.Exp, scale=0.125), reads=[Bs_], writes=[BPwT[jj]])
                            for h in range(4):
                                op('pe', lambda e, h=h: e.matmul(pOC[:, h * 97:(h + 1) * 97], lhsT=PcT[0:127, h * 128:(h + 1) * 128], rhs=Vc[0:127, g, :], start=True, stop=True),
                                   reads=[BPcT, BVc], writes=[BOC])
                            for h in range(4):
                                for jj, j in enumerate(wj):
                                    op('pe', lambda e, h=h, jj=jj, j=j: e.matmul(pOW[:, h * 65:(h + 1) * 65], lhsT=PwT[jj][:, h * 128:(h + 1) * 128], rhs=Vw[:, j, g, :],
                                                                                 start=(jj == 0), stop=(jj == len(wj) - 1)), reads=[BPwT[jj], BVw], writes=[BOW])
                            oc = pOC[:, 0:388].rearrange("p (h c) -> p h c", c=97)
                            op('dve', lambda e: e.tensor_scalar(out=rcp[:, 0, :].unsqueeze(2), in0=oc[:, :, 32:33], scalar1=1e-30, scalar2=None, op0=ALU.max),
                               reads=[BOC], writes=[Brcp])
                            op('dve', lambda e: e.reciprocal(out=rcp[:, 0, :], in_=rcp[:, 0, :]), reads=[Brcp], writes=[Brcp])
                            op('dve', lambda e: e.tensor_tensor(out=sel[:, 0:128].rearrange("p (h b) -> p h b", h=4), in0=oc[:, :, 0:32],
                                                                in1=rcp[:, 0, :].unsqueeze(2).broadcast_to([128, 4, 32]), op=ALU.mult), reads=[BOC, Brcp], writes=[Bsel])
                            op('dve', lambda e: e.tensor_reduce(out=sel[:, 128:160], in_=sel[:, 0:128].rearrange("p (h b) -> p b h", h=4), axis=AX.X, op=ALU.add),
                               reads=[Bsel], writes=[Bsel])
                            op('dve', lambda e, i=i: e.tensor_tensor(out=sel[:, 128:160], in0=sel[:, 128:160], in1=seladd_i, op=ALU.add), reads=[Bsel, Bseladd], writes=[Bsel])
                            op('dve', lambda e: e.max(out=m8[:, 0:8], in_=sel[:, 128:160]), reads=[Bsel], writes=[Bm8])
                            op('dve', lambda e: e.match_replace(out=sc2[:, :], in_to_replace=m8[:, 0:8], in_values=sel[:, 128:160], imm_value=-3.0e38),
                               reads=[Bsel, Bm8], writes=[Bsc2])
                            op('dve', lambda e: e.max(out=m8[:, 8:16], in_=sc2[:, :]), reads=[Bsc2], writes=[Bm8])
                            op('dve', lambda e: e.tensor_scalar(out=sc2[:, :], in0=sel[:, 128:160], scalar1=m8[:, 15:16], scalar2=None, op0=ALU.is_ge), reads=[Bsel, Bm8], writes=[Bsc2])
                            op('dve', lambda e: e.tensor_scalar(out=stg[:, 64:96], in0=sc2[:, :], scalar1=-NEG, scalar2=NEG, op0=ALU.mult, op1=ALU.add), reads=[Bsc2], writes=[Bstg])
                            pt, Bpp = next_pt()
                            op('pe', lambda e, pt=pt: e.transpose(out=pt[:, 0:128], in_=stg[:, :], identity=ident[:, :]), reads=[Bstg, Bident], writes=[Bpp])
                            op('dve', lambda e, pt=pt: e.tensor_copy(out=QT[64:96, 4 * g:4 * g + 4, q0:q0 + 128], in_=pt[64:96, 0:128].unsqueeze(1).broadcast_to([32, 4, 128])),
                               reads=[Bpp], writes=[BQT[i]])
                            for j in range(i + 1):
                                ps_, Bs_ = next_s()
                                op('pe', lambda e, ps_=ps_, j=j: e.matmul(ps_[:, :], lhsT=KsT[0:96, g, j * 128:(j + 1) * 128], rhs=qr96, start=True, stop=(j != i)),
                                   reads=[BKsT, BQT[i]], writes=[Bs_])
                                if j == i:
                                    op('pe', lambda e, ps_=ps_: e.matmul(ps_[:, :], lhsT=ident[:, :], rhs=diag4[:, :, :], start=False, stop=True), reads=[Bident, Bdiag4], writes=[Bs_])
                                op('act', lambda e, ps_=ps_, j=j: e.activation(out=PsT[j][:, :], in_=ps_[:, :], func=AF.Exp, scale=0.125), reads=[Bs_], writes=[BPsT[j]])
                            for h in range(4):
                                for j in range(i + 1):
                                    op('pe', lambda e, h=h, j=j: e.matmul(pOS[:, h * 65:(h + 1) * 65], lhsT=PsT[j][:, h * 128:(h + 1) * 128], rhs=Vs[:, j, g, :],
                                                                          start=(j == 0), stop=(j == i)), reads=[BPsT[j], BVs], writes=[BOS])
                            osv = pOS[:, 0:260].rearrange("p (h c) -> p h c", c=65); owv = pOW[:, 0:260].rearrange("p (h c) -> p h c", c=65)
                            op('dve', lambda e: e.tensor_scalar(out=rcp[:, 1, :].unsqueeze(2), in0=osv[:, :, 64:65], scalar1=1e-30, scalar2=None, op0=ALU.max), reads=[BOS], writes=[Brcp])
                            op('dve', lambda e: e.tensor_scalar(out=rcp[:, 2, :].unsqueeze(2), in0=owv[:, :, 64:65], scalar1=1e-30, scalar2=None, op0=ALU.max), reads=[BOW], writes=[Brcp])
                            op('dve', lambda e: e.reciprocal(out=rcp[:, 1:3, :], in_=rcp[:, 1:3, :]), reads=[Brcp], writes=[Brcp])
                            op('dve', lambda e, i=i, g=g: e.tensor_tensor(out=rcp[:, :, :], in0=rcp[:, :, :], in1=gates[:, i, :].rearrange("p (b h) -> p b h", b=3)[:, :, 4 * g:4 * g + 4],
                                                                          op=ALU.mult), reads=[Brcp, Bgates], writes=[Brcp])
                            op('dve', lambda e: e.tensor_tensor(out=of[:], in0=oc[:, :, 33:97], in1=rcp[:, 0, :].unsqueeze(2).broadcast_to([128, 4, HD]), op=ALU.mult),
                               reads=[BOC, Brcp], writes=[Bof])
                            op('dve', lambda e: e.tensor_tensor(out=ot[:], in0=osv[:, :, 0:64], in1=rcp[:, 1, :].unsqueeze(2).broadcast_to([128, 4, HD]), op=ALU.mult),
                               reads=[BOS, Brcp], writes=[Bot])
                            op('dve', lambda e: e.tensor_tensor(out=of[:], in0=of[:], in1=ot[:], op=ALU.add), reads=[Bof, Bot], writes=[Bof])
                            op('dve', lambda e: e.tensor_tensor(out=ot[:], in0=owv[:, :, 0:64], in1=rcp[:, 2, :].unsqueeze(2).broadcast_to([128, 4, HD]), op=ALU.mult),
                               reads=[BOW, Brcp], writes=[Bot])
                            op('dve', lambda e, g=g: e.tensor_tensor(out=onsa_tok[:, 4 * g:4 * g + 4, :], in0=of[:], in1=ot[:], op=ALU.add), reads=[Bof, Bot], writes=[Bonsa_tok])
                        pt, Bpp = next_pt()
                        for c in range(4):
                            op('pe', lambda e, c=c, pt=pt: e.transpose(out=pt[:, c * 128:(c + 1) * 128], in_=onsa_tok[:, 2 * c:2 * c + 2, :].rearrange("p h d -> p (h d)"),
                                                                        identity=ident[:, :]), reads=[Bonsa_tok, Bident], writes=[Bpp])
                        op('act', lambda e, pt=pt, q0=q0: e.copy(out=onsaT[:, :, q0:q0 + 128], in_=pt[:, 0:512].rearrange("p (c t) -> p c t", c=4)), reads=[Bpp], writes=[BonsaT])
                    for i_ in range(16):
                        prompt_iter(i_)
                        sample_iter(i_)
                    on16b = q_st[:].rearrange("s j d -> s (j d)"); Bon16b = Bq_st
                    dma('pool', q_st[:].rearrange("s j (a d) -> s (j a) d", a=2), scr[8192:8192 + 8 * NS * HD].rearrange("(h s d) -> s h d", h=8, s=NS), reads=[Bscr], writes=[Bq_st])
                    pt, Bp = next_pt()
                    for c in range(4):
                        op('pe', lambda e, c=c, pt=pt: e.transpose(out=pt[:, c * 128:c * 128 + NS], in_=on16b[:NS, c * 128:(c + 1) * 128], identity=ident[:NS, :NS]), reads=[Bon16b, Bident], writes=[Bp])
                    op('act', lambda e, pt=pt: e.copy(out=onsaT[:, :, T:TT], in_=pt[:, 0:512].rearrange("p (c t) -> p c t", c=4)[:, :, :NS]), reads=[Bp], writes=[BonsaT])
                    kb.barrier()
                    dma('sp', hT[:].rearrange("p k t -> p (k t)"), hts[:, :], reads=[Bhts], writes=BhT)
                    kb.barrier(); esF1.close(); sbF = sbF_keep
                    kb.barrier()
            kb.barrier()
        if phases >= 4:
            dbg_dump('omT', omT, BomT)
        if phases >= 6:
            dbg_dump('onsaT', onsaT, BonsaT)
        csT = sb("csT", [128, 4, TT], BF16); BcsT = Buf("csT")
        op('pool', lambda e: e.memset(csT[:, :, T:TT], 0.0), writes=[BcsT])
        if phases >= 5:
            esD = ExitStack()
            with esD:
                sbD = mk_sb(esD)
                wU = sbD("wU", [128, 8, 1024], BF16); BwU = Buf("wU")
                for k in range(8):
                    dma('pool', wU[:, k, :], w_in_v[:, k, O_UC:O_UC + 1024], writes=[BwU])
                cw34 = sbD("cw34", [34, CC]); Bcw34 = Buf("cw34")
                dma('sp', cw34[0:31, :], conv_w[:, :], writes=[Bcw34])
                dma('sp', cw34[31:32, :], conv_b[:, :], writes=[Bcw34])
                dma('sp', cw34[32:33, :], conv_ln_g[:, :], writes=[Bcw34])
                dma('sp', cw34[33:34, :], conv_ln_b[:, :], writes=[Bcw34])
                cwT = sbD("cwT", [128, 4, 34]); BcwT = Buf("cwT")
                pf, Bp = next_pf()
                for c in range(4):
                    op('pe', lambda e, c=c, pf=pf: e.transpose(out=pf[:, c * 34:(c + 1) * 34], in_=cw34[0:34, c * 128:(c + 1) * 128], identity=identf[0:34, 0:34]),
                       reads=[Bcw34, Bidentf], writes=[Bp])
                op('dve', lambda e, pf=pf: e.tensor_copy(out=cwT[:, :, :], in_=pf[:, 0:136].rearrange("p (c w) -> p c w", c=4)), reads=[Bp], writes=[BcwT])
                glu = sbD("glu", [128, 30 + T + NS]); Bglu_ = Buf("glu")
                cT = sbD("cT", [128, 4, TT]); BcT = Buf("cT")
                sg = [sbD("sg%d" % j, [128, 512]) for j in range(2)]; Bsg = [Buf("sg%d" % j) for j in range(2)]
                op('pool', lambda e: e.memset(glu[:, 0:30], 0.0), writes=[Bglu_])
                if phases >= 8:
                    ccs = sbD("ccs", [30, NS, CC]); Bccs = Buf("ccs")
                    dma('sp', ccs[:, :, :], cache_conv.rearrange("s w c -> w s c"), writes=[Bccs])
                    xp = sbD("xp", [128, NS, 31]); Bxp = Buf("xp")
                for c in range(4):
                    for tb, (c0, cn) in enumerate(TB if phases >= 8 else TB[:4]):
                        (pa, Ba), (pb, Bb) = next_pf(), next_pf()
                        for (pp, Bpp, off) in ((pa, Ba, 0), (pb, Bb, 512)):
                            for k in range(8):
                                op('pe', lambda e, k=k, pp=pp, off=off: e.matmul(pp[:, 0:cn], lhsT=wU[:, k, off + c * 128:off + (c + 1) * 128], rhs=hT[:, k, c0:c0 + cn],
                                                                                start=(k == 0), stop=(k == 7)), reads=BhT[c0 // 128:(c0 + cn + 127) // 128] + [BwU], writes=[Bpp])
                        s_ = sg[tb % 2]; Bs_ = Bsg[tb % 2]
                        op('act', lambda e, pb=pb, s_=s_: e.activation(out=s_[:, 0:cn], in_=pb[:, 0:cn], func=AF.Sigmoid), reads=[Bb], writes=[Bs_])
                        op('dve', lambda e, pa=pa, s_=s_: e.tensor_tensor(out=glu[:, 30 + c0:30 + c0 + cn], in0=pa[:, 0:cn], in1=s_[:, 0:cn], op=ALU.mult),
                           reads=[Ba, Bs_], writes=[Bglu_])
                    op('dve', lambda e, c=c: e.tensor_scalar(out=cT[:, c, 0:T], in0=glu[:, 30:30 + T], scalar1=cwT[:, c, 30:31], scalar2=cwT[:, c, 31:32],
                                                             op0=ALU.mult, op1=ALU.add), reads=[Bglu_, BcwT], writes=[BcT])
                    for w in range(30):
                        op('dve', lambda e, c=c, w=w: e.scalar_tensor_tensor(out=cT[:, c, 0:T], in0=glu[:, w:w + T], scalar=cwT[:, c, w:w + 1], in1=cT[:, c, 0:T],
                                                                             op0=ALU.mult, op1=ALU.add), reads=[Bglu_, BcwT, BcT], writes=[BcT])
                    if phases >= 8:
                        pfc, Bpfc = next_pf()
                        for s_i in range(NS):
                            op('pe', lambda e, s_i=s_i, c=c, pfc=pfc: e.transpose(out=pfc[:, s_i * 30:(s_i + 1) * 30], in_=ccs[0:30, s_i, c * 128:(c + 1) * 128], identity=identf[0:30, 0:30]),
                               reads=[Bccs, Bidentf], writes=[Bpfc])
                        op('dve', lambda e, pfc=pfc: e.tensor_copy(out=xp[:, :, 0:30], in_=pfc[:, 0:480].rearrange("p (s w) -> p s w", w=30)), reads=[Bpfc], writes=[Bxp])
                        op('dve', lambda e: e.tensor_copy(out=xp[:, :, 30:31], in_=glu[:, 30 + T:30 + TT].unsqueeze(2)), reads=[Bglu_], writes=[Bxp])
                        op('dve', lambda e, c=c: e.tensor_tensor(out=xp[:, :, :], in0=xp[:, :, :], in1=cwT[:, c, 0:31].unsqueeze(1).broadcast_to([128, NS, 31]), op=ALU.mult),
                           reads=[Bxp, BcwT], writes=[Bxp])
                        op('dve', lambda e, c=c: e.tensor_reduce(out=cT[:, c, T:TT], in_=xp[:, :, :], axis=AX.X, op=ALU.add), reads=[Bxp], writes=[BcT])
                        op('dve', lambda e, c=c: e.tensor_scalar(out=cT[:, c, T:TT], in0=cT[:, c, T:TT], scalar1=cwT[:, c, 31:32], scalar2=None, op0=ALU.add), reads=[BcT, BcwT], writes=[BcT])
                mean = sbD("mean", [128, 512]); Bmean = Buf("mean")
                var = sbD("var", [128, 512]); Bvar = Buf("var")
                for tb, (c0, cn) in enumerate(TB if phases >= 8 else TB[:4]):
                    (p1_, B1_), (p2_, B2_) = next_pf(), next_pf()
                    for c in range(4):
                        op('pe', lambda e, c=c, p1_=p1_: e.matmul(p1_[:, 0:cn], lhsT=onesf[:, :], rhs=cT[:, c, c0:c0 + cn], start=(c == 0), stop=(c == 3)),
                           reads=[BcT, Bonesf], writes=[B1_])
                    kb.pe_force_inc = True
                    for c in range(4):
                        s_ = sg[c % 2]; Bs_ = Bsg[c % 2]
                        op('act', lambda e, c=c, s_=s_: e.activation(out=s_[:, 0:cn], in_=cT[:, c, c0:c0 + cn], func=AF.Square), reads=[BcT], writes=[Bs_])
                        op('pe', lambda e, c=c, p2_=p2_, s_=s_: e.matmul(p2_[:, 0:cn], lhsT=onesf[:, :], rhs=s_[:, 0:cn], start=(c == 0), stop=(c == 3)),
                           reads=[Bs_, Bonesf], writes=[B2_])
                    kb.pe_force_inc = False
                    op('act', lambda e, p1_=p1_: e.activation(out=mean[:, 0:cn], in_=p1_[:, 0:cn], func=AF.Copy, scale=1.0 / CC), reads=[B1_], writes=[Bmean])
                    op('dve', lambda e: e.tensor_tensor(out=var[:, 0:cn], in0=mean[:, 0:cn], in1=mean[:, 0:cn], op=ALU.mult), reads=[Bmean], writes=[Bvar])
                    op('dve', lambda e, p2_=p2_: e.scalar_tensor_tensor(out=var[:, 0:cn], in0=p2_[:, 0:cn], scalar=1.0 / CC, in1=var[:, 0:cn], op0=ALU.mult, op1=ALU.subtract),
                       reads=[B2_, Bvar], writes=[Bvar])
                    op('dve', lambda e: e.tensor_scalar(out=var[:, 0:cn], in0=var[:, 0:cn], scalar1=EPS, scalar2=None, op0=ALU.add), reads=[Bvar], writes=[Bvar])
                    op('act', lambda e: e.activation(out=var[:, 0:cn], in_=var[:, 0:cn], func=AF.Sqrt), reads=[Bvar], writes=[Bvar])
                    op('dve', lambda e: e.reciprocal(out=var[:, 0:cn], in_=var[:, 0:cn]), reads=[Bvar], writes=[Bvar])
                    for c in range(4):
                        s_ = sg[c % 2]; Bs_ = Bsg[c % 2]
                        op('dve', lambda e, c=c, s_=s_: e.tensor_tensor(out=s_[:, 0:cn], in0=cT[:, c, c0:c0 + cn], in1=mean[:, 0:cn], op=ALU.subtract),
                           reads=[BcT, Bmean], writes=[Bs_])
                        op('dve', lambda e, s_=s_: e.tensor_tensor(out=s_[:, 0:cn], in0=s_[:, 0:cn], in1=var[:, 0:cn], op=ALU.mult), reads=[Bs_, Bvar], writes=[Bs_])
                        op('dve', lambda e, c=c, s_=s_: e.tensor_scalar(out=s_[:, 0:cn], in0=s_[:, 0:cn], scalar1=cwT[:, c, 32:33], scalar2=cwT[:, c, 33:34],
                                                                        op0=ALU.mult, op1=ALU.add), reads=[Bs_, BcwT], writes=[Bs_])
                        op('act', lambda e, c=c, s_=s_: e.activation(out=csT[:, c, c0:c0 + cn], in_=s_[:, 0:cn], func=AF.Silu), reads=[Bs_], writes=[BcsT])
                kb.barrier()


        if phases >= 5:
            dbg_dump('csT', csT, BcsT)

        if phases >= 7:
            esG = ExitStack()
            with esG:
                sbG = mk_sb(esG)
                gffn, Bgffn = load_const(sbG, "gffn", [128, D], bc(norm_ffn))
                mT = sbG("mT", [128, 8, TT], BF16); BmT = Buf("mT")
                wON = sbG("wON", [128, 4, D], BF16); BwON = Buf("wON")
                wOC = sbG("wOC", [128, 4, D], BF16); BwOC = Buf("wOC")
                wOM = sbG("wOM", [128, 2, D], BF16); BwOM = Buf("wOM")
                dma('pool', wON[:], w_o_nsa.rearrange("(kc p) c -> p kc c", p=128), writes=[BwON])
                dma('pool', wOC[:], w_o_conv.rearrange("(kc p) c -> p kc c", p=128), writes=[BwOC])
                dma('pool', wOM[:], w_o_mem.rearrange("(kc p) c -> p kc c", p=128), writes=[BwOM])
                wGM = [sbG("wGM%d" % j, [128, 8, 3, 128], BF16) for j in range(2)]; BwGM = [Buf("wGM%d" % j) for j in range(2)]
                sgm = sbG("sgm", [128, 3, 512]); Bsgm = Buf("sgm")
                mt1 = sbG("mt1", [128, 512]); Bmt1 = Buf("mt1")
                mt2 = sbG("mt2", [128, 512]); Bmt2 = Buf("mt2")
                for f in range(8):
                    wg, Bwg = wGM[f % 2], BwGM[f % 2]
                    for j in range(3):
                        dma('pool', wg[:, :, j, :], w_in_v[:, :, O_GM + j * D + f * 128:O_GM + j * D + (f + 1) * 128], writes=[Bwg])
                    for tb, (c0, cn) in enumerate(TB):
                        hb = BhT[c0 // 128:(c0 + cn + 127) // 128]
                        pya, Bya = PF[0], BPF[0]; pyb, Byb = PF[1], BPF[1]; pym, Bym = PF[2], BPF[2]
                        for (pp, Bpp, wt, Bwt, src, Bsrc, nk) in ((pya, Bya, wON, BwON, onsaT, BonsaT, 4), (pyb, Byb, wOC, BwOC, csT, BcsT, 4), (pym, Bym, wOM, BwOM, omT, BomT, 2)):
                            for k in range(nk):
                                op('pe', lambda e, k=k, pp=pp, wt=wt, src=src, nk=nk: e.matmul(pp[:, 0:cn], lhsT=wt[:, k, f * 128:(f + 1) * 128], rhs=src[:, k, c0:c0 + cn],
                                                                                            start=(k == 0), stop=(k == nk - 1)), reads=[Bwt, Bsrc], writes=[Bpp])
                        for j in range(3):
                            pg, Bpg = PF[3 + j], BPF[3 + j]
                            for k in range(8):
                                op('pe', lambda e, k=k, pg=pg, j=j: e.matmul(pg[:, 0:cn], lhsT=wg[:, k, j, :], rhs=hT[:, k, c0:c0 + cn], start=(k == 0), stop=(k == 7)),
                                   reads=hb + [Bwg], writes=[Bpg])
                            op('act', lambda e, pg=pg, j=j: e.activation(out=sgm[:, j, 0:cn], in_=pg[:, 0:cn], func=AF.Sigmoid), reads=[Bpg], writes=[Bsgm])
                        op('dve', lambda e: e.tensor_tensor(out=mt1[:, 0:cn], in0=pya[:, 0:cn], in1=sgm[:, 0, 0:cn], op=ALU.mult), reads=[Bya, Bsgm], writes=[Bmt1])
                        op('dve', lambda e: e.tensor_tensor(out=mt2[:, 0:cn], in0=pyb[:, 0:cn], in1=sgm[:, 1, 0:cn], op=ALU.mult), reads=[Byb, Bsgm], writes=[Bmt2])
                        op('dve', lambda e: e.tensor_tensor(out=mt1[:, 0:cn], in0=mt1[:, 0:cn], in1=mt2[:, 0:cn], op=ALU.add), reads=[Bmt1, Bmt2], writes=[Bmt1])
                        op('dve', lambda e: e.tensor_tensor(out=mt2[:, 0:cn], in0=pym[:, 0:cn], in1=sgm[:, 2, 0:cn], op=ALU.mult), reads=[Bym, Bsgm], writes=[Bmt2])
                        op('dve', lambda e: e.tensor_tensor(out=mT[:, f, c0:c0 + cn], in0=mt1[:, 0:cn], in1=mt2[:, 0:cn], op=ALU.add), reads=[Bmt1, Bmt2], writes=[BmT])
                dbg_dump('mT', mT, BmT)
                wO = sbG("wO", [128, 8, D], BF16); BwO = Buf("wO")
                for k in range(8):
                    dma('pool', wO[:, k, :], w_out[k * 128:(k + 1) * 128, :], writes=[BwO])
                x2t = [sbG("x2t%d" % j, [128, D]) for j in range(2)]; Bx2t = [Buf("x2t%d" % j) for j in range(2)]
                for i in range(NTL):
                    n = tsz(i); t0 = i * 128; sl = i % 2
                    x_t, Bx = xt[sl], Bxt[sl]
                    dma('sp', x_t[:n, :], xs[t0:t0 + n, :], writes=[Bx])
                    for cb in range(2):
                        pf, Bp = next_pf()
                        for k in range(8):
                            op('pe', lambda e, k=k, pf=pf, cb=cb: e.matmul(pf[:n, :], lhsT=mT[:, k, t0:t0 + n], rhs=wO[:, k, cb * 512:(cb + 1) * 512], start=(k == 0), stop=(k == 7)),
                               reads=[BmT, BwO], writes=[Bp])
                        op('dve', lambda e, pf=pf, cb=cb: e.tensor_tensor(out=x2t[sl][:n, cb * 512:(cb + 1) * 512], in0=pf[:n, :], in1=x_t[:n, cb * 512:(cb + 1) * 512], op=ALU.add),
                           reads=[Bp, Bx], writes=[Bx2t[sl]])
                    dma('sp', x2s[t0:t0 + n, :], x2t[sl][:n, :], reads=[Bx2t[sl]], writes=[Bx2s[i]])
                    rms_to_T(None, n, gffn, Bgffn, hT, t0, BhT[i], sl, from_sbuf=(x2t[sl], Bx2t[sl]))
                kb.barrier()

            esU = ExitStack()
            actT = sb("actT", [128, NFC, TT], BF16); BactT = Buf("actT")
            with esU:
                sbU = mk_sb(esU)
                fwT = sbU("fwT", [128, NFC, 4]); BfwT = Buf("fwT")
                cfT = sbU("cfT", [128, NFC, 32]); BcfT = Buf("cfT")
                fc36 = [sbU("fc36_%d" % j, [36, 128]) for j in range(2)]; Bfc36 = [Buf("fc36_%d" % j) for j in range(2)]
                cf_v = cache_ffn.rearrange("s j f -> (s j) f")
                for c in range(NFC):
                    t36, B36 = fc36[c % 2], Bfc36[c % 2]
                    dma('sp', t36[0:3, :], ffn_conv_w[:, c * 128:(c + 1) * 128], writes=[B36])
                    dma('sp', t36[3:4, :], ffn_conv_b[:, c * 128:(c + 1) * 128], writes=[B36])
                    dma('sp', t36[4:36, :], cf_v[:, c * 128:(c + 1) * 128], writes=[B36])
                    pf, Bp = next_pf()
                    op('pe', lambda e, pf=pf: e.transpose(out=pf[:, 0:36], in_=t36[0:36, :], identity=identf[0:36, 0:36]), reads=[B36, Bidentf], writes=[Bp])
                    op('dve', lambda e, c=c, pf=pf: e.tensor_copy(out=fwT[:, c, :], in_=pf[:, 0:4]), reads=[Bp], writes=[BfwT])
                    op('dve', lambda e, c=c, pf=pf: e.tensor_copy(out=cfT[:, c, :], in_=pf[:, 4:36]), reads=[Bp], writes=[BcfT])
                ust = sbU("ust", [128, NFC, 18]); Bust = Buf("ust")
                wup = [sbU("wup%d" % j, [128, 8, 2, 128], BF16) for j in range(2)]; Bwup = [Buf("wup%d" % j) for j in range(2)]
                ub = [sbU("ub0", [128, 2 + T + 48])] * 2; Bub = [Buf("ub0")] * 2
                uc = [sbU("uc%d" % j, [128, 512]) for j in range(2)]; Buc = [Buf("uc%d" % j) for j in range(2)]
                op('pool', lambda e: e.memset(ub[0][:, 0:2], 0.0), writes=[Bub[0]])
                w_up_v = w_ffn_up.rearrange("(kc p) c -> p kc c", p=128)
                for c in range(NFC):
                    wu, Bwu = wup[c % 2], Bwup[c % 2]; u_, Bu_ = ub[c % 2], Bub[c % 2]
                    dma('pool', wu[:, :, 0, :], w_up_v[:, :, c * 128:(c + 1) * 128], writes=[Bwu])
                    dma('pool', wu[:, :, 1, :], w_up_v[:, :, DFF + c * 128:DFF + (c + 1) * 128], writes=[Bwu])
                    usv = u_[:, 2 + T:2 + T + 48].rearrange("p (s j) -> p s j", j=3)
                    op('dve', lambda e, c=c, usv=usv: e.tensor_copy(out=usv[:, :, 0:2], in_=cfT[:, c, :].rearrange("p (s j) -> p s j", j=2)), reads=[BcfT], writes=[Bu_])
                    for tb, (c0, cn) in enumerate(TB):
                        hb = BhT[c0 // 128:(c0 + cn + 127) // 128]
                        (pu, Bpu), (pv, Bpv) = next_pf(), next_pf()
                        for (pp, Bpp, jj) in ((pu, Bpu, 0), (pv, Bpv, 1)):
                            for k in range(8):
                                op('pe', lambda e, k=k, pp=pp, jj=jj: e.matmul(pp[:, 0:cn], lhsT=wu[:, k, jj, :], rhs=hT[:, k, c0:c0 + cn], start=(k == 0), stop=(k == 7)),
                                   reads=hb + [Bwu], writes=[Bpp])
                        if tb < 4:
                            cur = u_[:, 2 + c0:2 + c0 + cn]; m1 = u_[:, 1 + c0:1 + c0 + cn]; m2 = u_[:, c0:c0 + cn]
                        else:
                            cur = usv[:, :, 2]; m1 = usv[:, :, 1]; m2 = usv[:, :, 0]
                        t_, Bt_ = uc[tb % 2], Buc[tb % 2]
                        op('act', lambda e, pu=pu, cur=cur: e.copy(out=cur, in_=pu[:, 0:cn]), reads=[Bpu], writes=[Bu_])
                        op('dve', lambda e, c=c, cur=cur, t_=t_: e.tensor_scalar(out=t_[:, 0:cn], in0=cur, scalar1=fwT[:, c, 2:3], scalar2=fwT[:, c, 3:4], op0=ALU.mult, op1=ALU.add),
                           reads=[Bu_, BfwT], writes=[Bt_])
                        op('dve', lambda e, c=c, m1=m1, t_=t_: e.scalar_tensor_tensor(out=t_[:, 0:cn], in0=m1, scalar=fwT[:, c, 1:2], in1=t_[:, 0:cn], op0=ALU.mult, op1=ALU.add),
                           reads=[Bu_, BfwT, Bt_], writes=[Bt_])
                        op('dve', lambda e, c=c, m2=m2, t_=t_: e.scalar_tensor_tensor(out=t_[:, 0:cn], in0=m2, scalar=fwT[:, c, 0:1], in1=t_[:, 0:cn], op0=ALU.mult, op1=ALU.add),
                           reads=[Bu_, BfwT, Bt_], writes=[Bt_])
                        op('act', lambda e, t_=t_: e.activation(out=t_[:, 0:cn], in_=t_[:, 0:cn], func=AF.Gelu_apprx_tanh), reads=[Bt_], writes=[Bt_])
                        op('dve', lambda e, c=c, pv=pv, t_=t_: e.tensor_tensor(out=actT[:, c, c0:c0 + cn], in0=pv[:, 0:cn], in1=t_[:, 0:cn], op=ALU.mult), reads=[Bpv, Bt_], writes=[BactT])
                    op('dve', lambda e, c=c: e.tensor_copy(out=ust[:, c, 0:2], in_=u_[:, T:T + 2]), reads=[Bu_], writes=[Bust])
                    op('dve', lambda e, c=c, usv=usv: e.tensor_copy(out=ust[:, c, 2:18], in_=usv[:, :, 2]), reads=[Bu_], writes=[Bust])
                fst = [sbU("fst%d" % j, [18, 128]) for j in range(2)]; Bfst = [Buf("fst%d" % j) for j in range(2)]
                for c in range(NFC):
                    pf, Bp = next_pf()
                    op('pe', lambda e, c=c, pf=pf: e.transpose(out=pf[0:18, 0:128], in_=ust[:, c, :], identity=identf[:, :]), reads=[Bust, Bidentf], writes=[Bp])
                    op('act', lambda e, c=c, pf=pf: e.copy(out=fst[c % 2][0:18, :], in_=pf[0:18, 0:128]), reads=[Bp], writes=[Bfst[c % 2]])
                    dma('sp', o_ffnp[:, c * 128:(c + 1) * 128], fst[c % 2][0:2, :], reads=[Bfst[c % 2]], is_output=True)
                    dma('sp', o_ffns[:, 1, c * 128:(c + 1) * 128], fst[c % 2][2:18, :], reads=[Bfst[c % 2]], is_output=True)
                kb.barrier()

            esW = ExitStack()
            with esW:
                sbW = mk_sb(esW)
                wd_extra = sbW("wd_extra", [128, 2, D], BF16)
                cs_flat = csT[:].rearrange("p a b -> p (a b)")[:, 0:8 * D].rearrange("p (c n) -> p c n", c=8)
                on_flat = onsaT[:].rearrange("p a b -> p (a b)")[:, 0:8 * D].rearrange("p (c n) -> p c n", c=8)
                om_flat = omT[:].rearrange("p a b -> p (a b)")[:, 0:4 * D].rearrange("p (c n) -> p c n", c=4)
                Bwd = Buf("wdn")

                def wd(c):
                    if c < 8:
                        return cs_flat[:, c, :]
                    if c < 16:
                        return on_flat[:, c - 8, :]
                    if c < 20:
                        return om_flat[:, c - 16, :]
                    return wd_extra[:, c - 20, :]
                for c in range(NFC):
                    dma('pool', wd(c), w_ffn_down[c * 128:(c + 1) * 128, :], writes=[Bwd])
                yt = [sbW("yt%d" % j, [128, D]) for j in range(2)]; Byt = [Buf("yt%d" % j) for j in range(2)]
                for i in range(NTL):
                    n = tsz(i); t0 = i * 128; sl = i % 2
                    x_t, Bx = xt[sl], Bxt[sl]
                    dma('sp', x_t[:n, :], x2s[t0:t0 + n, :], reads=[Bx2s[i]], writes=[Bx])
                    for cb in range(2):
                        pf, Bp = next_pf()
                        for c in range(NFC):
                            op('pe', lambda e, c=c, pf=pf, cb=cb: e.matmul(pf[:n, :], lhsT=actT[:, c, t0:t0 + n], rhs=wd(c)[:, cb * 512:(cb + 1) * 512], start=(c == 0), stop=(c == NFC - 1)),
                               reads=[BactT, Bwd], writes=[Bp])
                        op('dve', lambda e, pf=pf, cb=cb: e.tensor_tensor(out=yt[sl][:n, cb * 512:(cb + 1) * 512], in0=pf[:n, :], in1=x_t[:n, cb * 512:(cb + 1) * 512], op=ALU.add),
                           reads=[Bp, Bx], writes=[Byt[sl]])
                    dma('sp', o_y[t0:t0 + n, :], yt[sl][:n, :], reads=[Byt[sl]], is_output=True)

        kb.finish()
    return nc


_NC_CACHE = {}


def kernel(**inputs):
    phases = PHASES
    f32 = np.float32
    consts = _consts()
    if phases not in _NC_CACHE:
        _NC_CACHE[phases] = build(phases)
    nc = _NC_CACHE[phases]
    cn = np.ascontiguousarray(inputs['cache_nsa'][0].reshape(2560 * 128, 512))
    shared = {}
    for k in ['norm_attn', 'w_in', 'q_norm', 'w_o_nsa', 'conv_w', 'conv_b', 'conv_ln_g', 'conv_ln_b', 'w_o_conv', 'norm_mem',
              'w_mem_kv', 'mq_norm', 'mk_norm', 'w_o_mem', 'w_out', 'norm_ffn', 'w_ffn_up', 'ffn_conv_w', 'ffn_conv_b', 'w_ffn_down']:
        a = np.asarray(inputs[k])[0]
        if a.ndim == 1:
            a = a[None, :]
        shared[k] = np.ascontiguousarray(a, dtype=f32)
    shared['k_norm'] = np.ascontiguousarray(inputs['k_norm'][0], dtype=f32)
    shared['cmp_pe'] = np.ascontiguousarray(inputs['cmp_pe'][0], dtype=f32)
    shared['w_cmp'] = np.ascontiguousarray(inputs['w_cmp'][0], dtype=f32)
    shared['cache_nsa'] = cn if phases >= 9 else cn[:128]
    for k, v in consts.items():
        shared['c_' + k] = v
    in_maps = []
    for c in range(8):
        m = dict(shared)
        s0 = c * NS
        m['xs'] = np.ascontiguousarray(np.concatenate([inputs['x_prompt'][c], inputs['x_sample'][s0:s0 + NS, 0]], axis=0), dtype=f32)
        m['mem'] = np.ascontiguousarray(inputs['mem_prompt'][c], dtype=f32)
        m['cache_win'] = np.ascontiguousarray(inputs['cache_win'][0, s0:s0 + NS].reshape(NS, 512, 256))
        m['cache_conv'] = np.ascontiguousarray(inputs['cache_conv'][0, s0:s0 + NS])
        m['cache_ffn'] = np.ascontiguousarray(inputs['cache_ffn'][0, s0:s0 + NS])
        m['cache_mem'] = np.ascontiguousarray(inputs['cache_mem'][0, s0:s0 + NS].reshape(NS, 256, 512))
        m['ptab'] = np.ascontiguousarray(inputs['page_table'][s0:s0 + NS].reshape(1, 256).astype(np.int32))
        in_maps.append(m)
    res = run_bass_kernel_spmd(nc, in_maps, core_ids=list(range(8)))
    R = res.results

    def cat(name):
        return np.stack([np.asarray(r[name]) for r in R], axis=0)
    y = cat('o_y'); rows = cat('o_rows')
    y_p = y[:, :T]; y_s = y[:, T:].reshape(128, 1, D)
    rows_p = rows[:, :T].reshape(1, 8, T, 4, 2, 64); rows_s = rows[:, T:].reshape(1, 128, 1, 4, 2, 64)
    win_p = cat('o_winp').reshape(1, 8, 512, 2, 2, 64)
    win_s = cat('o_wins').reshape(1, 128, 512, 2, 2, 64)
    conv_p = cat('o_convp').reshape(1, 8, 30, CC); conv_s = cat('o_convs').reshape(1, 128, 30, CC)
    ffn_p = cat('o_ffnp').reshape(1, 8, 2, DFF); ffn_s = cat('o_ffns').reshape(1, 128, 2, DFF)
    memkv = cat('o_memkv').reshape(1, 8, 256, 2, 4, 64)
    return tuple(np.ascontiguousarray(a, dtype=f32) for a in
                 (y_p, y_s, rows_p, rows_s, win_p, win_s, conv_p, conv_s, ffn_p, ffn_s, memkv))
```
